# Optimizing a Trainium2 kernel written in Bass

```python
import jax, jax.numpy as jnp
from jax import lax
import numpy as np

D_MODEL = 2048
BATCH = 2
SEQ = 8192
DEPTH = 2

GRID_W = 64
CTX_LEN = 256

HEAD_DIM = 128
N_Q_HEADS = 8
N_KV_HEADS = 2
Q_PER_KV = N_Q_HEADS // N_KV_HEADS
ATTN_W = N_Q_HEADS * HEAD_DIM
KV_W = N_KV_HEADS * HEAD_DIM
Q_BLOCK = 128
ATTN_SCALE = HEAD_DIM ** -0.5
ROPE_THETA = 10000.0
AXIS_DIM = HEAD_DIM // 2
N_FREQ = AXIS_DIM // 2

CONV_CH = 1024
CONV_WIDTH = 31

CHUNK = 128
SGU_W = D_MODEL
SGU_GROUPS = 8
SGU_GW = SGU_W // SGU_GROUPS

EV_SPLITS = [KV_W, 2 * KV_W, 2 * KV_W + ATTN_W, 2 * KV_W + 2 * ATTN_W,
             2 * KV_W + 2 * ATTN_W + 2 * CONV_CH]
EV_IN = 2 * KV_W + 2 * ATTN_W + 3 * CONV_CH
EV_MIX = ATTN_W + CONV_CH
OD_IN = 3 * SGU_W

N_EVEN = (DEPTH + 1) // 2
N_ODD = DEPTH // 2
EPS = 1e-6

kernel_name = "hybrid_gqa_conformer_sgu_prefix_block"


def rmsnorm(x, g):
    xf = x.astype(jnp.float32)
    y = xf * lax.rsqrt(jnp.mean(xf * xf, axis=-1, keepdims=True) + EPS)
    return (y * g.astype(jnp.float32)).astype(x.dtype)


def layernorm(x, g, b):
    xf = x.astype(jnp.float32)
    mu = jnp.mean(xf, axis=-1, keepdims=True)
    var = jnp.mean(jnp.square(xf - mu), axis=-1, keepdims=True)
    y = (xf - mu) * lax.rsqrt(var + EPS)
    return (y * g.astype(jnp.float32) + b.astype(jnp.float32)).astype(x.dtype)


def modulate(h, shift, scale):
    return h * (1.0 + scale) + shift


def axial_rope_tables(n):
    rows = n // GRID_W
    row = jnp.repeat(jnp.arange(rows, dtype=jnp.float32), GRID_W)
    col = jnp.tile(jnp.arange(GRID_W, dtype=jnp.float32), rows)
    inv = jnp.power(ROPE_THETA, jnp.arange(N_FREQ, dtype=jnp.float32) * (-2.0 / AXIS_DIM))
    ang = jnp.concatenate([row[:, None] * inv, col[:, None] * inv], axis=-1)
    return jnp.cos(ang), jnp.sin(ang)


def apply_rope(x, cos, sin):
    shp = x.shape
    xf = x.astype(jnp.float32).reshape(shp[:-1] + (HEAD_DIM // 2, 2))
    x1, x2 = xf[..., 0], xf[..., 1]
    cs, sn = cos[None, :, None, :], sin[None, :, None, :]
    out = jnp.stack([x1 * cs - x2 * sn, x1 * sn + x2 * cs], axis=-1)
    return out.reshape(shp).astype(x.dtype)


def latent_attention(q, k_lat, v_lat, k_ctx, v_ctx):
    bsz, n = q.shape[:2]
    k_all = jnp.concatenate([k_lat, k_ctx], axis=1)
    v_all = jnp.concatenate([v_lat, v_ctx], axis=1)
    nb = n // Q_BLOCK
    qb = q.reshape(bsz, nb, Q_BLOCK, N_KV_HEADS, Q_PER_KV, HEAD_DIM).transpose(1, 0, 2, 3, 4, 5)

    def block(qi):
        s = jnp.einsum('bqhgd,bkhd->bhgqk', qi, k_all).astype(jnp.float32) * ATTN_SCALE
        p = jax.nn.softmax(s, axis=-1)
        return jnp.einsum('bhgqk,bkhd->bqhgd', p.astype(v_all.dtype), v_all)

    out = lax.map(block, qb)
    return out.transpose(1, 0, 2, 3, 4, 5).reshape(bsz, n, ATTN_W)


def context_attention(q, k, v):
    bsz, l = q.shape[:2]
    qg = q.reshape(bsz, l, N_KV_HEADS, Q_PER_KV, HEAD_DIM)
    s = jnp.einsum('bqhgd,bkhd->bhgqk', qg, k).astype(jnp.float32) * ATTN_SCALE
    p = jax.nn.softmax(s, axis=-1)
    o = jnp.einsum('bhgqk,bkhd->bqhgd', p.astype(v.dtype), v)
    return o.reshape(bsz, l, ATTN_W)


def conformer_conv(pair, dw_w, dw_b, ln_g, ln_b):
    a, b = jnp.split(pair, 2, axis=-1)
    y = a * jax.nn.sigmoid(b)
    y = lax.conv_general_dilated(
        y, dw_w[:, None, :], window_strides=(1,),
        padding=[(CONV_WIDTH // 2, CONV_WIDTH // 2)],
        dimension_numbers=('NWC', 'WIO', 'NWC'),
        feature_group_count=CONV_CH) + dw_b
    return jax.nn.silu(layernorm(y, ln_g, ln_b))


def spatial_gating(uv, ln_g, ln_b, ws, bs):
    bsz, n = uv.shape[:2]
    u, v = jnp.split(jax.nn.gelu(uv), 2, axis=-1)
    v = layernorm(v, ln_g, ln_b).reshape(bsz, n // CHUNK, CHUNK, SGU_GROUPS, SGU_GW)
    mixed = jnp.einsum('gij,bcjgd->bcigd', ws, v) + bs.T[None, None, :, :, None]
    return u * mixed.reshape(bsz, n, SGU_W)


def setup_inputs(seed: int = 0) -> dict:
    key = jax.random.key(seed)
    ks = iter(jax.random.split(key, 32))

    def nrm(shape, scale):
        return jax.random.normal(next(ks), shape, jnp.float32) * scale

    d = D_MODEL
    return {
        "x": nrm((BATCH, SEQ, d), 1.0),
        "c": nrm((BATCH, d), 1.0),
        "ctx": nrm((BATCH, CTX_LEN, d), 1.0),
        "c_ctx": nrm((d,), 1.0),
        "ada_w": nrm((DEPTH, d, 3 * d), 0.5 * d ** -0.5),
        "ada_b": nrm((DEPTH, 3 * d), 0.02),
        "norm_g": 1.0 + nrm((DEPTH, d), 0.02),
        "ev_w_in": nrm((N_EVEN, d, EV_IN), d ** -0.5),
        "ev_q_norm": 1.0 + nrm((N_EVEN, HEAD_DIM), 0.02),
        "ev_k_norm": 1.0 + nrm((N_EVEN, HEAD_DIM), 0.02),
        "ev_dw_w": nrm((N_EVEN, CONV_WIDTH, CONV_CH), CONV_WIDTH ** -0.5),
        "ev_dw_b": nrm((N_EVEN, CONV_CH), 0.02),
        "ev_ln_g": 1.0 + nrm((N_EVEN, CONV_CH), 0.02),
        "ev_ln_b": nrm((N_EVEN, CONV_CH), 0.02),
        "ev_w_out": nrm((N_EVEN, EV_MIX, d), EV_MIX ** -0.5),
        "od_w_in": nrm((N_ODD, d, OD_IN), d ** -0.5),
        "od_ln_g": 1.0 + nrm((N_ODD, SGU_W), 0.02),
        "od_ln_b": nrm((N_ODD, SGU_W), 0.02),
        "od_ws": nrm((N_ODD, SGU_GROUPS, CHUNK, CHUNK), CHUNK ** -0.5),
        "od_bs": 1.0 + nrm((N_ODD, SGU_GROUPS, CHUNK), 0.02),
        "od_w_out": nrm((N_ODD, SGU_W, d), SGU_W ** -0.5),
        "final_g": 1.0 + nrm((d,), 0.02),
    }


def reference(x, c, ctx, c_ctx, ada_w, ada_b, norm_g, ev_w_in, ev_q_norm, ev_k_norm,
              ev_dw_w, ev_dw_b, ev_ln_g, ev_ln_b, ev_w_out, od_w_in, od_ln_g, od_ln_b,
              od_ws, od_bs, od_w_out, final_g):
    bsz, n, _ = x.shape
    cos, sin = axial_rope_tables(n)
    sc = jax.nn.silu(c)
    scc = jax.nn.silu(c_ctx)
    xc = ctx
    lc = ctx.shape[1]
    for layer in range(DEPTH):
        ctx_needed = any(j % 2 == 0 for j in range(layer + 1, DEPTH))
        is_even = layer % 2 == 0
        mod = sc @ ada_w[layer] + ada_b[layer]
        shift, scale, gate = jnp.split(mod[:, None, :], 3, axis=-1)
        h = modulate(rmsnorm(x, norm_g[layer]), shift, scale)
        if is_even or ctx_needed:
            n_mod = 3 * D_MODEL if ctx_needed else 2 * D_MODEL
            mod_c = (scc @ ada_w[layer][:, :n_mod] + ada_b[layer][:n_mod])[None, None, :]
            hc = modulate(rmsnorm(xc, norm_g[layer]),
                          mod_c[..., :D_MODEL], mod_c[..., D_MODEL:2 * D_MODEL])
        if is_even:
            e = layer // 2
            w_in = ev_w_in[e]
            k, v, q, za, glu, zb = jnp.split(h @ w_in, EV_SPLITS, axis=-1)
            q = apply_rope(rmsnorm(q.reshape(bsz, n, N_Q_HEADS, HEAD_DIM), ev_q_norm[e]), cos, sin)
            k = apply_rope(rmsnorm(k.reshape(bsz, n, N_KV_HEADS, HEAD_DIM), ev_k_norm[e]), cos, sin)
            v = v.reshape(bsz, n, N_KV_HEADS, HEAD_DIM)
            if ctx_needed:
                kc, vc, qc, zac, gluc, zbc = jnp.split(hc @ w_in, EV_SPLITS, axis=-1)
            else:
                kc, vc = jnp.split(hc @ w_in[:, :2 * KV_W], 2, axis=-1)
            kc = rmsnorm(kc.reshape(bsz, lc, N_KV_HEADS, HEAD_DIM), ev_k_norm[e])
            vc = vc.reshape(bsz, lc, N_KV_HEADS, HEAD_DIM)
            attn = latent_attention(q, k, v, kc, vc)
            conv = conformer_conv(glu, ev_dw_w[e], ev_dw_b[e], ev_ln_g[e], ev_ln_b[e])
            mix = jnp.concatenate([attn * jax.nn.silu(za), conv * jax.nn.silu(zb)], axis=-1)
            x_new = x + gate * (mix @ ev_w_out[e])
            if ctx_needed:
                qc = rmsnorm(qc.reshape(bsz, lc, N_Q_HEADS, HEAD_DIM), ev_q_norm[e])
                attn_c = context_attention(qc, kc, vc)
                conv_c = conformer_conv(gluc, ev_dw_w[e], ev_dw_b[e], ev_ln_g[e], ev_ln_b[e])
                mix_c = jnp.concatenate([attn_c * jax.nn.silu(zac), conv_c * jax.nn.silu(zbc)], axis=-1)
                xc = xc + mod_c[..., 2 * D_MODEL:] * (mix_c @ ev_w_out[e])
            x = x_new
        else:
            o = layer // 2
            p = h @ od_w_in[o]
            mixed = spatial_gating(p[..., :2 * SGU_W], od_ln_g[o], od_ln_b[o], od_ws[o], od_bs[o])
            mixed = mixed * jax.nn.silu(p[..., 2 * SGU_W:])
            x_new = x + gate * (mixed @ od_w_out[o])
            if ctx_needed:
                pc = hc @ od_w_in[o]
                mixed_c = spatial_gating(pc[..., :2 * SGU_W], od_ln_g[o], od_ln_b[o], od_ws[o], od_bs[o])
                mixed_c = mixed_c * jax.nn.silu(pc[..., 2 * SGU_W:])
                xc = xc + mod_c[..., 2 * D_MODEL:] * (mixed_c @ od_w_out[o])
            x = x_new
    return rmsnorm(x, final_g)
```

```python
import contextlib
import numpy as np
import concourse.bass as bass
import concourse.mybir as mybir
from concourse.bass_utils import run_bass_kernel_spmd

F32 = mybir.dt.float32
BF16 = mybir.dt.bfloat16
AF = mybir.ActivationFunctionType
ALU = mybir.AluOpType
AX = mybir.AxisListType

ENGS = ("pe", "act", "dve", "pool", "sp")
D = 2048
NCH = 16
SEQ = 8192
CTX = 256
EPS = 1e-6
GC = 0.7978845608028654
GA = 0.044715


class _Op:
    __slots__ = ("eng", "fn", "deps", "stream", "idx", "signal", "is_dma", "semval")

    def __init__(self, eng, fn, deps, stream, is_dma):
        self.eng = eng
        self.fn = fn
        self.deps = deps
        self.stream = stream
        self.idx = None
        self.signal = False
        self.is_dma = is_dma
        self.semval = None


class Sched:
    def __init__(self, nc, es, n_dma_sems=None):
        self.nc = nc
        nd = n_dma_sems or {"sp": 8, "pool": 6, "act": 2}
        self.dma_pool = {q: ["dma_%s_%d" % (q, i) for i in range(n)] for q, n in nd.items()}
        self.dma_rr = {q: 0 for q in nd}
        self.stream_names = list(ENGS) + [s for q in self.dma_pool for s in self.dma_pool[q]]
        self.sems = {s: es.enter_context(nc.semaphore("s_" + s)) for s in self.stream_names}
        self.semcount = {s: 0 for s in self.stream_names}
        self.streams = {s: [] for s in self.stream_names}
        self.pending = {e: [] for e in ENGS}
        self.seen = {e: {} for e in ENGS}
        self.bufs = {}
        self.nops = 0

    def _bs(self, k):
        st = self.bufs.get(k)
        if st is None:
            st = [None, {}]
            self.bufs[k] = st
        return st

    def _collect(self, reads, writes):
        deps = []
        for k in reads:
            st = self._bs(k)
            if st[0] is not None:
                deps.append(st[0])
        for k in writes:
            st = self._bs(k)
            if st[0] is not None:
                deps.append(st[0])
            deps.extend(st[1].values())
        return deps

    def _commit(self, op, reads, writes):
        for k in reads:
            st = self._bs(k)
            cur = st[1].get(op.stream)
            if cur is None or cur.idx < op.idx:
                st[1][op.stream] = op
        for k in writes:
            st = self._bs(k)
            st[0] = op
            st[1] = {}

    def op(self, eng, fn, reads=(), writes=()):
        deps = self._collect(reads, writes)
        o = _Op(eng, fn, deps, eng, False)
        o.idx = len(self.streams[eng])
        self.streams[eng].append(o)
        self.pending[eng].append(o)
        self._commit(o, reads, writes)
        self.nops += 1
        return o

    def dma(self, q, fn, reads=(), writes=()):
        pool = self.dma_pool[q]
        s = pool[self.dma_rr[q] % len(pool)]
        self.dma_rr[q] += 1
        deps = self._collect(reads, writes)
        if self.streams[s]:
            deps.append(self.streams[s][-1])
        o = _Op(q, fn, deps, s, True)
        o.idx = len(self.streams[s])
        self.streams[s].append(o)
        self.pending[q].append(o)
        self._commit(o, reads, writes)
        self.nops += 1
        return o

    def emit_block(self):
        nc = self.nc
        plan = {e: [] for e in ENGS}
        for e in ENGS:
            seen = self.seen[e]
            for o in self.pending[e]:
                need = {}
                for d in o.deps:
                    if d.stream == "pe" and e == "pe":
                        continue
                    if seen.get(d.stream, -1) >= d.idx:
                        continue
                    if d.stream not in need or need[d.stream].idx < d.idx:
                        need[d.stream] = d
                for sname, d in need.items():
                    seen[sname] = d.idx
                    d.signal = True
                plan[e].append((o, list(need.values())))
        lasts = []
        for s in self.stream_names:
            lst = self.streams[s]
            if lst and lst[-1].semval is None:
                lst[-1].signal = True
            if lst:
                lasts.append(lst[-1])
        newops = []
        for e in ENGS:
            newops.extend(self.pending[e])
        for s in self.stream_names:
            for o in self.streams[s]:
                if o.semval is not None:
                    continue
                if o.is_dma:
                    self.semcount[s] += 16
                    o.semval = self.semcount[s]
                elif o.signal:
                    self.semcount[s] += 1
                    o.semval = self.semcount[s]
                else:
                    o.semval = -1
        sems = self.sems
        with nc.Block() as block:
            def run(e):
                def body(h):
                    for o, waits in plan[e]:
                        for d in waits:
                            h.wait_ge(sems[d.stream], d.semval)
                        ins = o.fn(h)
                        if o.is_dma:
                            ins.then_inc(sems[o.stream], 16)
                        elif o.signal:
                            ins.then_inc(sems[o.stream], 1)
                    for d in lasts:
                        if self.seen[e].get(d.stream, -1) < d.idx:
                            h.wait_ge(sems[d.stream], d.semval)
                            self.seen[e][d.stream] = d.idx
                return body

            block.tensor(run("pe"))
            block.scalar(run("act"))
            block.vector(run("dve"))
            block.gpsimd(run("pool"))
            block.sync(run("sp"))
        self.pending = {e: [] for e in ENGS}
        self.bufs = {}


class Ctx:
    pass


def _common_consts(nc, S, es, C):
    sb = lambda n, s, d: es.enter_context(nc.sbuf_tensor("t_" + n, s, d))
    C.ident = sb("ident", [128, 128], BF16)
    C.ones_bf = sb("ones_bf", [1, 128], BF16)
    C.ones_f = sb("ones_f", [1, 128], F32)
    C.mhalf = sb("mhalf", [128, 512], F32)
    S.op("pool", lambda h: h.memset(C.ident[:], 1.0), writes=["ident"])
    S.op("pool", lambda h: h.affine_select(out=C.ident[:], in_=C.ident[:], pattern=[[-1, 128]],
                                          compare_op=ALU.is_equal, fill=0.0, base=0, channel_multiplier=1),
         reads=["ident"], writes=["ident"])
    S.op("pool", lambda h: h.memset(C.ones_bf[:], 1.0), writes=["ones_bf"])
    S.op("pool", lambda h: h.memset(C.ones_f[:], 1.0), writes=["ones_f"])
    S.op("pool", lambda h: h.memset(C.mhalf[:], -0.5), writes=["mhalf"])


def _rstd(S, C, out_ap, in_ap, mult, add, rkeys, wkey, n=1):
    tmpk = ("rstd_tmp", wkey)
    S.op("dve", lambda h: h.tensor_scalar(out=out_ap, in0=in_ap, scalar1=mult, scalar2=add,
                                         op0=ALU.mult, op1=ALU.add), reads=rkeys, writes=[tmpk, wkey])
    S.op("pool", lambda h: h.tensor_tensor(out=out_ap, in0=out_ap, in1=C.mhalf[:, 0:n], op=ALU.pow),
         reads=[tmpk, "mhalf"], writes=[wkey])


def _mod_rows(S, nc, C, es, ada_w_ap, ada_b_ap, ccol_ap, ncols, tag, consume):
    sb = lambda n, s, d: es.enter_context(nc.sbuf_tensor("t_" + n, s, d))
    cc = sb("cc_" + tag, [128, NCH], F32)
    ct = sb("ct_" + tag, [128, NCH], F32)
    scb = sb("scb_" + tag, [128, NCH], BF16)
    S.dma("sp", lambda h: h.dma_start(out=cc[:], in_=ccol_ap), writes=["cc" + tag])
    S.op("act", lambda h: h.activation(out=ct[:], in_=cc[:], func=AF.Tanh, scale=0.5),
         reads=["cc" + tag], writes=["ct" + tag])
    S.op("dve", lambda h: h.scalar_tensor_tensor(out=ct[:], in0=ct[:], scalar=1.0, in1=cc[:],
                                                op0=ALU.add, op1=ALU.mult),
         reads=["ct" + tag, "cc" + tag], writes=["ct" + tag])
    S.op("dve", lambda h: h.tensor_scalar(out=scb[:], in0=ct[:], scalar1=0.5, scalar2=None, op0=ALU.mult),
         reads=["ct" + tag], writes=["scb" + tag])
    wv = ada_w_ap.rearrange("(k p) n -> p k n", p=128)
    for blk in range(ncols // 512):
        wb = C.wbuf[blk % 2]
        wk = ("wbuf", blk % 2)
        S.dma("pool", lambda h, wb=wb, blk=blk: h.dma_start(out=wb[:], in_=wv[:, :, blk * 512:(blk + 1) * 512]),
              writes=[wk])
        rb = C.rowb[blk % 2]
        rk = ("rowb", blk % 2)
        S.dma("sp", lambda h, rb=rb, blk=blk: h.dma_start(out=rb[0:1, :], in_=ada_b_ap[0:1, blk * 512:(blk + 1) * 512]),
              writes=[rk])
        ps = C.psF[blk % 2]
        pk = ("psF", blk % 2)
        for k in range(NCH):
            S.op("pe", lambda h, ps=ps, wb=wb, k=k: h.matmul(ps[0:1, :], lhsT=scb[:, k:k + 1], rhs=wb[:, k, :],
                                                             start=(k == 0), stop=(k == NCH - 1)),
                 reads=["scb" + tag, wk], writes=[pk])
        S.op("dve", lambda h, ps=ps, rb=rb: h.tensor_tensor(out=rb[0:1, :], in0=ps[0:1, :], in1=rb[0:1, :], op=ALU.add),
             reads=[pk, rk], writes=[rk])
        consume(blk, rb, rk)


def _bcast_rows(S, C, rb, rk, dst_ap, dkey, post=None):
    ps = C.psF[2]
    pk = ("psF", 2)
    S.op("pe", lambda h: h.matmul(ps[:, :], lhsT=C.ones_f[0:1, :], rhs=rb[0:1, :], start=True, stop=True),
         reads=[rk, "ones_f"], writes=[pk])
    if post is None:
        S.op("act", lambda h: h.activation(out=dst_ap, in_=ps[:, :], func=AF.Copy), reads=[pk], writes=[dkey])
    else:
        post(ps, pk)


def _row_to_cols(S, C, rb, rk, dst_ap, dkey, base):
    ps = C.psF[3]
    pk = ("psF", 3)
    for j in range(4):
        S.op("pe", lambda h, j=j: h.matmul(ps[:, j:j + 1], lhsT=rb[0:1, j * 128:(j + 1) * 128], rhs=C.ones_f[0:1, 0:1],
                                           start=True, stop=True),
             reads=[rk, "ones_f"], writes=[pk])
    S.op("act", lambda h: h.activation(out=dst_ap, in_=ps[:, 0:4], func=AF.Copy), reads=[pk], writes=[dkey])


def _norm_transpose(S, C, xsrc_ap, xkeys, sbc, sbc_keys, hT_dst_fn, hT_keys, tagi):
    i2 = tagi % 2
    st = C.stat
    sk = ("stat_n", i2)
    junk = C.junk
    S.op("act", lambda h: h.activation(out=junk[:], in_=xsrc_ap, func=AF.Square, accum_out=st[:, i2:i2 + 1]),
         reads=list(xkeys), writes=["junk", ("hn", i2), sk])
    _rstd(S, C, st[:, 2 + i2:3 + i2], st[:, i2:i2 + 1], 1.0 / D, EPS, [sk], ("stat_r", i2))
    hn = C.hn[i2]
    hk = ("hn", i2)
    S.op("dve", lambda h: h.scalar_tensor_tensor(out=hn[:], in0=xsrc_ap, scalar=st[:, 2 + i2:3 + i2], in1=sbc[:],
                                                op0=ALU.mult, op1=ALU.mult),
         reads=list(xkeys) + [("stat_r", i2)] + list(sbc_keys), writes=[hk])
    for half in range(2):
        pT = C.psT[half]
        pk = ("psT", half)
        for c in range(8):
            cc = half * 8 + c
            S.op("pe", lambda h, pT=pT, c=c, cc=cc: h.transpose(out=pT[:, c * 128:(c + 1) * 128],
                                                               in_=hn[:, cc * 128:(cc + 1) * 128], identity=C.ident[:]),
                 reads=[hk, "ident"], writes=[pk])
        dst = hT_dst_fn(half)
        src = pT[:, :].rearrange("p (c t) -> p c t", c=8)
        if half == 0:
            S.op("act", lambda h, dst=dst, src=src: h.activation(out=dst, in_=src, func=AF.Copy),
                 reads=[pk], writes=list(hT_keys))
        else:
            S.op("dve", lambda h, dst=dst, src=src: h.tensor_copy(out=dst, in_=src),
                 reads=[pk], writes=list(hT_keys))


def emit_l1(S, nc, C, es, NT, G, x1_ap, out_ap, dr):
    sb = lambda n, s, d: es.enter_context(nc.sbuf_tensor("t_" + n, s, d))
    GT = G * 128
    C.hn = [sb("hn%d" % i, [128, D], BF16) for i in range(2)]
    C.junk = sb("junk", [128, D], BF16)
    C.stat = sb("stat", [128, 32], F32)
    hT = sb("hT", [128, NCH, GT], BF16)
    mixT = sb("mixT", [128, NCH, GT], BF16)
    A = [sb("A%d" % i, [128, D], F32) for i in range(G)]
    C.wbuf = [sb("wbuf%d" % i, [128, NCH, 512], BF16) for i in range(2)]
    C.rowb = [sb("rowb%d" % i, [1, 512], F32) for i in range(2)]
    s_bc = sb("s_bc", [128, D], F32)
    gate_bc = sb("gate_bc", [128, D], F32)
    lng_bc = sb("lng_bc", [128, D], F32)
    lnb_bc = sb("lnb_bc", [128, D], F32)
    fg_bc = sb("fg_bc", [128, D], F32)
    shT = sb("shT", [128, NCH], BF16)
    vhat = sb("vhat", [128, D], BF16)
    tmp1 = [sb("tmp1_%d" % i, [128, 512], F32) for i in range(2)]
    tmp2 = [sb("tmp2_%d" % i, [128, 512], F32) for i in range(2)]
    mixblk = [sb("mixblk%d" % i, [128, 512], BF16) for i in range(2)]
    brow = [sb("brow%d" % i, [1, 512], BF16) for i in range(2)]
    wsT = sb("wsT", [128, 8, 128], BF16)
    bscol = sb("bscol", [128, 8], F32)
    st2 = sb("st2", [128, 16], F32)

    Ak = lambda t: [("A", t, b) for b in range(4)]

    S.dma("sp", lambda h: h.dma_start(out=s_bc[:], in_=dr["norm_g1_bc"]), writes=["s_bc"])
    S.dma("sp", lambda h: h.dma_start(out=lng_bc[:], in_=dr["lng_bc"]), writes=["lng_bc"])
    S.dma("sp", lambda h: h.dma_start(out=lnb_bc[:], in_=dr["lnb_bc"]), writes=["lnb_bc"])
    S.dma("sp", lambda h: h.dma_start(out=fg_bc[:], in_=dr["fg_bc"]), writes=["fg_bc"])
    S.dma("sp", lambda h: h.dma_start(out=bscol[:], in_=dr["bs_col"]), writes=["bscol"])
    S.dma("pool", lambda h: h.dma_start(out=wsT[:], in_=dr["wsT"]), writes=["wsT"])

    def consume(blk, rb, rk):
        sec, j = divmod(blk, 4)
        cols = slice(j * 512, (j + 1) * 512)
        if sec == 0:
            _row_to_cols(S, C, rb, rk, shT[:, j * 4:(j + 1) * 4], "shT", j * 4)
        elif sec == 1:
            def post(ps, pk):
                S.op("dve", lambda h: h.scalar_tensor_tensor(out=s_bc[:, cols], in0=ps[:, :], scalar=1.0,
                                                            in1=s_bc[:, cols], op0=ALU.add, op1=ALU.mult),
                     reads=[pk, "s_bc"], writes=["s_bc"])
            _bcast_rows(S, C, rb, rk, None, None, post)
        else:
            _bcast_rows(S, C, rb, rk, gate_bc[:, cols], "gate_bc")

    _mod_rows(S, nc, C, es, dr["ada_w1"], dr["ada_b1"], dr["c_col"], 3 * D, "l1", consume)

    w_in = dr["od_w_in"].rearrange("(k p) n -> p k n", p=128)
    w_out = dr["od_w_out"].rearrange("(k p) n -> p k n", p=128)
    x1t = x1_ap.rearrange("(t p) d -> t p d", p=128)
    outt = out_ap.rearrange("(t p) d -> t p d", p=128)

    wctr = [0]
    psctr = [0]
    tctr = [0]
    out_dmas = []

    def load_w(src_view, c0):
        i = wctr[0] % 2
        wctr[0] += 1
        wb = C.wbuf[i]
        S.dma("pool", lambda h: h.dma_start(out=wb[:], in_=src_view[:, :, c0:c0 + 512]), writes=[("wbuf", i)])
        return wb, ("wbuf", i)

    def bias_row(wb, wk):
        i = wctr[0] % 2
        ps = C.psF[4 + i]
        pk = ("psF", 4 + i)
        for k in range(NCH):
            S.op("pe", lambda h, k=k: h.matmul(ps[0:1, :], lhsT=shT[:, k:k + 1], rhs=wb[:, k, :],
                                               start=(k == 0), stop=(k == NCH - 1)),
                 reads=["shT", wk], writes=[pk])
        br = brow[i]
        S.op("act", lambda h: h.activation(out=br[0:1, :], in_=ps[0:1, :], func=AF.Copy),
             reads=[pk], writes=[("brow", i)])
        return br, ("brow", i)

    def proj_tile(lhs_buf, lhs_keys, tl, wb, wk, br, bk):
        i = psctr[0] % 4
        psctr[0] += 1
        ps = C.psF[i]
        pk = ("psF", i)
        for k in range(NCH):
            S.op("pe", lambda h, k=k: h.matmul(ps[:, :], lhsT=lhs_buf[:, k, tl * 128:(tl + 1) * 128], rhs=wb[:, k, :],
                                               start=(k == 0), stop=(k == NCH - 1 and br is None)),
                 reads=list(lhs_keys) + [wk], writes=[pk])
        if br is not None:
            S.op("pe", lambda h: h.matmul(ps[:, :], lhsT=C.ones_bf[0:1, :], rhs=br[0:1, :], start=False, stop=True),
                 reads=["ones_bf", bk], writes=[pk])
        return ps, pk

    def gelu2(ps, pk, out_fn, outkeys, extra_reads=()):
        i = tctr[0] % 2
        tctr[0] += 1
        t1, t2 = tmp1[i], tmp2[i]
        k1, k2 = ("tmp1", i), ("tmp2", i)
        S.op("act", lambda h: h.activation(out=t1[:], in_=ps[:, :], func=AF.Square, scale=float(np.sqrt(GA))),
             reads=[pk], writes=[k1])
        S.op("dve", lambda h: h.scalar_tensor_tensor(out=t2[:], in0=t1[:], scalar=1.0, in1=ps[:, :],
                                                    op0=ALU.add, op1=ALU.mult), reads=[k1, pk], writes=[k2])
        S.op("act", lambda h: h.activation(out=t1[:], in_=t2[:], func=AF.Tanh, scale=GC),
             reads=[k2], writes=[k1])
        return t1, k1, t2, k2

    for grp in range(NT // G):
        for tl in range(G):
            tg = grp * G + tl
            S.dma("sp", lambda h, tl=tl, tg=tg: h.dma_start(out=A[tl][:], in_=x1t[tg]), writes=Ak(tl))
            _norm_transpose(S, C, A[tl][:], Ak(tl), s_bc, ["s_bc"],
                            lambda half, tl=tl: hT[:, half * 8:(half + 1) * 8, tl * 128:(tl + 1) * 128],
                            [("hT", tl)], tg)
        hT_keys = [("hT", t) for t in range(G)]
        for vb in range(4):
            wb, wk = load_w(w_in, D + vb * 512)
            br, bk = bias_row(wb, wk)
            for tl in range(G):
                ps, pk = proj_tile(hT, [("hT", tl)], tl, wb, wk, br, bk)
                t1, k1, t2, k2 = gelu2(ps, pk, None, None)
                S.op("dve", lambda h, t1=t1, ps=ps, tl=tl, vb=vb: h.scalar_tensor_tensor(
                    out=A[tl][:, vb * 512:(vb + 1) * 512], in0=t1[:], scalar=1.0, in1=ps[:, :],
                    op0=ALU.add, op1=ALU.mult), reads=[k1, pk], writes=[("A", tl, vb)])
        for tl in range(G):
            i2 = tl % 2
            b0 = i2 * 8
            S.op("act", lambda h, tl=tl, b0=b0: h.activation(out=C.junk[:], in_=A[tl][:], func=AF.Identity,
                                                           accum_out=st2[:, b0:b0 + 1]),
                 reads=Ak(tl), writes=["junk", ("st2s", i2)])
            S.op("act", lambda h, tl=tl, b0=b0: h.activation(out=C.junk[:], in_=A[tl][:], func=AF.Square,
                                                           accum_out=st2[:, b0 + 1:b0 + 2]),
                 reads=Ak(tl), writes=["junk", ("st2q", i2)])
            S.op("dve", lambda h, b0=b0: h.tensor_scalar(out=st2[:, b0 + 2:b0 + 3], in0=st2[:, b0:b0 + 1],
                                                        scalar1=1.0 / D, scalar2=None, op0=ALU.mult),
                 reads=[("st2s", i2)], writes=[("st2m", i2)])
            S.op("dve", lambda h, b0=b0: h.tensor_tensor(out=st2[:, b0 + 3:b0 + 4], in0=st2[:, b0 + 2:b0 + 3],
                                                        in1=st2[:, b0 + 2:b0 + 3], op=ALU.mult),
                 reads=[("st2m", i2)], writes=[("st2mm", i2)])
            S.op("dve", lambda h, b0=b0: h.scalar_tensor_tensor(out=st2[:, b0 + 4:b0 + 5], in0=st2[:, b0 + 1:b0 + 2],
                                                               scalar=1.0 / D, in1=st2[:, b0 + 3:b0 + 4],
                                                               op0=ALU.mult, op1=ALU.subtract),
                 reads=[("st2q", i2), ("st2mm", i2)], writes=[("st2v", i2)])
            _rstd(S, C, st2[:, b0 + 5:b0 + 6], st2[:, b0 + 4:b0 + 5], 0.25, EPS, [("st2v", i2)], ("st2r", i2))
            S.op("dve", lambda h, b0=b0: h.tensor_scalar(out=st2[:, b0 + 5:b0 + 6], in0=st2[:, b0 + 5:b0 + 6],
                                                        scalar1=0.5, scalar2=None, op0=ALU.mult),
                 reads=[("st2r", i2)], writes=[("st2r", i2)])
            S.op("dve", lambda h, b0=b0: h.scalar_tensor_tensor(out=st2[:, b0 + 6:b0 + 7], in0=st2[:, b0 + 2:b0 + 3],
                                                               scalar=-1.0, in1=st2[:, b0 + 5:b0 + 6],
                                                               op0=ALU.mult, op1=ALU.mult),
                 reads=[("st2m", i2), ("st2r", i2)], writes=[("st2nb", i2)])
            S.op("act", lambda h, tl=tl, b0=b0: h.activation(out=A[tl][:], in_=A[tl][:], func=AF.Identity,
                                                           scale=st2[:, b0 + 5:b0 + 6], bias=st2[:, b0 + 6:b0 + 7]),
                 reads=Ak(tl) + [("st2r", i2), ("st2nb", i2)], writes=Ak(tl))
            S.op("dve", lambda h, tl=tl: h.tensor_tensor(out=A[tl][:], in0=A[tl][:], in1=lng_bc[:], op=ALU.mult),
                 reads=Ak(tl) + ["lng_bc"], writes=Ak(tl))
            S.op("dve", lambda h, tl=tl: h.tensor_tensor(out=vhat[:], in0=A[tl][:], in1=lnb_bc[:], op=ALU.add),
                 reads=Ak(tl) + ["lnb_bc"], writes=["vhat"])
            for gp in range(4):
                ps = C.psF[4 + gp % 2]
                pk = ("psF", 4 + gp % 2)
                for j in range(2):
                    g = gp * 2 + j
                    S.op("pe", lambda h, ps=ps, g=g, j=j: h.matmul(ps[:, j * 256:(j + 1) * 256], lhsT=wsT[:, g, :],
                                                                  rhs=vhat[:, g * 256:(g + 1) * 256], start=True, stop=True),
                         reads=["wsT", "vhat"], writes=[pk])
                for j in range(2):
                    g = gp * 2 + j
                    S.op("act", lambda h, ps=ps, g=g, j=j, tl=tl: h.activation(
                        out=A[tl][:, g * 256:(g + 1) * 256], in_=ps[:, j * 256:(j + 1) * 256], func=AF.Identity,
                        bias=bscol[:, g:g + 1]), reads=[pk, "bscol"], writes=[("A", tl, gp)])
        for kb in range(4):
            wb, wk = load_w(w_in, kb * 512)
            br, bk = bias_row(wb, wk)
            for tl in range(G):
                ps, pk = proj_tile(hT, [("hT", tl)], tl, wb, wk, br, bk)
                t1, k1, t2, k2 = gelu2(ps, pk, None, None)
                S.op("dve", lambda h, t1=t1, t2=t2, ps=ps: h.scalar_tensor_tensor(
                    out=t2[:], in0=t1[:], scalar=1.0, in1=ps[:, :], op0=ALU.add, op1=ALU.mult),
                    reads=[k1, pk], writes=[k2])
                S.op("pool", lambda h, t2=t2, tl=tl, kb=kb: h.tensor_tensor(
                    out=A[tl][:, kb * 512:(kb + 1) * 512], in0=A[tl][:, kb * 512:(kb + 1) * 512], in1=t2[:], op=ALU.mult),
                    reads=[k2, ("A", tl, kb)], writes=[("A", tl, kb)])
            wb, wk = load_w(w_in, 2 * D + kb * 512)
            br, bk = bias_row(wb, wk)
            for tl in range(G):
                ps, pk = proj_tile(hT, [("hT", tl)], tl, wb, wk, br, bk)
                i = tctr[0] % 2
                tctr[0] += 1
                t1, t2, mb = tmp1[i], tmp2[i], mixblk[i]
                k1, k2, mk = ("tmp1", i), ("tmp2", i), ("mixblk", i)
                S.op("act", lambda h, t1=t1, ps=ps: h.activation(out=t1[:], in_=ps[:, :], func=AF.Tanh, scale=0.5),
                     reads=[pk], writes=[k1])
                S.op("dve", lambda h, t1=t1, t2=t2, ps=ps: h.scalar_tensor_tensor(
                    out=t2[:], in0=t1[:], scalar=1.0, in1=ps[:, :], op0=ALU.add, op1=ALU.mult),
                    reads=[k1, pk], writes=[k2])
                S.op("dve", lambda h, t2=t2, mb=mb, tl=tl, kb=kb: h.scalar_tensor_tensor(
                    out=mb[:], in0=A[tl][:, kb * 512:(kb + 1) * 512], scalar=0.25, in1=t2[:],
                    op0=ALU.mult, op1=ALU.mult), reads=[k2, ("A", tl, kb)], writes=[mk])
                pT = C.psT[i]
                ptk = ("psT", i)
                for j in range(4):
                    S.op("pe", lambda h, pT=pT, mb=mb, j=j: h.transpose(out=pT[:, j * 128:(j + 1) * 128],
                                                                       in_=mb[:, j * 128:(j + 1) * 128], identity=C.ident[:]),
                         reads=[mk, "ident"], writes=[ptk])
                S.op("act", lambda h, pT=pT, tl=tl, kb=kb: h.activation(
                    out=mixT[:, kb * 4:(kb + 1) * 4, tl * 128:(tl + 1) * 128],
                    in_=pT[:, 0:512].rearrange("p (c t) -> p c t", c=4), func=AF.Copy),
                    reads=[ptk], writes=[("mixT", tl, kb)])
        for tl in range(G):
            tg = grp * G + tl
            S.dma("sp", lambda h, tl=tl, tg=tg: h.dma_start(out=A[tl][:], in_=x1t[tg]), writes=Ak(tl))
        for ob in range(4):
            wb, wk = load_w(w_out, ob * 512)
            for tl in range(G):
                ps, pk = proj_tile(mixT, [("mixT", tl, b) for b in range(4)], tl, wb, wk, None, None)
                i = tctr[0] % 2
                tctr[0] += 1
                t2 = tmp2[i]
                k2 = ("tmp2", i)
                S.op("dve", lambda h, t2=t2, ps=ps, ob=ob: h.tensor_tensor(
                    out=t2[:], in0=ps[:, :], in1=gate_bc[:, ob * 512:(ob + 1) * 512], op=ALU.mult),
                    reads=[pk, "gate_bc"], writes=[k2])
                S.op("pool", lambda h, t2=t2, tl=tl, ob=ob: h.tensor_tensor(
                    out=A[tl][:, ob * 512:(ob + 1) * 512], in0=A[tl][:, ob * 512:(ob + 1) * 512], in1=t2[:], op=ALU.add),
                    reads=[k2, ("A", tl, ob)], writes=[("A", tl, ob)])
        for tl in range(G):
            tg = grp * G + tl
            i2 = tl % 2
            S.op("act", lambda h, tl=tl, i2=i2: h.activation(out=C.junk[:], in_=A[tl][:], func=AF.Square,
                                                           accum_out=C.stat[:, 8 + i2:9 + i2]),
                 reads=Ak(tl), writes=["junk", ("fst", i2)])
            _rstd(S, C, C.stat[:, 10 + i2:11 + i2], C.stat[:, 8 + i2:9 + i2], 1.0 / D, EPS, [("fst", i2)], ("fsr", i2))
            S.op("dve", lambda h, tl=tl, i2=i2: h.scalar_tensor_tensor(
                out=A[tl][:], in0=A[tl][:], scalar=C.stat[:, 10 + i2:11 + i2], in1=fg_bc[:],
                op0=ALU.mult, op1=ALU.mult), reads=Ak(tl) + [("fsr", i2), "fg_bc"], writes=Ak(tl))
            out_dmas.append(S.dma("sp", lambda h, tl=tl, tg=tg: h.dma_start(out=outt[tg], in_=A[tl][:]), reads=Ak(tl)))
    return out_dmas


def build_l1_only(NT=16, G=4):
    nc = bass.Bass("TRN2", target_bir_lowering=False)
    T = NT * 128
    dr = {}
    def din(name, shape):
        dr[name] = nc.dram_tensor(name, list(shape), F32, kind="ExternalInput").ap()
    din("x1", (T, D))
    din("c_col", (128, NCH))
    din("ada_w1", (D, 3 * D))
    din("ada_b1", (1, 3 * D))
    din("norm_g1_bc", (128, D))
    din("od_w_in", (D, 3 * D))
    din("od_w_out", (D, D))
    din("lng_bc", (128, D))
    din("lnb_bc", (128, D))
    din("wsT", (128, 8, 128))
    din("bs_col", (128, 8))
    din("fg_bc", (128, D))
    out = nc.dram_tensor("out", [T, D], F32, kind="ExternalOutput").ap()
    with contextlib.ExitStack() as es_top:
        S = Sched(nc, es_top)
        C = Ctx()
        C.psF = [es_top.enter_context(nc.psum_tensor("psF%d" % i, [128, 512], F32)) for i in range(6)]
        C.psT = [es_top.enter_context(nc.psum_tensor("psT%d" % i, [128, 1024], BF16)) for i in range(2)]
        _common_consts(nc, S, es_top, C)
        with contextlib.ExitStack() as es1:
            emit_l1(S, nc, C, es1, NT, G, dr["x1"], out, dr)
            S.emit_block()
    return nc


def l1_host_inputs(c_b, ada_w1, ada_b1, norm_g1, od_w_in, od_ln_g, od_ln_b, od_ws, od_bs, od_w_out, final_g):
    bc = lambda v: np.ascontiguousarray(np.broadcast_to(np.asarray(v, np.float32)[None, :], (128, v.shape[0])))
    return {
        "c_col": np.ascontiguousarray(np.asarray(c_b, np.float32).reshape(NCH, 128).T),
        "ada_w1": np.ascontiguousarray(ada_w1, dtype=np.float32),
        "ada_b1": np.ascontiguousarray(np.asarray(ada_b1, np.float32)[None, :]),
        "norm_g1_bc": bc(norm_g1),
        "od_w_in": np.ascontiguousarray(od_w_in, dtype=np.float32),
        "od_w_out": np.ascontiguousarray(od_w_out, dtype=np.float32),
        "lng_bc": bc(od_ln_g),
        "lnb_bc": bc(od_ln_b),
        "wsT": np.ascontiguousarray(np.transpose(np.asarray(od_ws, np.float32), (2, 0, 1))),
        "bs_col": np.ascontiguousarray(np.asarray(od_bs, np.float32).T),
        "fg_bc": bc(final_g),
    }


def _mod_section(S, nc, C, es, ada_w_ap, ada_b_ap, scb, scbk, c0, nblk, consume, tag):
    wv = ada_w_ap.rearrange("(k p) n -> p k n", p=128)
    for blk in range(nblk):
        cs = c0 + blk * 512
        wb = C.wbuf[blk % 2]
        wk = ("wbuf", blk % 2)
        S.dma("pool", lambda h, wb=wb, cs=cs: h.dma_start(out=wb[:], in_=wv[:, :, cs:cs + 512]), writes=[wk])
        rb = C.rowb[blk % 2]
        rk = ("rowb", blk % 2)
        S.dma("sp", lambda h, rb=rb, cs=cs: h.dma_start(out=rb[0:1, :], in_=ada_b_ap[0:1, cs:cs + 512]), writes=[rk])
        ps = C.psF[blk % 2]
        pk = ("psF", blk % 2)
        for k in range(NCH):
            S.op("pe", lambda h, ps=ps, wb=wb, k=k: h.matmul(ps[0:1, :], lhsT=scb[:, k:k + 1], rhs=wb[:, k, :],
                                                             start=(k == 0), stop=(k == NCH - 1)),
                 reads=[scbk, wk], writes=[pk])
        S.op("dve", lambda h, ps=ps, rb=rb: h.tensor_tensor(out=rb[0:1, :], in0=ps[0:1, :], in1=rb[0:1, :], op=ALU.add),
             reads=[pk, rk], writes=[rk])
        consume(blk, rb, rk)


def _silu_cols(S, nc, C, es, ccol_ap, tag):
    sb = lambda n, s, d: es.enter_context(nc.sbuf_tensor("t_" + n, s, d))
    cc = sb("cc_" + tag, [128, NCH], F32)
    ct = sb("ct_" + tag, [128, NCH], F32)
    scb = sb("scb_" + tag, [128, NCH], BF16)
    S.dma("sp", lambda h: h.dma_start(out=cc[:], in_=ccol_ap), writes=["cc" + tag])
    S.op("act", lambda h: h.activation(out=ct[:], in_=cc[:], func=AF.Tanh, scale=0.5),
         reads=["cc" + tag], writes=["ct" + tag])
    S.op("dve", lambda h: h.scalar_tensor_tensor(out=ct[:], in0=ct[:], scalar=1.0, in1=cc[:],
                                                op0=ALU.add, op1=ALU.mult),
         reads=["ct" + tag, "cc" + tag], writes=["ct" + tag])
    S.op("dve", lambda h: h.tensor_scalar(out=scb[:], in0=ct[:], scalar1=0.5, scalar2=None, op0=ALU.mult),
         reads=["ct" + tag], writes=["scb" + tag])
    return scb, "scb" + tag


def _rope(S, eng, src, dst, rt, H, tmps, rkeys, wkeys, tkey):
    sv = src.rearrange("p (h i two) -> p h i two", h=H, two=2)
    dv = dst.rearrange("p (h i two) -> p h i two", h=H, two=2)
    e, o = sv[:, :, :, 0], sv[:, :, :, 1]
    cosb = rt[:, 0:64].unsqueeze(1).broadcast_to([128, H, 64])
    sinb = rt[:, 64:128].unsqueeze(1).broadcast_to([128, H, 64])
    ta = tmps[0].rearrange("p (h i) -> p h i", h=H)
    tb = tmps[1].rearrange("p (h i) -> p h i", h=H)
    ka, kb = (tkey, 0), (tkey, 1)
    S.op(eng, lambda h: h.tensor_tensor(out=ta, in0=e, in1=cosb, op=ALU.mult), reads=rkeys, writes=[ka])
    S.op(eng, lambda h: h.tensor_tensor(out=tb, in0=o, in1=sinb, op=ALU.mult), reads=rkeys, writes=[kb])
    S.op(eng, lambda h: h.tensor_tensor(out=dv[:, :, :, 0], in0=ta, in1=tb, op=ALU.subtract),
         reads=[ka, kb], writes=wkeys)
    S.op(eng, lambda h: h.tensor_tensor(out=ta, in0=e, in1=sinb, op=ALU.mult), reads=rkeys + wkeys, writes=[ka])
    S.op(eng, lambda h: h.tensor_tensor(out=tb, in0=o, in1=cosb, op=ALU.mult), reads=rkeys, writes=[kb])
    S.op(eng, lambda h: h.tensor_tensor(out=dv[:, :, :, 1], in0=ta, in1=tb, op=ALU.add),
         reads=[ka, kb], writes=wkeys)


def emit_l0_attn(S, nc, C, es, dr, QT, s_bc, shT0, scbl, scblk, NTK, NTO):
    sb = lambda n, s, d: es.enter_context(nc.sbuf_tensor("t_" + n, s, d))
    NKT = NTK + 2
    KT = sb("KT", [128, 2, NKT * 128], BF16)
    V = sb("V", [128, NKT, 256], BF16)
    C.wbuf = [sb("wbufA%d" % i, [128, NCH, 512], BF16) for i in range(2)]
    C.rowb = [sb("rowbA%d" % i, [1, 512], F32) for i in range(2)]
    C.hn = [sb("hnA", [128, D], BF16)] * 2
    C.junk = C.hn[0]
    C.stat = sb("statA", [128, 32], F32)
    xt = sb("xtA", [128, D], F32)
    hTt = sb("hTt", [128, NCH, 128], BF16)
    rt = [sb("rt%d" % i, [128, 128], F32) for i in range(2)]
    kqsq = sb("kqsq", [128, 512], F32)
    kqf = sb("kqf", [128, 512], F32)
    kqb = sb("kqb", [128, 512], BF16)
    rtmp = [sb("rtmp%d" % i, [128, 256], F32) for i in range(2)]
    kst = sb("kst", [128, 16], F32)
    knb = sb("knb", [128, 128], F32)
    qnb = sb("qnb", [128, 128], F32)
    shTc = sb("shTc", [128, NCH], BF16)
    brow = [sb("browA%d" % i, [1, 512], BF16) for i in range(3)]
    PT = [sb("PT%d" % i, [128, 512], BF16) for i in range(4)]
    rec = sb("rec", [128, 512], F32)
    ones128 = sb("ones128", [128, 128], BF16)
    S.op("pool", lambda h: h.memset(ones128[:], 1.0), writes=["ones128"])
    S.dma("sp", lambda h: h.dma_start(out=knb[:], in_=dr["kn_bc"]), writes=["knb"])
    S.dma("sp", lambda h: h.dma_start(out=qnb[:], in_=dr["qn_bc"]), writes=["qnb"])

    w_in = dr["ev_w_in"].rearrange("(k p) n -> p k n", p=128)
    xall = dr["xall"].rearrange("(t p) d -> t p d", p=128)
    xown = dr["xown"].rearrange("(t p) d -> t p d", p=128)
    rall = dr["rope_all"].rearrange("(t p) d -> t p d", p=128)
    rown = dr["rope_own"].rearrange("(t p) d -> t p d", p=128)

    def set_sbc(scb, scbk, shT_dst, shkey, tag):
        S.dma("sp", lambda h: h.dma_start(out=s_bc[:], in_=dr["norm_g0_bc"]), writes=["s_bc"])

        def consume(blk, rb, rk):
            sec, j = divmod(blk, 4)
            cols = slice(j * 512, (j + 1) * 512)
            if sec == 0:
                _row_to_cols(S, C, rb, rk, shT_dst[:, j * 4:(j + 1) * 4], shkey, j * 4)
            else:
                def post(ps, pk):
                    S.op("dve", lambda h: h.scalar_tensor_tensor(out=s_bc[:, cols], in0=ps[:, :], scalar=1.0,
                                                                in1=s_bc[:, cols], op0=ALU.add, op1=ALU.mult),
                         reads=[pk, "s_bc"], writes=["s_bc"])
                _bcast_rows(S, C, rb, rk, None, None, post)
        _mod_section(S, nc, C, es, dr["ada_w0"], dr["ada_b0"], scb, scbk, 0, 8, consume, tag)

    def bias_row(wb, wk, shT, shk, i):
        ps = C.psF[4 + i % 2]
        pk = ("psF", 4 + i % 2)
        for k in range(NCH):
            S.op("pe", lambda h, k=k: h.matmul(ps[0:1, :], lhsT=shT[:, k:k + 1], rhs=wb[:, k, :],
                                               start=(k == 0), stop=(k == NCH - 1)),
                 reads=[shk, wk], writes=[pk])
        br = brow[i]
        S.op("act", lambda h: h.activation(out=br[0:1, :], in_=ps[0:1, :], func=AF.Copy),
             reads=[pk], writes=[("browA", i)])
        return br, ("browA", i)

    pctr = [0]

    def tile_proj(wb, wk, br, bk):
        i = pctr[0] % 4
        pctr[0] += 1
        ps = C.psF[i]
        pk = ("psF", i)
        for k in range(NCH):
            S.op("pe", lambda h, k=k: h.matmul(ps[:, :], lhsT=hTt[:, k, :], rhs=wb[:, k, :],
                                               start=(k == 0), stop=False),
                 reads=["hTt", wk], writes=[pk])
        S.op("pe", lambda h: h.matmul(ps[:, :], lhsT=C.ones_bf[0:1, :], rhs=br[0:1, :], start=False, stop=True),
             reads=["ones_bf", bk], writes=[pk])
        return ps, pk

    def headnorm(ps, pk, H, gbc, gk):
        W = H * 128
        S.op("act", lambda h: h.activation(out=kqsq[:, 0:W], in_=ps[:, 0:W], func=AF.Square),
             reads=[pk], writes=["kqsq"])
        S.op("dve", lambda h: h.tensor_reduce(out=kst[:, 0:H], in_=kqsq[:, 0:W].rearrange("p (h d) -> p h d", h=H),
                                             axis=AX.X, op=ALU.add), reads=["kqsq"], writes=["kst0"])
        _rstd(S, C, kst[:, 8:8 + H], kst[:, 0:H], 1.0 / 128, EPS, ["kst0"], "kst1", n=H)
        for hh in range(H):
            S.op("dve", lambda h, hh=hh: h.scalar_tensor_tensor(
                out=kqf[:, hh * 128:(hh + 1) * 128], in0=ps[:, hh * 128:(hh + 1) * 128],
                scalar=kst[:, 8 + hh:9 + hh], in1=gbc[:], op0=ALU.mult, op1=ALU.mult),
                reads=[pk, "kst1", gk], writes=["kqf"])

    def kv_tile(t, src_tiles, wkv, wkvk, br, bk, rope_tiles, kpos):
        S.dma("sp", lambda h: h.dma_start(out=xt[:], in_=src_tiles[t]), writes=["xt"])
        if rope_tiles is not None:
            r = rt[t % 2]
            S.dma("sp", lambda h: h.dma_start(out=r[:], in_=rope_tiles[t]), writes=[("rt", t % 2)])
        _norm_transpose(S, C, xt[:], ["xt"], s_bc, ["s_bc"],
                        lambda half: hTt[:, half * 8:(half + 1) * 8, :], ["hTt"], 0)
        ps, pk = tile_proj(wkv, wkvk, br, bk)
        S.op("act", lambda h: h.activation(out=V[:, kpos, :], in_=ps[:, 256:512], func=AF.Copy),
             reads=[pk], writes=[("V", kpos)])
        headnorm(ps, pk, 2, knb, "knb")
        if rope_tiles is not None:
            _rope(S, "pool", kqf[:, 0:256], kqb[:, 0:256], rt[t % 2], 2,
                  [rtmp[0][:, 0:128], rtmp[1][:, 0:128]], ["kqf", ("rt", t % 2)], ["kqb"], "rtmpk")
        else:
            S.op("dve", lambda h: h.tensor_copy(out=kqb[:, 0:256], in_=kqf[:, 0:256]), reads=["kqf"], writes=["kqb"])
        pT = C.psT[t % 2]
        ptk = ("psT", t % 2)
        for hh in range(2):
            S.op("pe", lambda h, hh=hh: h.transpose(out=pT[:, hh * 128:(hh + 1) * 128],
                                                   in_=kqb[:, hh * 128:(hh + 1) * 128], identity=C.ident[:]),
                 reads=["kqb", "ident"], writes=[ptk])
        S.op("act", lambda h: h.activation(out=KT[:, :, kpos * 128:(kpos + 1) * 128],
                                          in_=pT[:, 0:256].rearrange("p (g t) -> p g t", g=2), func=AF.Copy),
             reads=[ptk], writes=[("KT", kpos)])

    scbc, scbck = _silu_cols(S, nc, C, es, dr["cctx_col"], "ctx")
    set_sbc(scbc, scbck, shTc, "shTc", "ctx")
    wkv = C.wbuf[0]
    S.dma("pool", lambda h: h.dma_start(out=wkv[:], in_=w_in[:, :, 0:512]), writes=[("wbuf", 0)])
    brc, brck = bias_row(wkv, ("wbuf", 0), shTc, "shTc", 0)
    for t in range(2):
        kv_tile(NTK + t, xall, wkv, ("wbuf", 0), brc, brck, None, NTK + t)
    set_sbc(scbl, scblk, shT0, "shT0", "lat")
    S.dma("pool", lambda h: h.dma_start(out=wkv[:], in_=w_in[:, :, 0:512]), writes=[("wbuf", 0)])
    brl, brlk = bias_row(wkv, ("wbuf", 0), shT0, "shT0", 1)
    for t in range(NTK):
        kv_tile(t, xall, wkv, ("wbuf", 0), brl, brlk, rall, t)
    wq = [C.wbuf[1], C.wbuf[0]]
    wqk = [("wbuf", 1), ("wbuf", 0)]
    brq = []
    for b in range(2):
        S.dma("pool", lambda h, b=b: h.dma_start(out=wq[b][:], in_=w_in[:, :, 512 + b * 512:1024 + b * 512]),
              writes=[wqk[b]])
        brq.append(bias_row(wq[b], wqk[b], shT0, "shT0", 1 + b))
    for t in range(NTO):
        S.dma("sp", lambda h, t=t: h.dma_start(out=xt[:], in_=xown[t]), writes=["xt"])
        r = rt[t % 2]
        S.dma("sp", lambda h, t=t, r=r: h.dma_start(out=r[:], in_=rown[t]), writes=[("rt", t % 2)])
        _norm_transpose(S, C, xt[:], ["xt"], s_bc, ["s_bc"],
                        lambda half: hTt[:, half * 8:(half + 1) * 8, :], ["hTt"], 0)
        for b in range(2):
            ps, pk = tile_proj(wq[b], wqk[b], brq[b][0], brq[b][1])
            headnorm(ps, pk, 4, qnb, "qnb")
            _rope(S, "dve", kqf[:, 0:512], kqb[:, 0:512], rt[t % 2], 4,
                  [rtmp[0][:, 0:256], rtmp[1][:, 0:256]], ["kqf", ("rt", t % 2)], ["kqb"], "rtmpk")
            pT = C.psT[b]
            ptk = ("psT", b)
            for hh in range(4):
                S.op("pe", lambda h, hh=hh, pT=pT: h.transpose(out=pT[:, hh * 128:(hh + 1) * 128],
                                                              in_=kqb[:, hh * 128:(hh + 1) * 128], identity=C.ident[:]),
                     reads=["kqb", "ident"], writes=[ptk])
            S.op("act", lambda h, b=b, t=t, pT=pT: h.activation(
                out=QT[:, b * 4:(b + 1) * 4, t * 128:(t + 1) * 128],
                in_=pT[:, 0:512].rearrange("p (g t) -> p g t", g=4), func=AF.Copy),
                reads=[ptk], writes=[("QT", b * 4 + hh, t // 4) for hh in range(4)])
    NQB = NTO // 4
    blocks = [(g, hq, qb) for g in range(2) for hq in range(4) for qb in range(NQB)]
    steps = [(bi, kt) for bi in range(len(blocks)) for kt in range(NKT)]
    scale = 128 ** -0.5
    kkeys = [("KT", k) for k in range(NKT)]

    def emit_S(si):
        bi, kt = steps[si]
        g, hq, qb = blocks[bi]
        hd = g * 4 + hq
        ps = C.psF[si % 2]
        S.op("pe", lambda h: h.matmul(ps[:, :], lhsT=KT[:, g, kt * 128:(kt + 1) * 128],
                                      rhs=QT[:, hd, qb * 512:(qb + 1) * 512], start=True, stop=True),
             reads=[("KT", kt), ("QT", hd, qb)], writes=[("psF", si % 2)])
        p = PT[si % 4]
        S.op("act", lambda h: h.activation(out=p[:], in_=ps[:, :], func=AF.Exp, scale=scale),
             reads=[("psF", si % 2)], writes=[("PT", si % 4)])

    def emit_OR(si):
        bi, kt = steps[si]
        g, hq, qb = blocks[bi]
        hd = g * 4 + hq
        pO = C.psF[2 + bi % 2]
        pR = C.psF[4 + bi % 2]
        p = PT[si % 4]
        S.op("pe", lambda h: h.matmul(pO[:, :], lhsT=V[:, kt, g * 128:(g + 1) * 128], rhs=p[:],
                                      start=(kt == 0), stop=(kt == NKT - 1)),
             reads=[("V", kt), ("PT", si % 4)], writes=[("psF", 2 + bi % 2)])
        S.op("pe", lambda h: h.matmul(pR[:, :], lhsT=ones128[:], rhs=p[:],
                                      start=(kt == 0), stop=(kt == NKT - 1)),
             reads=["ones128", ("PT", si % 4)], writes=[("psF", 4 + bi % 2)])
        if kt == NKT - 1:
            S.op("dve", lambda h: h.reciprocal(out=rec[:], in_=pR[:, :]), reads=[("psF", 4 + bi % 2)], writes=["rec"])
            S.op("dve", lambda h: h.tensor_tensor(out=QT[:, hd, qb * 512:(qb + 1) * 512], in0=pO[:, :], in1=rec[:],
                                                 op=ALU.mult),
                 reads=[("psF", 2 + bi % 2), "rec"], writes=[("QT", hd, qb)])

    emit_S(0)
    for si in range(len(steps)):
        if si + 1 < len(steps):
            emit_S(si + 1)
        emit_OR(si)


def emit_l0_mix(S, nc, C, es, dr, QT, s_bc, shT0, scbl, scblk, NTO, x1_ap):
    sb = lambda n, s, d: es.enter_context(nc.sbuf_tensor("t_" + n, s, d))
    G = 4
    GT = 512
    NG = NTO // G
    C.wbuf = [sb("wbufM%d" % i, [128, NCH, 512], BF16) for i in range(2)]
    C.rowb = [sb("rowbM%d" % i, [1, 512], F32) for i in range(2)]
    C.hn = [sb("hnM", [128, D], BF16)] * 2
    C.junk = C.hn[0]
    C.stat = sb("statM", [128, 32], F32)
    xt = sb("xtM", [128, D], F32)
    hTo = sb("hTo", [128, NCH, GT], BF16)
    hTh = sb("hTh", [128, NCH, 128], BF16)
    gate_bc = sb("gate_bcM", [128, D], F32)
    mixc = sb("mixc", [128, 8, GT], BF16)
    cT = sb("cT", [128, 8, GT], F32)
    y = [sb("y%d" % i, [128, GT + 32], F32) for i in range(2)]
    yh = sb("yh", [128, 32], F32)
    t1 = [sb("mt1_%d" % i, [128, 512], F32) for i in range(2)]
    t2 = [sb("mt2_%d" % i, [128, 512], F32) for i in range(2)]
    t3 = [sb("mt3_%d" % i, [128, 512], F32) for i in range(2)]
    mean_sb = sb("mean_sb", [128, 512], F32)
    rstd_sb = sb("rstd_sb", [128, 512], F32)
    bcol = sb("bcol", [128, 32], F32)
    hbcol = sb("hbcol", [128, 32], F32)
    dwT = sb("dwT", [128, 8, 31], F32)
    dwb = sb("dwb", [128, 8], F32)
    lng = sb("lngc", [128, 8], F32)
    lnb = sb("lnbc", [128, 8], F32)
    hlng = sb("hlng", [128, 8], F32)
    hlnb = sb("hlnb", [128, 8], F32)
    mask = sb("maskM", [128, 32], F32)
    onesN = sb("onesN", [128, 128], BF16)
    c16 = [sb("c16_%d" % i, [128, 512], BF16) for i in range(2)]
    s16 = [sb("s16_%d" % i, [128, 512], BF16) for i in range(2)]
    xblk = [sb("xblk%d" % i, [128, 512], F32) for i in range(2)]

    S.op("pool", lambda h: h.memset(onesN[:], 1.0 / 1024), writes=["onesN"])
    S.dma("sp", lambda h: h.dma_start(out=dwT[:], in_=dr["dwT"]), writes=["dwT"])
    S.op("dve", lambda h: h.tensor_scalar(out=dwT[:], in0=dwT[:], scalar1=0.5, scalar2=None, op0=ALU.mult),
         reads=["dwT"], writes=["dwT"])
    S.dma("sp", lambda h: h.dma_start(out=dwb[:], in_=dr["dwb_col"]), writes=["dwb"])
    S.dma("sp", lambda h: h.dma_start(out=lng[:], in_=dr["elng_col"]), writes=["lng"])
    S.dma("sp", lambda h: h.dma_start(out=lnb[:], in_=dr["elnb_col"]), writes=["lnb"])
    S.op("dve", lambda h: h.tensor_scalar(out=hlng[:], in0=lng[:], scalar1=0.5, scalar2=None, op0=ALU.mult),
         reads=["lng"], writes=["hlng"])
    S.op("dve", lambda h: h.tensor_scalar(out=hlnb[:], in0=lnb[:], scalar1=0.5, scalar2=None, op0=ALU.mult),
         reads=["lnb"], writes=["hlnb"])

    def consume(blk, rb, rk):
        _bcast_rows(S, C, rb, rk, gate_bc[:, blk * 512:(blk + 1) * 512], "gate_bc")
    _mod_section(S, nc, C, es, dr["ada_w0"], dr["ada_b0"], scbl, scblk, 2 * D, 4, consume, "gate")

    w_in = dr["ev_w_in"].rearrange("(k p) n -> p k n", p=128)
    w_out = dr["ev_w_out"].rearrange("(k p) n -> p k n", p=128)
    xown = dr["xown"].rearrange("(t p) d -> t p d", p=128)
    xhalo = dr["xhalo"].rearrange("(t p) d -> t p d", p=128)
    x1t = x1_ap.rearrange("(t p) d -> t p d", p=128)

    wctr = [0]
    pctr = [0]
    tc = [0]

    def load_w(src_view, c0):
        i = wctr[0] % 2
        wctr[0] += 1
        wb = C.wbuf[i]
        S.dma("pool", lambda h: h.dma_start(out=wb[:], in_=src_view[:, :, c0:c0 + 512]), writes=[("wbuf", i)])
        return wb, ("wbuf", i)

    def bias_cols(wb, wk, base):
        ps = C.psF[3]
        pk = ("psF", 3)
        for k in range(NCH):
            S.op("pe", lambda h, k=k: h.matmul(ps[0:1, :], lhsT=shT0[:, k:k + 1], rhs=wb[:, k, :],
                                               start=(k == 0), stop=(k == NCH - 1)),
                 reads=[wk, "shT0"], writes=[pk])
        rb = C.rowb[0]
        rk = ("rowb", 0)
        S.op("act", lambda h: h.activation(out=rb[0:1, :], in_=ps[0:1, :], func=AF.Copy), reads=[pk], writes=[rk])
        for j in range(4):
            S.op("pe", lambda h, j=j: h.matmul(ps[:, j:j + 1], lhsT=rb[0:1, j * 128:(j + 1) * 128],
                                               rhs=C.ones_f[0:1, 0:1], start=True, stop=True),
                 reads=[rk, "ones_f"], writes=[pk])
        S.op("act", lambda h: h.activation(out=bcol[:, base:base + 4], in_=ps[:, 0:4], func=AF.Copy),
             reads=[pk], writes=[("bcol", base)])
        S.op("dve", lambda h: h.tensor_scalar(out=hbcol[:, base:base + 4], in0=bcol[:, base:base + 4], scalar1=0.5,
                                             scalar2=None, op0=ALU.mult), reads=[("bcol", base)], writes=[("hbcol", base)])

    def projT(wb, wk, cl, rhs_fn, rkeys, n):
        i = pctr[0] % 3
        pctr[0] += 1
        ps = C.psF[i]
        pk = ("psF", i)
        for k in range(NCH):
            S.op("pe", lambda h, k=k: h.matmul(ps[:, 0:n], lhsT=wb[:, k, cl * 128:(cl + 1) * 128], rhs=rhs_fn(k),
                                               start=(k == 0), stop=(k == NCH - 1)),
                 reads=[wk] + list(rkeys), writes=[pk])
        return ps, pk

    def silu2(ps, pk, n, bidx):
        i = tc[0] % 2
        tc[0] += 1
        a, b_ = t1[i], t2[i]
        ka, kb = ("mt1", i), ("mt2", i)
        bb = (bidx // 4) * 4
        S.op("act", lambda h: h.activation(out=b_[:, 0:n], in_=ps[:, 0:n], func=AF.Identity,
                                          bias=bcol[:, bidx:bidx + 1]), reads=[pk, ("bcol", bb)], writes=[kb])
        S.op("act", lambda h: h.activation(out=a[:, 0:n], in_=b_[:, 0:n], func=AF.Tanh, scale=0.5),
             reads=[kb], writes=[ka])
        S.op("dve", lambda h: h.scalar_tensor_tensor(out=b_[:, 0:n], in0=a[:, 0:n], scalar=1.0, in1=b_[:, 0:n],
                                                    op0=ALU.add, op1=ALU.mult), reads=[ka, kb], writes=[kb])
        return b_, kb

    out_dmas = []
    for gq in range(NG):
        for tl in range(G):
            tg = gq * G + tl
            S.dma("sp", lambda h, tg=tg: h.dma_start(out=xt[:], in_=xown[tg]), writes=["xt"])
            _norm_transpose(S, C, xt[:], ["xt"], s_bc, ["s_bc"],
                            lambda half, tl=tl: hTo[:, half * 8:(half + 1) * 8, tl * 128:(tl + 1) * 128],
                            [("hTo", tl)], 0)
        S.dma("sp", lambda h, gq=gq: h.dma_start(out=xt[:], in_=xhalo[gq]), writes=["xt"])
        _norm_transpose(S, C, xt[:], ["xt"], s_bc, ["s_bc"],
                        lambda half: hTh[:, half * 8:(half + 1) * 8, :], ["hTh"], 0)
        S.dma("sp", lambda h, gq=gq: h.dma_start(out=mask[:], in_=dr["halo_mask"][gq]), writes=["mask"])
        hkeys = [("hTo", t) for t in range(G)]
        if getattr(C, "upto", None) == "hT":
            return out_dmas
        for wbk in range(2):
            wb, wk = load_w(w_in, 1536 + wbk * 512)
            bias_cols(wb, wk, wbk * 4)
            for cl in range(4):
                c = wbk * 4 + cl
                u_ = getattr(C, "upto", None)
                if u_ == "za_w":
                    continue
                ps, pk = projT(wb, wk, cl, lambda k: hTo[:, k, :], hkeys, 512)
                if u_ == "za_p":
                    continue
                sres, sk = silu2(ps, pk, 512, c)
                if u_ in ("za_s", "za_s1", "za_s2"):
                    continue
                S.op("dve", lambda h, c=c, sres=sres, gq=gq: h.scalar_tensor_tensor(
                    out=QT[:, c, gq * 512:(gq + 1) * 512], in0=QT[:, c, gq * 512:(gq + 1) * 512], scalar=0.5,
                    in1=sres[:, :], op0=ALU.mult, op1=ALU.mult),
                    reads=[sk, ("QT", c, gq)], writes=[("QT", c, gq)])
        if getattr(C, "upto", None) in ("za", "za_w", "za_p", "za_s", "za_s1", "za_s2"):
            return out_dmas
        for wbk in range(2):
            wa, wak = load_w(w_in, 2560 + wbk * 512)
            bias_cols(wa, wak, 8 + wbk * 4)
            wbb, wbbk = load_w(w_in, 3584 + wbk * 512)
            bias_cols(wbb, wbbk, 16 + wbk * 4)
            for cl in range(4):
                c = wbk * 4 + cl
                yb = y[c % 2]
                yk = ("y", c % 2)
                for part in range(2):
                    n = 512 if part == 0 else 32
                    rf = (lambda k: hTo[:, k, :]) if part == 0 else (lambda k: hTh[:, k, 0:32])
                    rk_ = hkeys if part == 0 else ["hTh"]
                    psa, pka = projT(wa, wak, cl, rf, rk_, n)
                    psb, pkb = projT(wbb, wbbk, cl, rf, rk_, n)
                    i = tc[0] % 2
                    tc[0] += 1
                    a_, b_ = t1[i], t2[i]
                    ka, kb = ("mt1", i), ("mt2", i)
                    S.op("act", lambda h, a_=a_, psb=psb, n=n, c=c: h.activation(
                        out=a_[:, 0:n], in_=psb[:, 0:n], func=AF.Tanh, scale=0.5, bias=hbcol[:, 16 + c:17 + c]),
                        reads=[pkb, ("hbcol", 16 + wbk * 4)], writes=[ka])
                    S.op("act", lambda h, b_=b_, psa=psa, n=n, c=c: h.activation(
                        out=b_[:, 0:n], in_=psa[:, 0:n], func=AF.Identity, bias=bcol[:, 8 + c:9 + c]),
                        reads=[pka, ("bcol", 8 + wbk * 4)], writes=[kb])
                    if part == 0:
                        S.op("dve", lambda h, a_=a_, b_=b_, yb=yb: h.scalar_tensor_tensor(
                            out=yb[:, 15:527], in0=a_[:, :], scalar=1.0, in1=b_[:, :], op0=ALU.add, op1=ALU.mult),
                            reads=[ka, kb], writes=[yk])
                    else:
                        S.op("dve", lambda h, a_=a_, b_=b_: h.scalar_tensor_tensor(
                            out=yh[:, :], in0=a_[:, 0:32], scalar=1.0, in1=b_[:, 0:32], op0=ALU.add, op1=ALU.mult),
                            reads=[ka, kb], writes=["yh"])
                        S.op("dve", lambda h, yb=yb: h.tensor_tensor(out=yb[:, 0:15], in0=yh[:, 0:15], in1=mask[:, 0:15],
                                                                    op=ALU.mult), reads=["yh", "mask"], writes=[yk])
                        S.op("dve", lambda h, yb=yb: h.tensor_tensor(out=yb[:, 527:542], in0=yh[:, 15:30],
                                                                    in1=mask[:, 15:30], op=ALU.mult),
                             reads=["yh", "mask"], writes=[yk])
                eng = "dve"
                ck = ("cT", c)
                S.op(eng, lambda h, c=c, yb=yb: h.tensor_scalar(out=cT[:, c, :], in0=yb[:, 0:512],
                                                              scalar1=dwT[:, c, 0:1], scalar2=dwb[:, c:c + 1],
                                                              op0=ALU.mult, op1=ALU.add),
                     reads=[yk, "dwT", "dwb"], writes=[ck])
                for j in range(1, 31):
                    S.op(eng, lambda h, c=c, yb=yb, j=j: h.scalar_tensor_tensor(
                        out=cT[:, c, :], in0=yb[:, j:j + 512], scalar=dwT[:, c, j:j + 1], in1=cT[:, c, :],
                        op0=ALU.mult, op1=ALU.add), reads=[yk, "dwT", ck], writes=[ck])
                i = tc[0] % 2
                tc[0] += 1
                cb, sq = c16[i], s16[i]
                S.op("act", lambda h, c=c, cb=cb: h.activation(out=cb[:], in_=cT[:, c, :], func=AF.Copy),
                     reads=[ck], writes=[("c16", i)])
                S.op("act", lambda h, c=c, sq=sq: h.activation(out=sq[:], in_=cT[:, c, :], func=AF.Square),
                     reads=[ck], writes=[("s16", i)])
                S.op("pe", lambda h, c=c, cb=cb: h.matmul(C.psF[4][:, :], lhsT=onesN[:], rhs=cb[:],
                                                          start=(c == 0), stop=(c == 7)),
                     reads=[("c16", i), "onesN"], writes=[("psF", 4)])
                S.op("pe", lambda h, c=c, sq=sq: h.matmul(C.psF[5][:, :], lhsT=onesN[:], rhs=sq[:],
                                                          start=(c == 0), stop=(c == 7)),
                     reads=[("s16", i), "onesN"], writes=[("psF", 5)])
        if getattr(C, "upto", None) == "conv":
            return out_dmas
        S.op("act", lambda h: h.activation(out=mean_sb[:], in_=C.psF[4][:, :], func=AF.Copy),
             reads=[("psF", 4)], writes=["mean_sb"])
        S.op("dve", lambda h: h.tensor_tensor(out=rstd_sb[:], in0=mean_sb[:], in1=mean_sb[:], op=ALU.mult),
             reads=["mean_sb"], writes=["rstd_sb"])
        S.op("dve", lambda h: h.tensor_tensor(out=rstd_sb[:], in0=C.psF[5][:, :], in1=rstd_sb[:], op=ALU.subtract),
             reads=[("psF", 5), "rstd_sb"], writes=["rstd_sb"])
        S.op("dve", lambda h: h.tensor_scalar(out=rstd_sb[:], in0=rstd_sb[:], scalar1=EPS, scalar2=None, op0=ALU.add),
             reads=["rstd_sb"], writes=["rstd_sb"])
        S.op("pool", lambda h: h.tensor_tensor(out=rstd_sb[:], in0=rstd_sb[:], in1=C.mhalf[:, :], op=ALU.pow),
             reads=["rstd_sb", "mhalf"], writes=["rstd_sb"])
        if getattr(C, "upto", None) == "ln":
            return out_dmas
        for wbk in range(2):
            wb, wk = load_w(w_in, 4608 + wbk * 512)
            bias_cols(wb, wk, 24 + wbk * 4)
            for cl in range(4):
                c = wbk * 4 + cl
                ck = ("cT", c)
                S.op("dve", lambda h, c=c: h.tensor_tensor(out=cT[:, c, :], in0=cT[:, c, :], in1=mean_sb[:], op=ALU.subtract),
                     reads=[ck, "mean_sb"], writes=[ck])
                S.op("dve", lambda h, c=c: h.tensor_tensor(out=cT[:, c, :], in0=cT[:, c, :], in1=rstd_sb[:], op=ALU.mult),
                     reads=[ck, "rstd_sb"], writes=[ck])
                i = tc[0] % 2
                tc[0] += 1
                a_ = t3[i]
                ka = ("mt3", i)
                S.op("act", lambda h, c=c, a_=a_: h.activation(out=a_[:], in_=cT[:, c, :], func=AF.Tanh,
                                                             scale=hlng[:, c:c + 1], bias=hlnb[:, c:c + 1]),
                     reads=[ck, "hlng", "hlnb"], writes=[ka])
                S.op("dve", lambda h, c=c: h.tensor_scalar(out=cT[:, c, :], in0=cT[:, c, :], scalar1=lng[:, c:c + 1],
                                                          scalar2=lnb[:, c:c + 1], op0=ALU.mult, op1=ALU.add),
                     reads=[ck, "lng", "lnb"], writes=[ck])
                S.op("dve", lambda h, c=c, a_=a_: h.scalar_tensor_tensor(out=cT[:, c, :], in0=a_[:], scalar=1.0,
                                                                       in1=cT[:, c, :], op0=ALU.add, op1=ALU.mult),
                     reads=[ka, ck], writes=[ck])
                ps, pk = projT(wb, wk, cl, lambda k: hTo[:, k, :], hkeys, 512)
                sres, sk = silu2(ps, pk, 512, 24 + c)
                S.op("dve", lambda h, c=c, sres=sres: h.scalar_tensor_tensor(
                    out=mixc[:, c, :], in0=cT[:, c, :], scalar=0.25, in1=sres[:, :], op0=ALU.mult, op1=ALU.mult),
                    reads=[sk, ck], writes=[("mixc", c)])
        if getattr(C, "upto", None) == "zb":
            return out_dmas
        mkeys = [("mixc", c) for c in range(8)] + [("QT", c, gq) for c in range(8)]
        for ob in range(4):
            wb, wk = load_w(w_out, ob * 512)
            for tl in range(G):
                tg = gq * G + tl
                i = pctr[0] % 3
                pctr[0] += 1
                ps = C.psF[i]
                pk = ("psF", i)
                for k in range(NCH):
                    lhs = QT[:, k, tg * 128:(tg + 1) * 128] if k < 8 else mixc[:, k - 8, tl * 128:(tl + 1) * 128]
                    S.op("pe", lambda h, k=k, lhs=lhs, ps=ps, wb=wb: h.matmul(ps[:, :], lhsT=lhs, rhs=wb[:, k, :],
                                                                             start=(k == 0), stop=(k == NCH - 1)),
                         reads=mkeys + [wk], writes=[pk])
                j = tc[0] % 2
                tc[0] += 1
                xb = xblk[j]
                xk = ("xblk", j)
                tt = t3[j]
                S.dma("sp", lambda h, xb=xb, tg=tg, ob=ob: h.dma_start(out=xb[:], in_=xown[tg][:, ob * 512:(ob + 1) * 512]),
                      writes=[xk])
                S.op("dve", lambda h, tt=tt, ps=ps, ob=ob: h.tensor_tensor(
                    out=tt[:], in0=ps[:, :], in1=gate_bc[:, ob * 512:(ob + 1) * 512], op=ALU.mult),
                    reads=[pk, "gate_bc"], writes=[("mt3", j)])
                S.op("pool", lambda h, tt=tt, xb=xb: h.tensor_tensor(out=xb[:], in0=xb[:], in1=tt[:], op=ALU.add),
                     reads=[("mt3", j), xk], writes=[xk])
                out_dmas.append(S.dma("sp", lambda h, xb=xb, tg=tg, ob=ob: h.dma_start(
                    out=x1t[tg][:, ob * 512:(ob + 1) * 512], in_=xb[:]), reads=[xk]))
    return out_dmas


L0_INPUTS = {
    "xall": None, "xown": None, "rope_all": None, "rope_own": None, "xhalo": None, "halo_mask": None,
    "c_col": (128, NCH), "cctx_col": (128, NCH), "ada_w0": (D, 3 * D), "ada_b0": (1, 3 * D),
    "norm_g0_bc": (128, D), "ev_w_in": (D, 5632), "ev_w_out": (D, D), "kn_bc": (128, 128), "qn_bc": (128, 128),
    "dwT": (128, 8, 31), "dwb_col": (128, 8), "elng_col": (128, 8), "elnb_col": (128, 8),
}
L1_INPUTS = {
    "ada_w1": (D, 3 * D), "ada_b1": (1, 3 * D), "norm_g1_bc": (128, D), "od_w_in": (D, 3 * D), "od_w_out": (D, D),
    "lng_bc": (128, D), "lnb_bc": (128, D), "wsT": (128, 8, 128), "bs_col": (128, 8), "fg_bc": (128, D),
}


def build_full(NTK=64, NTO=16, stop=None, upto=None):
    nc = bass.Bass("TRN2", target_bir_lowering=False)
    dr = {}
    shapes = dict(L0_INPUTS)
    shapes.update(L1_INPUTS)
    shapes["xall"] = ((NTK + 2) * 128, D)
    shapes["xown"] = (NTO * 128, D)
    shapes["rope_all"] = (NTK * 128, 128)
    shapes["rope_own"] = (NTO * 128, 128)
    shapes["xhalo"] = ((NTO // 4) * 128, D)
    shapes["halo_mask"] = (NTO // 4, 128, 32)
    for name, shp in shapes.items():
        dr[name] = nc.dram_tensor(name, list(shp), F32, kind="ExternalInput").ap()
    x1s = nc.dram_tensor("x1s", [NTO * 128, D], F32, kind="Internal").ap()
    out = nc.dram_tensor("out", [NTO * 128, D], F32, kind="ExternalOutput").ap()
    with contextlib.ExitStack() as top:
        S = Sched(nc, top)
        C = Ctx()
        C.upto = upto
        C.psF = [top.enter_context(nc.psum_tensor("psF%d" % i, [128, 512], F32)) for i in range(6)]
        C.psT = [top.enter_context(nc.psum_tensor("psT%d" % i, [128, 1024], BF16)) for i in range(2)]
        _common_consts(nc, S, top, C)
        with contextlib.ExitStack() as es0:
            sb = lambda n, s, d: es0.enter_context(nc.sbuf_tensor("t_" + n, s, d))
            QT = sb("QT", [128, 8, NTO * 128], BF16)
            s_bc = sb("s_bc0", [128, D], F32)
            shT0 = sb("shT0", [128, NCH], BF16)
            scbl, scblk = _silu_cols(S, nc, C, es0, dr["c_col"], "lat")
            with contextlib.ExitStack() as esA:
                emit_l0_attn(S, nc, C, esA, dr, QT, s_bc, shT0, scbl, scblk, NTK, NTO)
                if stop == "attn":
                    dbg = nc.dram_tensor("dbg", [128, 8 * NTO * 128], BF16, kind="ExternalOutput").ap()
                    S.dma("sp", lambda h: h.dma_start(out=dbg, in_=QT[:, :, :].rearrange("p a b -> p (a b)")),
                          reads=[("QT", c, q) for c in range(8) for q in range(NTO // 4)])
                S.emit_block()
                if stop == "attn":
                    return nc
            with contextlib.ExitStack() as esM:
                emit_l0_mix(S, nc, C, esM, dr, QT, s_bc, shT0, scbl, scblk, NTO, x1s if stop != "mix" else out)
                S.emit_block()
                if stop == "mix":
                    return nc
        with contextlib.ExitStack() as es1:
            emit_l1(S, nc, C, es1, NTO, 4, x1s, out, dr)
            S.emit_block()
    return nc


def rope_table(n):
    rows = n // 64
    row = np.repeat(np.arange(rows, dtype=np.float32), 64)
    col = np.tile(np.arange(64, dtype=np.float32), rows)
    inv = np.power(np.float32(10000.0), np.arange(32, dtype=np.float32) * np.float32(-2.0 / 64)).astype(np.float32)
    ang = np.concatenate([row[:, None] * inv, col[:, None] * inv], axis=-1).astype(np.float32)
    return np.concatenate([np.cos(ang), np.sin(ang)], axis=-1).astype(np.float32)


def host_inputs(inp, b, T0, NTO, seq):
    f = lambda a: np.ascontiguousarray(np.asarray(a, dtype=np.float32))
    bc = lambda v, n=128: np.ascontiguousarray(np.broadcast_to(f(v)[None, :], (n, np.asarray(v).shape[0])))
    col = lambda v, k: np.ascontiguousarray(f(v).reshape(k, 128).T)
    x = f(inp["x"][b][:seq])
    T = NTO * 128
    rope = rope_table(seq)
    m = {}
    m["xall"] = np.ascontiguousarray(np.concatenate([x, f(inp["ctx"][b])], axis=0))
    m["xown"] = np.ascontiguousarray(x[T0:T0 + T])
    m["rope_all"] = rope
    m["rope_own"] = np.ascontiguousarray(rope[T0:T0 + T])
    NG = NTO // 4
    xh = np.zeros((NG * 128, D), np.float32)
    hm = np.zeros((NG, 128, 32), np.float32)
    for g in range(NG):
        base = T0 + g * 512
        for j in range(15):
            t = base - 15 + j
            if 0 <= t < seq:
                xh[g * 128 + j] = x[t]
                hm[g, :, j] = 1.0
            t = base + 512 + j
            if 0 <= t < seq:
                xh[g * 128 + 15 + j] = x[t]
                hm[g, :, 15 + j] = 1.0
    m["xhalo"] = xh
    m["halo_mask"] = hm
    m["c_col"] = col(inp["c"][b], NCH)
    m["cctx_col"] = col(inp["c_ctx"], NCH)
    m["ada_w0"] = f(inp["ada_w"][0])
    m["ada_b0"] = f(inp["ada_b"][0])[None, :]
    m["norm_g0_bc"] = bc(inp["norm_g"][0])
    m["ev_w_in"] = f(inp["ev_w_in"][0])
    m["ev_w_out"] = f(inp["ev_w_out"][0])
    m["kn_bc"] = bc(inp["ev_k_norm"][0])
    m["qn_bc"] = bc(inp["ev_q_norm"][0])
    m["dwT"] = np.ascontiguousarray(np.transpose(f(inp["ev_dw_w"][0]).reshape(31, 8, 128), (2, 1, 0)))
    m["dwb_col"] = col(inp["ev_dw_b"][0], 8)
    m["elng_col"] = col(inp["ev_ln_g"][0], 8)
    m["elnb_col"] = col(inp["ev_ln_b"][0], 8)
    m.update(l1_host_inputs(inp["c"][b], inp["ada_w"][1], inp["ada_b"][1], inp["norm_g"][1], inp["od_w_in"][0],
                            inp["od_ln_g"][0], inp["od_ln_b"][0], inp["od_ws"][0], inp["od_bs"][0],
                            inp["od_w_out"][0], inp["final_g"]))
    return m


_NC_CACHE = {}


def kernel(**inputs):
    inp = {k: np.asarray(v) for k, v in inputs.items()}
    B, seq, _ = inp["x"].shape
    ncores = 8
    per = ncores // B
    T = seq // per
    NTO = T // 128
    NTK = seq // 128
    key = (NTK, NTO)
    if key not in _NC_CACHE:
        _NC_CACHE[key] = build_full(NTK, NTO)
    nc = _NC_CACHE[key]
    in_maps = []
    for i in range(ncores):
        b, q = divmod(i, per)
        in_maps.append(host_inputs(inp, b, q * T, NTO, seq))
    res = run_bass_kernel_spmd(nc, in_maps, core_ids=list(range(ncores)))
    out = np.zeros((B, seq, D), np.float32)
    for i in range(ncores):
        b, q = divmod(i, per)
        out[b, q * T:(q + 1) * T] = res.results[i]["out"]
    return out
```

```python
import contextlib
import numpy as np
import concourse.bass as bass
import concourse.mybir as mybir
from concourse.bass_utils import run_bass_kernel_spmd

F32 = mybir.dt.float32
BF16 = mybir.dt.bfloat16
AF = mybir.ActivationFunctionType
ALU = mybir.AluOpType
AX = mybir.AxisListType

ENGS = ("pe", "act", "dve", "pool", "sp")
D = 2048
NCH = 16
SEQ = 8192
CTX = 256
EPS = 1e-6
GC = 0.7978845608028654
GA = 0.044715


class _Op:
    __slots__ = ("eng", "fn", "deps", "stream", "idx", "signal", "is_dma", "semval")

    def __init__(self, eng, fn, deps, stream, is_dma):
        self.eng = eng
        self.fn = fn
        self.deps = deps
        self.stream = stream
        self.idx = None
        self.signal = False
        self.is_dma = is_dma
        self.semval = None


class Sched:
    def __init__(self, nc, es, n_dma_sems=None):
        self.nc = nc
        nd = n_dma_sems or {"sp": 8, "pool": 6, "act": 2}
        self.dma_pool = {q: ["dma_%s_%d" % (q, i) for i in range(n)] for q, n in nd.items()}
        self.dma_rr = {q: 0 for q in nd}
        self.stream_names = list(ENGS) + [s for q in self.dma_pool for s in self.dma_pool[q]]
        self.sems = {s: es.enter_context(nc.semaphore("s_" + s)) for s in self.stream_names}
        self.semcount = {s: 0 for s in self.stream_names}
        self.streams = {s: [] for s in self.stream_names}
        self.pending = {e: [] for e in ENGS}
        self.seen = {e: {} for e in ENGS}
        self.bufs = {}
        self.nops = 0

    def _bs(self, k):
        st = self.bufs.get(k)
        if st is None:
            st = [None, {}]
            self.bufs[k] = st
        return st

    def _collect(self, reads, writes):
        deps = []
        for k in reads:
            st = self._bs(k)
            if st[0] is not None:
                deps.append(st[0])
        for k in writes:
            st = self._bs(k)
            if st[0] is not None:
                deps.append(st[0])
            deps.extend(st[1].values())
        return deps

    def _commit(self, op, reads, writes):
        for k in reads:
            st = self._bs(k)
            cur = st[1].get(op.stream)
            if cur is None or cur.idx < op.idx:
                st[1][op.stream] = op
        for k in writes:
            st = self._bs(k)
            st[0] = op
            st[1] = {}

    def op(self, eng, fn, reads=(), writes=()):
        deps = self._collect(reads, writes)
        o = _Op(eng, fn, deps, eng, False)
        o.idx = len(self.streams[eng])
        self.streams[eng].append(o)
        self.pending[eng].append(o)
        self._commit(o, reads, writes)
        self.nops += 1
        return o

    def dma(self, q, fn, reads=(), writes=()):
        pool = self.dma_pool[q]
        s = pool[self.dma_rr[q] % len(pool)]
        self.dma_rr[q] += 1
        deps = self._collect(reads, writes)
        if self.streams[s]:
            deps.append(self.streams[s][-1])
        o = _Op(q, fn, deps, s, True)
        o.idx = len(self.streams[s])
        self.streams[s].append(o)
        self.pending[q].append(o)
        self._commit(o, reads, writes)
        self.nops += 1
        return o

    def emit_block(self):
        nc = self.nc
        plan = {e: [] for e in ENGS}
        for e in ENGS:
            seen = self.seen[e]
            for o in self.pending[e]:
                need = {}
                for d in o.deps:
                    if d.stream == "pe" and e == "pe":
                        continue
                    if seen.get(d.stream, -1) >= d.idx:
                        continue
                    if d.stream not in need or need[d.stream].idx < d.idx:
                        need[d.stream] = d
                for sname, d in need.items():
                    seen[sname] = d.idx
                    d.signal = True
                plan[e].append((o, list(need.values())))
        lasts = []
        for s in self.stream_names:
            lst = self.streams[s]
            if lst and lst[-1].semval is None:
                lst[-1].signal = True
            if lst:
                lasts.append(lst[-1])
        newops = []
        for e in ENGS:
            newops.extend(self.pending[e])
        for s in self.stream_names:
            for o in self.streams[s]:
                if o.semval is not None:
                    continue
                if o.is_dma:
                    self.semcount[s] += 16
                    o.semval = self.semcount[s]
                elif o.signal:
                    self.semcount[s] += 1
                    o.semval = self.semcount[s]
                else:
                    o.semval = -1
        sems = self.sems
        with nc.Block() as block:
            def run(e):
                def body(h):
                    for o, waits in plan[e]:
                        for d in waits:
                            h.wait_ge(sems[d.stream], d.semval)
                        ins = o.fn(h)
                        if o.is_dma:
                            ins.then_inc(sems[o.stream], 16)
                        elif o.signal:
                            ins.then_inc(sems[o.stream], 1)
                    for d in lasts:
                        if self.seen[e].get(d.stream, -1) < d.idx:
                            h.wait_ge(sems[d.stream], d.semval)
                            self.seen[e][d.stream] = d.idx
                return body

            block.tensor(run("pe"))
            block.scalar(run("act"))
            block.vector(run("dve"))
            block.gpsimd(run("pool"))
            block.sync(run("sp"))
        self.pending = {e: [] for e in ENGS}
        self.bufs = {}


class Ctx:
    pass


def _common_consts(nc, S, es, C):
    sb = lambda n, s, d: es.enter_context(nc.sbuf_tensor("t_" + n, s, d))
    C.ident = sb("ident", [128, 128], BF16)
    C.ones_bf = sb("ones_bf", [1, 128], BF16)
    C.ones_f = sb("ones_f", [1, 128], F32)
    C.mhalf = sb("mhalf", [128, 512], F32)
    S.op("pool", lambda h: h.memset(C.ident[:], 1.0), writes=["ident"])
    S.op("pool", lambda h: h.affine_select(out=C.ident[:], in_=C.ident[:], pattern=[[-1, 128]],
                                          compare_op=ALU.is_equal, fill=0.0, base=0, channel_multiplier=1),
         reads=["ident"], writes=["ident"])
    S.op("pool", lambda h: h.memset(C.ones_bf[:], 1.0), writes=["ones_bf"])
    S.op("pool", lambda h: h.memset(C.ones_f[:], 1.0), writes=["ones_f"])
    S.op("pool", lambda h: h.memset(C.mhalf[:], -0.5), writes=["mhalf"])


def _rstd(S, C, out_ap, in_ap, mult, add, rkeys, wkey, n=1):
    tmpk = ("rstd_tmp", wkey)
    S.op("dve", lambda h: h.tensor_scalar(out=out_ap, in0=in_ap, scalar1=mult, scalar2=add,
                                         op0=ALU.mult, op1=ALU.add), reads=rkeys, writes=[tmpk, wkey])
    S.op("pool", lambda h: h.tensor_tensor(out=out_ap, in0=out_ap, in1=C.mhalf[:, 0:n], op=ALU.pow),
         reads=[tmpk, "mhalf"], writes=[wkey])


def wstream(S, C, blocks, nbuf=2, hold=1):
    n = len(blocks)
    base = getattr(C, "wctr", 0)
    C.wctr = base + n
    issued = [0]

    def issue(j):
        i = (base + j) % nbuf
        view, c0 = blocks[j]
        wb = C.wbuf[i]
        S.dma("pool", lambda h: h.dma_start(out=wb[:], in_=view[:, :, c0:c0 + 512]), writes=[("wbuf", i)])

    for j in range(n):
        while issued[0] < min(n, j + nbuf - hold + 1):
            issue(issued[0])
            issued[0] += 1
        i = (base + j) % nbuf
        yield C.wbuf[i], ("wbuf", i)


def _mod_rows(S, nc, C, es, ada_w_ap, ada_b_ap, ccol_ap, ncols, tag, consume):
    sb = lambda n, s, d: es.enter_context(nc.sbuf_tensor("t_" + n, s, d))
    cc = sb("cc_" + tag, [128, NCH], F32)
    ct = sb("ct_" + tag, [128, NCH], F32)
    scb = sb("scb_" + tag, [128, NCH], BF16)
    S.dma("sp", lambda h: h.dma_start(out=cc[:], in_=ccol_ap), writes=["cc" + tag])
    S.op("act", lambda h: h.activation(out=ct[:], in_=cc[:], func=AF.Tanh, scale=0.5),
         reads=["cc" + tag], writes=["ct" + tag])
    S.op("dve", lambda h: h.scalar_tensor_tensor(out=ct[:], in0=ct[:], scalar=1.0, in1=cc[:],
                                                op0=ALU.add, op1=ALU.mult),
         reads=["ct" + tag, "cc" + tag], writes=["ct" + tag])
    S.op("dve", lambda h: h.tensor_scalar(out=scb[:], in0=ct[:], scalar1=0.5, scalar2=None, op0=ALU.mult),
         reads=["ct" + tag], writes=["scb" + tag])
    wv = ada_w_ap.rearrange("(k p) n -> p k n", p=128)
    for blk in range(ncols // 512):
        wb = C.wbuf[blk % 2]
        wk = ("wbuf", blk % 2)
        S.dma("pool", lambda h, wb=wb, blk=blk: h.dma_start(out=wb[:], in_=wv[:, :, blk * 512:(blk + 1) * 512]),
              writes=[wk])
        rb = C.rowb[blk % 2]
        rk = ("rowb", blk % 2)
        S.dma("sp", lambda h, rb=rb, blk=blk: h.dma_start(out=rb[0:1, :], in_=ada_b_ap[0:1, blk * 512:(blk + 1) * 512]),
              writes=[rk])
        ps = C.psF[blk % 2]
        pk = ("psF", blk % 2)
        for k in range(NCH):
            S.op("pe", lambda h, ps=ps, wb=wb, k=k: h.matmul(ps[0:1, :], lhsT=scb[:, k:k + 1], rhs=wb[:, k, :],
                                                             start=(k == 0), stop=(k == NCH - 1)),
                 reads=["scb" + tag, wk], writes=[pk])
        S.op("dve", lambda h, ps=ps, rb=rb: h.tensor_tensor(out=rb[0:1, :], in0=ps[0:1, :], in1=rb[0:1, :], op=ALU.add),
             reads=[pk, rk], writes=[rk])
        consume(blk, rb, rk)


def _bcast_rows(S, C, rb, rk, dst_ap, dkey, post=None):
    ps = C.psF[2]
    pk = ("psF", 2)
    S.op("pe", lambda h: h.matmul(ps[:, :], lhsT=C.ones_f[0:1, :], rhs=rb[0:1, :], start=True, stop=True),
         reads=[rk, "ones_f"], writes=[pk])
    if post is None:
        S.op("act", lambda h: h.activation(out=dst_ap, in_=ps[:, :], func=AF.Copy), reads=[pk], writes=[dkey])
    else:
        post(ps, pk)


def _row_to_cols(S, C, rb, rk, dst_ap, dkey, base):
    ps = C.psF[3]
    pk = ("psF", 3)
    for j in range(4):
        S.op("pe", lambda h, j=j: h.matmul(ps[:, j:j + 1], lhsT=rb[0:1, j * 128:(j + 1) * 128], rhs=C.ones_f[0:1, 0:1],
                                           start=True, stop=True),
             reads=[rk, "ones_f"], writes=[pk])
    S.op("act", lambda h: h.activation(out=dst_ap, in_=ps[:, 0:4], func=AF.Copy), reads=[pk], writes=[dkey])


def _norm_transpose(S, C, xsrc_ap, xkeys, sbc, sbc_keys, hT_dst_fn, hT_keys, tagi):
    i2 = tagi % 2
    st = C.stat
    sk = ("stat_n", i2)
    junk = C.junk if C.junk is not None else C.hn[i2]
    S.op("act", lambda h: h.activation(out=junk[:], in_=xsrc_ap, func=AF.Square, accum_out=st[:, i2:i2 + 1]),
         reads=list(xkeys), writes=["junk", ("hn", i2), sk])
    _rstd(S, C, st[:, 2 + i2:3 + i2], st[:, i2:i2 + 1], 1.0 / D, EPS, [sk], ("stat_r", i2))
    hn = C.hn[i2]
    hk = ("hn", i2)
    S.op("dve", lambda h: h.scalar_tensor_tensor(out=hn[:], in0=xsrc_ap, scalar=st[:, 2 + i2:3 + i2], in1=sbc[:],
                                                op0=ALU.mult, op1=ALU.mult),
         reads=list(xkeys) + [("stat_r", i2)] + list(sbc_keys), writes=[hk])
    for half in range(2):
        pT = C.psT[half]
        pk = ("psT", half)
        for c in range(8):
            cc = half * 8 + c
            S.op("pe", lambda h, pT=pT, c=c, cc=cc: h.transpose(out=pT[:, c * 128:(c + 1) * 128],
                                                               in_=hn[:, cc * 128:(cc + 1) * 128], identity=C.ident[:]),
                 reads=[hk, "ident"], writes=[pk])
        dst = hT_dst_fn(half)
        src = pT[:, :].rearrange("p (c t) -> p c t", c=8)
        if half == 0:
            S.op("act", lambda h, dst=dst, src=src: h.activation(out=dst, in_=src, func=AF.Copy),
                 reads=[pk], writes=list(hT_keys))
        else:
            S.op("dve", lambda h, dst=dst, src=src: h.tensor_copy(out=dst, in_=src),
                 reads=[pk], writes=list(hT_keys))


def emit_l1(S, nc, C, es, NT, G, x1_ap, out_ap, dr):
    sb = lambda n, s, d: es.enter_context(nc.sbuf_tensor("t_" + n, s, d))
    GT = G * 128
    C.hn = [sb("hn%d" % i, [128, D], BF16) for i in range(2)]
    C.junk = sb("junk", [128, D], BF16)
    C.stat = sb("stat", [128, 32], F32)
    hT = sb("hT", [128, NCH, GT], BF16)
    mixT = sb("mixT", [128, NCH, GT], BF16)
    A = [sb("A%d" % i, [128, D], F32) for i in range(G)]
    C.wbuf = [sb("wbuf%d" % i, [128, NCH, 512], BF16) for i in range(2)]
    C.rowb = [sb("rowb%d" % i, [1, 512], F32) for i in range(2)]
    s_bc = sb("s_bc", [128, D], F32)
    gate_bc = sb("gate_bc", [128, D], F32)
    lng_bc = sb("lng_bc", [128, D], F32)
    lnb_bc = sb("lnb_bc", [128, D], F32)
    fg_bc = sb("fg_bc", [128, D], F32)
    shT = sb("shT", [128, NCH], BF16)
    vhat = sb("vhat", [128, D], BF16)
    tmp1 = [sb("tmp1_%d" % i, [128, 512], F32) for i in range(2)]
    tmp2 = [sb("tmp2_%d" % i, [128, 512], F32) for i in range(2)]
    mixblk = [sb("mixblk%d" % i, [128, 512], BF16) for i in range(2)]
    brow = [sb("brow%d" % i, [1, 512], BF16) for i in range(2)]
    wsT = sb("wsT", [128, 8, 128], BF16)
    bscol = sb("bscol", [128, 8], F32)
    st2 = sb("st2", [128, 16], F32)

    Ak = lambda t: [("A", t, b) for b in range(4)]

    S.dma("sp", lambda h: h.dma_start(out=s_bc[:], in_=dr["norm_g1_bc"]), writes=["s_bc"])
    S.dma("sp", lambda h: h.dma_start(out=lng_bc[:], in_=dr["lng_bc"]), writes=["lng_bc"])
    S.dma("sp", lambda h: h.dma_start(out=lnb_bc[:], in_=dr["lnb_bc"]), writes=["lnb_bc"])
    S.dma("sp", lambda h: h.dma_start(out=fg_bc[:], in_=dr["fg_bc"]), writes=["fg_bc"])
    S.dma("sp", lambda h: h.dma_start(out=bscol[:], in_=dr["bs_col"]), writes=["bscol"])
    S.dma("pool", lambda h: h.dma_start(out=wsT[:], in_=dr["wsT"]), writes=["wsT"])

    def consume(blk, rb, rk):
        sec, j = divmod(blk, 4)
        cols = slice(j * 512, (j + 1) * 512)
        if sec == 0:
            _row_to_cols(S, C, rb, rk, shT[:, j * 4:(j + 1) * 4], "shT", j * 4)
        elif sec == 1:
            def post(ps, pk):
                S.op("dve", lambda h: h.scalar_tensor_tensor(out=s_bc[:, cols], in0=ps[:, :], scalar=1.0,
                                                            in1=s_bc[:, cols], op0=ALU.add, op1=ALU.mult),
                     reads=[pk, "s_bc"], writes=["s_bc"])
            _bcast_rows(S, C, rb, rk, None, None, post)
        else:
            _bcast_rows(S, C, rb, rk, gate_bc[:, cols], "gate_bc")

    _mod_rows(S, nc, C, es, dr["ada_w1"], dr["ada_b1"], dr["c_col"], 3 * D, "l1", consume)

    w_in = dr["od_w_in"].rearrange("(k p) n -> p k n", p=128)
    w_out = dr["od_w_out"].rearrange("(k p) n -> p k n", p=128)
    x1t = x1_ap.rearrange("(t p) d -> t p d", p=128)
    outt = out_ap.rearrange("(t p) d -> t p d", p=128)

    wctr = [0]
    psctr = [0]
    tctr = [0]
    out_dmas = []

    wblocks = []
    for _g in range(NT // G):
        wblocks += [(w_in, D + vb * 512) for vb in range(4)]
        for kb in range(4):
            wblocks += [(w_in, kb * 512), (w_in, 2 * D + kb * 512)]
        wblocks += [(w_out, ob * 512) for ob in range(4)]
    C.wctr = 0
    ws = wstream(S, C, wblocks, 2)

    def load_w(src_view, c0):
        wctr[0] += 1
        return next(ws)

    def bias_row(wb, wk):
        i = wctr[0] % 2
        ps = C.psF[4 + i]
        pk = ("psF", 4 + i)
        for k in range(NCH):
            S.op("pe", lambda h, k=k: h.matmul(ps[0:1, :], lhsT=shT[:, k:k + 1], rhs=wb[:, k, :],
                                               start=(k == 0), stop=(k == NCH - 1)),
                 reads=["shT", wk], writes=[pk])
        br = brow[i]
        S.op("act", lambda h: h.activation(out=br[0:1, :], in_=ps[0:1, :], func=AF.Copy),
             reads=[pk], writes=[("brow", i)])
        return br, ("brow", i)

    def proj_tile(lhs_buf, lhs_keys, tl, wb, wk, br, bk):
        i = psctr[0] % 4
        psctr[0] += 1
        ps = C.psF[i]
        pk = ("psF", i)
        for k in range(NCH):
            S.op("pe", lambda h, k=k: h.matmul(ps[:, :], lhsT=lhs_buf[:, k, tl * 128:(tl + 1) * 128], rhs=wb[:, k, :],
                                               start=(k == 0), stop=(k == NCH - 1 and br is None)),
                 reads=list(lhs_keys) + [wk], writes=[pk])
        if br is not None:
            S.op("pe", lambda h: h.matmul(ps[:, :], lhsT=C.ones_bf[0:1, :], rhs=br[0:1, :], start=False, stop=True),
                 reads=["ones_bf", bk], writes=[pk])
        return ps, pk

    def gelu2(ps, pk, out_fn, outkeys, extra_reads=()):
        i = tctr[0] % 2
        tctr[0] += 1
        t1, t2 = tmp1[i], tmp2[i]
        k1, k2 = ("tmp1", i), ("tmp2", i)
        S.op("act", lambda h: h.activation(out=t1[:], in_=ps[:, :], func=AF.Square, scale=float(np.sqrt(GA))),
             reads=[pk], writes=[k1])
        S.op("dve", lambda h: h.scalar_tensor_tensor(out=t2[:], in0=t1[:], scalar=1.0, in1=ps[:, :],
                                                    op0=ALU.add, op1=ALU.mult), reads=[k1, pk], writes=[k2])
        S.op("act", lambda h: h.activation(out=t1[:], in_=t2[:], func=AF.Tanh, scale=GC),
             reads=[k2], writes=[k1])
        return t1, k1, t2, k2

    for grp in range(NT // G):
        for tl in range(G):
            tg = grp * G + tl
            S.dma("sp", lambda h, tl=tl, tg=tg: h.dma_start(out=A[tl][:], in_=x1t[tg]), writes=Ak(tl))
            _norm_transpose(S, C, A[tl][:], Ak(tl), s_bc, ["s_bc"],
                            lambda half, tl=tl: hT[:, half * 8:(half + 1) * 8, tl * 128:(tl + 1) * 128],
                            [("hT", tl)], tg)
        hT_keys = [("hT", t) for t in range(G)]
        for vb in range(4):
            wb, wk = load_w(w_in, D + vb * 512)
            br, bk = bias_row(wb, wk)
            for tl in range(G):
                ps, pk = proj_tile(hT, [("hT", tl)], tl, wb, wk, br, bk)
                t1, k1, t2, k2 = gelu2(ps, pk, None, None)
                S.op("dve", lambda h, t1=t1, ps=ps, tl=tl, vb=vb: h.scalar_tensor_tensor(
                    out=A[tl][:, vb * 512:(vb + 1) * 512], in0=t1[:], scalar=1.0, in1=ps[:, :],
                    op0=ALU.add, op1=ALU.mult), reads=[k1, pk], writes=[("A", tl, vb)])
        for tl in range(G):
            i2 = tl % 2
            b0 = i2 * 8
            S.op("act", lambda h, tl=tl, b0=b0: h.activation(out=C.junk[:], in_=A[tl][:], func=AF.Identity,
                                                           accum_out=st2[:, b0:b0 + 1]),
                 reads=Ak(tl), writes=["junk", ("st2s", i2)])
            S.op("act", lambda h, tl=tl, b0=b0: h.activation(out=C.junk[:], in_=A[tl][:], func=AF.Square,
                                                           accum_out=st2[:, b0 + 1:b0 + 2]),
                 reads=Ak(tl), writes=["junk", ("st2q", i2)])
            S.op("dve", lambda h, b0=b0: h.tensor_scalar(out=st2[:, b0 + 2:b0 + 3], in0=st2[:, b0:b0 + 1],
                                                        scalar1=1.0 / D, scalar2=None, op0=ALU.mult),
                 reads=[("st2s", i2)], writes=[("st2m", i2)])
            S.op("dve", lambda h, b0=b0: h.tensor_tensor(out=st2[:, b0 + 3:b0 + 4], in0=st2[:, b0 + 2:b0 + 3],
                                                        in1=st2[:, b0 + 2:b0 + 3], op=ALU.mult),
                 reads=[("st2m", i2)], writes=[("st2mm", i2)])
            S.op("dve", lambda h, b0=b0: h.scalar_tensor_tensor(out=st2[:, b0 + 4:b0 + 5], in0=st2[:, b0 + 1:b0 + 2],
                                                               scalar=1.0 / D, in1=st2[:, b0 + 3:b0 + 4],
                                                               op0=ALU.mult, op1=ALU.subtract),
                 reads=[("st2q", i2), ("st2mm", i2)], writes=[("st2v", i2)])
            _rstd(S, C, st2[:, b0 + 5:b0 + 6], st2[:, b0 + 4:b0 + 5], 0.25, EPS, [("st2v", i2)], ("st2r", i2))
            S.op("dve", lambda h, b0=b0: h.tensor_scalar(out=st2[:, b0 + 5:b0 + 6], in0=st2[:, b0 + 5:b0 + 6],
                                                        scalar1=0.5, scalar2=None, op0=ALU.mult),
                 reads=[("st2r", i2)], writes=[("st2r", i2)])
            S.op("dve", lambda h, b0=b0: h.scalar_tensor_tensor(out=st2[:, b0 + 6:b0 + 7], in0=st2[:, b0 + 2:b0 + 3],
                                                               scalar=-1.0, in1=st2[:, b0 + 5:b0 + 6],
                                                               op0=ALU.mult, op1=ALU.mult),
                 reads=[("st2m", i2), ("st2r", i2)], writes=[("st2nb", i2)])
            S.op("act", lambda h, tl=tl, b0=b0: h.activation(out=A[tl][:], in_=A[tl][:], func=AF.Identity,
                                                           scale=st2[:, b0 + 5:b0 + 6], bias=st2[:, b0 + 6:b0 + 7]),
                 reads=Ak(tl) + [("st2r", i2), ("st2nb", i2)], writes=Ak(tl))
            S.op("dve", lambda h, tl=tl: h.tensor_tensor(out=A[tl][:], in0=A[tl][:], in1=lng_bc[:], op=ALU.mult),
                 reads=Ak(tl) + ["lng_bc"], writes=Ak(tl))
            S.op("dve", lambda h, tl=tl: h.tensor_tensor(out=vhat[:], in0=A[tl][:], in1=lnb_bc[:], op=ALU.add),
                 reads=Ak(tl) + ["lnb_bc"], writes=["vhat"])
            for gp in range(4):
                ps = C.psF[4 + gp % 2]
                pk = ("psF", 4 + gp % 2)
                for j in range(2):
                    g = gp * 2 + j
                    S.op("pe", lambda h, ps=ps, g=g, j=j: h.matmul(ps[:, j * 256:(j + 1) * 256], lhsT=wsT[:, g, :],
                                                                  rhs=vhat[:, g * 256:(g + 1) * 256], start=True, stop=True),
                         reads=["wsT", "vhat"], writes=[pk])
                for j in range(2):
                    g = gp * 2 + j
                    S.op("act", lambda h, ps=ps, g=g, j=j, tl=tl: h.activation(
                        out=A[tl][:, g * 256:(g + 1) * 256], in_=ps[:, j * 256:(j + 1) * 256], func=AF.Identity,
                        bias=bscol[:, g:g + 1]), reads=[pk, "bscol"], writes=[("A", tl, gp)])
        for kb in range(4):
            wb, wk = load_w(w_in, kb * 512)
            br, bk = bias_row(wb, wk)
            for tl in range(G):
                ps, pk = proj_tile(hT, [("hT", tl)], tl, wb, wk, br, bk)
                t1, k1, t2, k2 = gelu2(ps, pk, None, None)
                S.op("dve", lambda h, t1=t1, t2=t2, ps=ps: h.scalar_tensor_tensor(
                    out=t2[:], in0=t1[:], scalar=1.0, in1=ps[:, :], op0=ALU.add, op1=ALU.mult),
                    reads=[k1, pk], writes=[k2])
                S.op("pool", lambda h, t2=t2, tl=tl, kb=kb: h.tensor_tensor(
                    out=A[tl][:, kb * 512:(kb + 1) * 512], in0=A[tl][:, kb * 512:(kb + 1) * 512], in1=t2[:], op=ALU.mult),
                    reads=[k2, ("A", tl, kb)], writes=[("A", tl, kb)])
            wb, wk = load_w(w_in, 2 * D + kb * 512)
            br, bk = bias_row(wb, wk)
            for tl in range(G):
                ps, pk = proj_tile(hT, [("hT", tl)], tl, wb, wk, br, bk)
                i = tctr[0] % 2
                tctr[0] += 1
                t1, t2, mb = tmp1[i], tmp2[i], mixblk[i]
                k1, k2, mk = ("tmp1", i), ("tmp2", i), ("mixblk", i)
                S.op("act", lambda h, t1=t1, ps=ps: h.activation(out=t1[:], in_=ps[:, :], func=AF.Tanh, scale=0.5),
                     reads=[pk], writes=[k1])
                S.op("dve", lambda h, t1=t1, t2=t2, ps=ps: h.scalar_tensor_tensor(
                    out=t2[:], in0=t1[:], scalar=1.0, in1=ps[:, :], op0=ALU.add, op1=ALU.mult),
                    reads=[k1, pk], writes=[k2])
                S.op("dve", lambda h, t2=t2, mb=mb, tl=tl, kb=kb: h.scalar_tensor_tensor(
                    out=mb[:], in0=A[tl][:, kb * 512:(kb + 1) * 512], scalar=0.25, in1=t2[:],
                    op0=ALU.mult, op1=ALU.mult), reads=[k2, ("A", tl, kb)], writes=[mk])
                pT = C.psT[i]
                ptk = ("psT", i)
                for j in range(4):
                    S.op("pe", lambda h, pT=pT, mb=mb, j=j: h.transpose(out=pT[:, j * 128:(j + 1) * 128],
                                                                       in_=mb[:, j * 128:(j + 1) * 128], identity=C.ident[:]),
                         reads=[mk, "ident"], writes=[ptk])
                S.op("act", lambda h, pT=pT, tl=tl, kb=kb: h.activation(
                    out=mixT[:, kb * 4:(kb + 1) * 4, tl * 128:(tl + 1) * 128],
                    in_=pT[:, 0:512].rearrange("p (c t) -> p c t", c=4), func=AF.Copy),
                    reads=[ptk], writes=[("mixT", tl, kb)])
        for tl in range(G):
            tg = grp * G + tl
            S.dma("sp", lambda h, tl=tl, tg=tg: h.dma_start(out=A[tl][:], in_=x1t[tg]), writes=Ak(tl))
        for ob in range(4):
            wb, wk = load_w(w_out, ob * 512)
            for tl in range(G):
                ps, pk = proj_tile(mixT, [("mixT", tl, b) for b in range(4)], tl, wb, wk, None, None)
                i = tctr[0] % 2
                tctr[0] += 1
                t2 = tmp2[i]
                k2 = ("tmp2", i)
                S.op("dve", lambda h, t2=t2, ps=ps, ob=ob: h.tensor_tensor(
                    out=t2[:], in0=ps[:, :], in1=gate_bc[:, ob * 512:(ob + 1) * 512], op=ALU.mult),
                    reads=[pk, "gate_bc"], writes=[k2])
                S.op("pool", lambda h, t2=t2, tl=tl, ob=ob: h.tensor_tensor(
                    out=A[tl][:, ob * 512:(ob + 1) * 512], in0=A[tl][:, ob * 512:(ob + 1) * 512], in1=t2[:], op=ALU.add),
                    reads=[k2, ("A", tl, ob)], writes=[("A", tl, ob)])
        for tl in range(G):
            tg = grp * G + tl
            i2 = tl % 2
            S.op("act", lambda h, tl=tl, i2=i2: h.activation(out=C.junk[:], in_=A[tl][:], func=AF.Square,
                                                           accum_out=C.stat[:, 8 + i2:9 + i2]),
                 reads=Ak(tl), writes=["junk", ("fst", i2)])
            _rstd(S, C, C.stat[:, 10 + i2:11 + i2], C.stat[:, 8 + i2:9 + i2], 1.0 / D, EPS, [("fst", i2)], ("fsr", i2))
            S.op("dve", lambda h, tl=tl, i2=i2: h.scalar_tensor_tensor(
                out=A[tl][:], in0=A[tl][:], scalar=C.stat[:, 10 + i2:11 + i2], in1=fg_bc[:],
                op0=ALU.mult, op1=ALU.mult), reads=Ak(tl) + [("fsr", i2), "fg_bc"], writes=Ak(tl))
            out_dmas.append(S.dma("sp", lambda h, tl=tl, tg=tg: h.dma_start(out=outt[tg], in_=A[tl][:]), reads=Ak(tl)))
    return out_dmas


def build_l1_only(NT=16, G=4):
    nc = bass.Bass("TRN2", target_bir_lowering=False)
    T = NT * 128
    dr = {}
    def din(name, shape):
        dr[name] = nc.dram_tensor(name, list(shape), F32, kind="ExternalInput").ap()
    din("x1", (T, D))
    din("c_col", (128, NCH))
    din("ada_w1", (D, 3 * D))
    din("ada_b1", (1, 3 * D))
    din("norm_g1_bc", (128, D))
    din("od_w_in", (D, 3 * D))
    din("od_w_out", (D, D))
    din("lng_bc", (128, D))
    din("lnb_bc", (128, D))
    din("wsT", (128, 8, 128))
    din("bs_col", (128, 8))
    din("fg_bc", (128, D))
    out = nc.dram_tensor("out", [T, D], F32, kind="ExternalOutput").ap()
    with contextlib.ExitStack() as es_top:
        S = Sched(nc, es_top)
        C = Ctx()
        C.psF = [es_top.enter_context(nc.psum_tensor("psF%d" % i, [128, 512], F32)) for i in range(6)]
        C.psT = [es_top.enter_context(nc.psum_tensor("psT%d" % i, [128, 1024], BF16)) for i in range(2)]
        _common_consts(nc, S, es_top, C)
        with contextlib.ExitStack() as es1:
            emit_l1(S, nc, C, es1, NT, G, dr["x1"], out, dr)
            S.emit_block()
    return nc


def l1_host_inputs(c_b, ada_w1, ada_b1, norm_g1, od_w_in, od_ln_g, od_ln_b, od_ws, od_bs, od_w_out, final_g):
    bc = lambda v: np.ascontiguousarray(np.broadcast_to(np.asarray(v, np.float32)[None, :], (128, v.shape[0])))
    return {
        "c_col": np.ascontiguousarray(np.asarray(c_b, np.float32).reshape(NCH, 128).T),
        "ada_w1": np.ascontiguousarray(ada_w1, dtype=np.float32),
        "ada_b1": np.ascontiguousarray(np.asarray(ada_b1, np.float32)[None, :]),
        "norm_g1_bc": bc(norm_g1),
        "od_w_in": np.ascontiguousarray(od_w_in, dtype=np.float32),
        "od_w_out": np.ascontiguousarray(od_w_out, dtype=np.float32),
        "lng_bc": bc(od_ln_g),
        "lnb_bc": bc(od_ln_b),
        "wsT": np.ascontiguousarray(np.transpose(np.asarray(od_ws, np.float32), (2, 0, 1))),
        "bs_col": np.ascontiguousarray(np.asarray(od_bs, np.float32).T),
        "fg_bc": bc(final_g),
    }


def _mod_section(S, nc, C, es, ada_w_ap, ada_b_ap, scb, scbk, c0, nblk, consume, tag):
    wv = ada_w_ap.rearrange("(k p) n -> p k n", p=128)
    for blk in range(nblk):
        cs = c0 + blk * 512
        wb = C.wbuf[blk % 2]
        wk = ("wbuf", blk % 2)
        S.dma("pool", lambda h, wb=wb, cs=cs: h.dma_start(out=wb[:], in_=wv[:, :, cs:cs + 512]), writes=[wk])
        rb = C.rowb[blk % 2]
        rk = ("rowb", blk % 2)
        S.dma("sp", lambda h, rb=rb, cs=cs: h.dma_start(out=rb[0:1, :], in_=ada_b_ap[0:1, cs:cs + 512]), writes=[rk])
        ps = C.psF[blk % 2]
        pk = ("psF", blk % 2)
        for k in range(NCH):
            S.op("pe", lambda h, ps=ps, wb=wb, k=k: h.matmul(ps[0:1, :], lhsT=scb[:, k:k + 1], rhs=wb[:, k, :],
                                                             start=(k == 0), stop=(k == NCH - 1)),
                 reads=[scbk, wk], writes=[pk])
        S.op("dve", lambda h, ps=ps, rb=rb: h.tensor_tensor(out=rb[0:1, :], in0=ps[0:1, :], in1=rb[0:1, :], op=ALU.add),
             reads=[pk, rk], writes=[rk])
        consume(blk, rb, rk)


def _silu_cols(S, nc, C, es, ccol_ap, tag):
    sb = lambda n, s, d: es.enter_context(nc.sbuf_tensor("t_" + n, s, d))
    cc = sb("cc_" + tag, [128, NCH], F32)
    ct = sb("ct_" + tag, [128, NCH], F32)
    scb = sb("scb_" + tag, [128, NCH], BF16)
    S.dma("sp", lambda h: h.dma_start(out=cc[:], in_=ccol_ap), writes=["cc" + tag])
    S.op("act", lambda h: h.activation(out=ct[:], in_=cc[:], func=AF.Tanh, scale=0.5),
         reads=["cc" + tag], writes=["ct" + tag])
    S.op("dve", lambda h: h.scalar_tensor_tensor(out=ct[:], in0=ct[:], scalar=1.0, in1=cc[:],
                                                op0=ALU.add, op1=ALU.mult),
         reads=["ct" + tag, "cc" + tag], writes=["ct" + tag])
    S.op("dve", lambda h: h.tensor_scalar(out=scb[:], in0=ct[:], scalar1=0.5, scalar2=None, op0=ALU.mult),
         reads=["ct" + tag], writes=["scb" + tag])
    return scb, "scb" + tag


def _rope(S, eng, src, dst, rt, H, tmps, rkeys, wkeys, tkey):
    sv = src.rearrange("p (h i two) -> p h i two", h=H, two=2)
    dv = dst.rearrange("p (h i two) -> p h i two", h=H, two=2)
    e, o = sv[:, :, :, 0], sv[:, :, :, 1]
    cosb = rt[:, 0:64].unsqueeze(1).broadcast_to([128, H, 64])
    sinb = rt[:, 64:128].unsqueeze(1).broadcast_to([128, H, 64])
    ta = tmps[0].rearrange("p (h i) -> p h i", h=H)
    tb = tmps[1].rearrange("p (h i) -> p h i", h=H)
    ka, kb = (tkey, 0), (tkey, 1)
    S.op(eng, lambda h: h.tensor_tensor(out=ta, in0=e, in1=cosb, op=ALU.mult), reads=rkeys, writes=[ka])
    S.op(eng, lambda h: h.tensor_tensor(out=tb, in0=o, in1=sinb, op=ALU.mult), reads=rkeys, writes=[kb])
    S.op(eng, lambda h: h.tensor_tensor(out=dv[:, :, :, 0], in0=ta, in1=tb, op=ALU.subtract),
         reads=[ka, kb], writes=wkeys)
    S.op(eng, lambda h: h.tensor_tensor(out=ta, in0=e, in1=sinb, op=ALU.mult), reads=rkeys + wkeys, writes=[ka])
    S.op(eng, lambda h: h.tensor_tensor(out=tb, in0=o, in1=cosb, op=ALU.mult), reads=rkeys, writes=[kb])
    S.op(eng, lambda h: h.tensor_tensor(out=dv[:, :, :, 1], in0=ta, in1=tb, op=ALU.add),
         reads=[ka, kb], writes=wkeys)


def emit_l0_attn(S, nc, C, es, dr, QT, s_bc, shT0, scbl, scblk, NTK, NTO):
    sb = lambda n, s, d: es.enter_context(nc.sbuf_tensor("t_" + n, s, d))
    NKT = NTK + 2
    KT = sb("KT", [128, 2, NKT * 128], BF16)
    V = sb("V", [128, NKT, 256], BF16)
    C.wbuf = [sb("wbufA%d" % i, [128, NCH, 512], BF16) for i in range(2)]
    C.rowb = [sb("rowbA%d" % i, [1, 512], F32) for i in range(2)]
    C.hn = [sb("hnA%d" % i, [128, D], BF16) for i in range(2)]
    C.junk = None
    C.stat = sb("statA", [128, 32], F32)
    xt = [sb("xtA%d" % i, [128, D], F32) for i in range(2)]
    hTt = [sb("hTt%d" % i, [128, NCH, 128], BF16) for i in range(2)]
    rt = [sb("rt%d" % i, [128, 128], F32) for i in range(2)]
    kqsq = [sb("kqsq%d" % i, [128, 512], F32) for i in range(2)]
    kqf = [sb("kqf%d" % i, [128, 512], F32) for i in range(2)]
    kqb = [sb("kqb%d" % i, [128, 512], BF16) for i in range(2)]
    rtmp = [sb("rtmp%d" % i, [128, 256], F32) for i in range(4)]
    kst = sb("kst", [128, 32], F32)
    knb = sb("knb", [128, 128], F32)
    qnb = sb("qnb", [128, 128], F32)
    shTc = sb("shTc", [128, NCH], BF16)
    brow = [sb("browA%d" % i, [1, 512], BF16) for i in range(3)]
    PT = [sb("PT%d" % i, [128, 512], BF16) for i in range(4)]
    rec = sb("rec", [128, 512], F32)
    ones128 = sb("ones128", [128, 128], BF16)
    S.op("pool", lambda h: h.memset(ones128[:], 1.0), writes=["ones128"])
    S.dma("sp", lambda h: h.dma_start(out=knb[:], in_=dr["kn_bc"]), writes=["knb"])
    S.dma("sp", lambda h: h.dma_start(out=qnb[:], in_=dr["qn_bc"]), writes=["qnb"])

    w_in = dr["ev_w_in"].rearrange("(k p) n -> p k n", p=128)
    xall = dr["xall"].rearrange("(t p) d -> t p d", p=128)
    xown = dr["xown"].rearrange("(t p) d -> t p d", p=128)
    rall = dr["rope_all"].rearrange("(t p) d -> t p d", p=128)
    rown = dr["rope_own"].rearrange("(t p) d -> t p d", p=128)
    psK = C.psF[5][:, :].bitcast(BF16)
    psKk = ("psF", 5)

    def set_sbc(scb, scbk, shT_dst, shkey, tag):
        S.dma("sp", lambda h: h.dma_start(out=s_bc[:], in_=dr["norm_g0_bc"]), writes=["s_bc"])

        def consume(blk, rb, rk):
            sec, j = divmod(blk, 4)
            cols = slice(j * 512, (j + 1) * 512)
            if sec == 0:
                _row_to_cols(S, C, rb, rk, shT_dst[:, j * 4:(j + 1) * 4], shkey, j * 4)
            else:
                def post(ps, pk):
                    S.op("dve", lambda h: h.scalar_tensor_tensor(out=s_bc[:, cols], in0=ps[:, :], scalar=1.0,
                                                                in1=s_bc[:, cols], op0=ALU.add, op1=ALU.mult),
                         reads=[pk, "s_bc"], writes=["s_bc"])
                _bcast_rows(S, C, rb, rk, None, None, post)
        _mod_section(S, nc, C, es, dr["ada_w0"], dr["ada_b0"], scb, scbk, 0, 8, consume, tag)

    def bias_row(wb, wk, shT, shk, i):
        ps = C.psF[4]
        pk = ("psF", 4)
        for k in range(NCH):
            S.op("pe", lambda h, k=k: h.matmul(ps[0:1, :], lhsT=shT[:, k:k + 1], rhs=wb[:, k, :],
                                               start=(k == 0), stop=(k == NCH - 1)),
                 reads=[shk, wk], writes=[pk])
        br = brow[i]
        S.op("act", lambda h: h.activation(out=br[0:1, :], in_=ps[0:1, :], func=AF.Copy),
             reads=[pk], writes=[("browA", i)])
        return br, ("browA", i)

    pctr = [0]

    def stage_A(i, src_ap, rope_ap):
        p = i % 2
        S.dma("sp", lambda h: h.dma_start(out=xt[p][:], in_=src_ap), writes=[("xt", p)])
        if rope_ap is not None:
            S.dma("sp", lambda h: h.dma_start(out=rt[p][:], in_=rope_ap), writes=[("rt", p)])
        _norm_transpose(S, C, xt[p][:], [("xt", p)], s_bc, ["s_bc"],
                        lambda half: hTt[p][:, half * 8:(half + 1) * 8, :], [("hTt", p)], i)

    def tile_proj(i, wb, wk, br, bk):
        p = i % 2
        j = pctr[0] % 4
        pctr[0] += 1
        ps = C.psF[j]
        pk = ("psF", j)
        for k in range(NCH):
            S.op("pe", lambda h, k=k: h.matmul(ps[:, :], lhsT=hTt[p][:, k, :], rhs=wb[:, k, :],
                                               start=(k == 0), stop=False),
                 reads=[("hTt", p), wk], writes=[pk])
        S.op("pe", lambda h: h.matmul(ps[:, :], lhsT=C.ones_bf[0:1, :], rhs=br[0:1, :], start=False, stop=True),
             reads=["ones_bf", bk], writes=[pk])
        return ps, pk

    def headnorm(j, ps, pk, H, gbc, gk):
        q = j % 2
        W = H * 128
        c0 = q * 16
        S.op("act", lambda h: h.activation(out=kqsq[q][:, 0:W], in_=ps[:, 0:W], func=AF.Square),
             reads=[pk], writes=[("kqsq", q)])
        S.op("dve", lambda h: h.tensor_reduce(out=kst[:, c0:c0 + H],
                                             in_=kqsq[q][:, 0:W].rearrange("p (h d) -> p h d", h=H),
                                             axis=AX.X, op=ALU.add), reads=[("kqsq", q)], writes=[("kst0", q)])
        _rstd(S, C, kst[:, c0 + 8:c0 + 8 + H], kst[:, c0:c0 + H], 1.0 / 128, EPS, [("kst0", q)], ("kst1", q), n=H)
        for hh in range(H):
            S.op("dve", lambda h, hh=hh: h.scalar_tensor_tensor(
                out=kqf[q][:, hh * 128:(hh + 1) * 128], in0=ps[:, hh * 128:(hh + 1) * 128],
                scalar=kst[:, c0 + 8 + hh:c0 + 9 + hh], in1=gbc[:], op0=ALU.mult, op1=ALU.mult),
                reads=[pk, ("kst1", q), gk], writes=[("kqf", q)])

    def kv_B1(i, wkv, wkvk, br, bk, has_rope, kpos):
        p = i % 2
        ps, pk = tile_proj(i, wkv, wkvk, br, bk)
        S.op("act", lambda h: h.activation(out=V[:, kpos, :], in_=ps[:, 256:512], func=AF.Copy),
             reads=[pk], writes=[("V", kpos)])
        headnorm(i, ps, pk, 2, knb, "knb")
        if has_rope:
            _rope(S, "pool", kqf[p][:, 0:256], kqb[p][:, 0:256], rt[p], 2,
                  [rtmp[2 * p][:, 0:128], rtmp[2 * p + 1][:, 0:128]], [("kqf", p), ("rt", p)], [("kqb", p)],
                  ("rtmpk", p))
        else:
            S.op("dve", lambda h: h.tensor_copy(out=kqb[p][:, 0:256], in_=kqf[p][:, 0:256]),
                 reads=[("kqf", p)], writes=[("kqb", p)])

    def kv_B2(i, kpos):
        p = i % 2
        for hh in range(2):
            S.op("pe", lambda h, hh=hh: h.transpose(out=psK[:, p * 512 + hh * 128:p * 512 + (hh + 1) * 128],
                                                   in_=kqb[p][:, hh * 128:(hh + 1) * 128], identity=C.ident[:]),
                 reads=[("kqb", p), "ident"], writes=[psKk])
        S.op("act", lambda h: h.activation(out=KT[:, :, kpos * 128:(kpos + 1) * 128],
                                          in_=psK[:, p * 512:p * 512 + 256].rearrange("p (g t) -> p g t", g=2),
                                          func=AF.Copy),
             reads=[psKk], writes=[("KT", kpos)])

    def kv_pipeline(tiles, wkv, wkvk, br, bk, has_rope, i0):
        n = len(tiles)
        stage_A(i0, tiles[0][0], tiles[0][1])
        for t in range(n):
            if t + 1 < n:
                stage_A(i0 + t + 1, tiles[t + 1][0], tiles[t + 1][1])
            kv_B1(i0 + t, wkv, wkvk, br, bk, has_rope, tiles[t][2])
            if t >= 1:
                kv_B2(i0 + t - 1, tiles[t - 1][2])
        kv_B2(i0 + n - 1, tiles[n - 1][2])

    scbc, scbck = _silu_cols(S, nc, C, es, dr["cctx_col"], "ctx")
    set_sbc(scbc, scbck, shTc, "shTc", "ctx")
    wkv = C.wbuf[0]
    S.dma("pool", lambda h: h.dma_start(out=wkv[:], in_=w_in[:, :, 0:512]), writes=[("wbuf", 0)])
    brc, brck = bias_row(wkv, ("wbuf", 0), shTc, "shTc", 0)
    kv_pipeline([(xall[NTK + t], None, NTK + t) for t in range(2)], wkv, ("wbuf", 0), brc, brck, False, 0)
    set_sbc(scbl, scblk, shT0, "shT0", "lat")
    S.dma("pool", lambda h: h.dma_start(out=wkv[:], in_=w_in[:, :, 0:512]), writes=[("wbuf", 0)])
    brl, brlk = bias_row(wkv, ("wbuf", 0), shT0, "shT0", 1)
    wq = [C.wbuf[1], C.wbuf[0]]
    wqk = [("wbuf", 1), ("wbuf", 0)]
    S.dma("pool", lambda h: h.dma_start(out=wq[0][:], in_=w_in[:, :, 512:1024]), writes=[wqk[0]])
    kv_pipeline([(xall[t], rall[t], t) for t in range(NTK)], wkv, ("wbuf", 0), brl, brlk, True, 2)
    S.dma("pool", lambda h: h.dma_start(out=wq[1][:], in_=w_in[:, :, 1024:1536]), writes=[wqk[1]])
    brq = [bias_row(wq[0], wqk[0], shT0, "shT0", 0), bias_row(wq[1], wqk[1], shT0, "shT0", 2)]

    def q_B1(i, t, b):
        j = 2 * i + b
        ps, pk = tile_proj(i, wq[b], wqk[b], brq[b][0], brq[b][1])
        headnorm(j, ps, pk, 4, qnb, "qnb")
        q = j % 2
        _rope(S, "dve", kqf[q][:, 0:512], kqb[q][:, 0:512], rt[i % 2], 4,
              [rtmp[2 * q][:, 0:256], rtmp[2 * q + 1][:, 0:256]], [("kqf", q), ("rt", i % 2)], [("kqb", q)],
              ("rtmpk", q))

    def q_B2(i, t, b):
        j = 2 * i + b
        q = j % 2
        for hh in range(4):
            S.op("pe", lambda h, hh=hh: h.transpose(out=psK[:, q * 512 + hh * 128:q * 512 + (hh + 1) * 128],
                                                   in_=kqb[q][:, hh * 128:(hh + 1) * 128], identity=C.ident[:]),
                 reads=[("kqb", q), "ident"], writes=[psKk])
        S.op("act", lambda h: h.activation(
            out=QT[:, b * 4:(b + 1) * 4, t * 128:(t + 1) * 128],
            in_=psK[:, q * 512:(q + 1) * 512].rearrange("p (g t) -> p g t", g=4), func=AF.Copy),
            reads=[psKk], writes=[("QT", b * 4 + hh, t // 4) for hh in range(4)])

    i0 = 2 + NTK
    stage_A(i0, xown[0], rown[0])
    pend = None
    for t in range(NTO):
        if t + 1 < NTO:
            stage_A(i0 + t + 1, xown[t + 1], rown[t + 1])
        for b in range(2):
            q_B1(i0 + t, t, b)
            if pend is not None:
                q_B2(*pend)
            pend = (i0 + t, t, b)
    q_B2(*pend)
    NQB = NTO // 4
    blocks = [(g, hq, qb) for g in range(2) for hq in range(4) for qb in range(NQB)]
    steps = [(bi, kt) for bi in range(len(blocks)) for kt in range(NKT)]
    scale = 128 ** -0.5
    kkeys = [("KT", k) for k in range(NKT)]

    def emit_S(si):
        bi, kt = steps[si]
        g, hq, qb = blocks[bi]
        hd = g * 4 + hq
        ps = C.psF[si % 2]
        S.op("pe", lambda h: h.matmul(ps[:, :], lhsT=KT[:, g, kt * 128:(kt + 1) * 128],
                                      rhs=QT[:, hd, qb * 512:(qb + 1) * 512], start=True, stop=True),
             reads=[("KT", kt), ("QT", hd, qb)], writes=[("psF", si % 2)])
        p = PT[si % 4]
        S.op("act", lambda h: h.activation(out=p[:], in_=ps[:, :], func=AF.Exp, scale=scale),
             reads=[("psF", si % 2)], writes=[("PT", si % 4)])

    def emit_OR(si):
        bi, kt = steps[si]
        g, hq, qb = blocks[bi]
        hd = g * 4 + hq
        pO = C.psF[2 + bi % 2]
        pR = C.psF[4 + bi % 2]
        p = PT[si % 4]
        S.op("pe", lambda h: h.matmul(pO[:, :], lhsT=V[:, kt, g * 128:(g + 1) * 128], rhs=p[:],
                                      start=(kt == 0), stop=(kt == NKT - 1)),
             reads=[("V", kt), ("PT", si % 4)], writes=[("psF", 2 + bi % 2)])
        S.op("pe", lambda h: h.matmul(pR[:, :], lhsT=ones128[:], rhs=p[:],
                                      start=(kt == 0), stop=(kt == NKT - 1)),
             reads=["ones128", ("PT", si % 4)], writes=[("psF", 4 + bi % 2)])
        if kt == NKT - 1:
            S.op("dve", lambda h: h.reciprocal(out=rec[:], in_=pR[:, :]), reads=[("psF", 4 + bi % 2)], writes=["rec"])
            S.op("dve", lambda h: h.tensor_tensor(out=QT[:, hd, qb * 512:(qb + 1) * 512], in0=pO[:, :], in1=rec[:],
                                                 op=ALU.mult),
                 reads=[("psF", 2 + bi % 2), "rec"], writes=[("QT", hd, qb)])

    emit_S(0)
    for si in range(len(steps)):
        if si + 1 < len(steps):
            emit_S(si + 1)
        emit_OR(si)


def emit_l0_mix(S, nc, C, es, dr, QT, s_bc, shT0, scbl, scblk, NTO, x1_ap):
    sb = lambda n, s, d: es.enter_context(nc.sbuf_tensor("t_" + n, s, d))
    G = 4
    GT = 512
    NG = NTO // G
    NWB = 4
    C.wbuf = [sb("wbufM%d" % i, [128, NCH, 512], BF16) for i in range(NWB)]
    C.rowb = [sb("rowbM%d" % i, [1, 512], F32) for i in range(2)]
    C.hn = [sb("hnM", [128, D], BF16)] * 2
    C.junk = C.hn[0]
    C.stat = sb("statM", [128, 32], F32)
    xt = sb("xtM", [128, D], F32)
    hTo = sb("hTo", [128, NCH, GT], BF16)
    hTh = sb("hTh", [128, NCH, 128], BF16)
    gate_bc = sb("gate_bcM", [128, D], F32)
    mixc = sb("mixc", [128, 8, GT], BF16)
    cT = sb("cT", [128, 8, GT], F32)
    y = [sb("y%d" % i, [128, GT + 32], F32) for i in range(2)]
    yh = sb("yh", [128, 32], F32)
    t1 = [sb("mt1_%d" % i, [128, 512], F32) for i in range(2)]
    t2 = [sb("mt2_%d" % i, [128, 512], F32) for i in range(2)]
    t3 = [sb("mt3_%d" % i, [128, 512], F32) for i in range(2)]
    mean_sb = sb("mean_sb", [128, 512], F32)
    rstd_sb = sb("rstd_sb", [128, 512], F32)
    bcol = sb("bcol", [128, 32], F32)
    hbcol = sb("hbcol", [128, 32], F32)
    dwT = sb("dwT", [128, 8, 31], F32)
    dwb = sb("dwb", [128, 8], F32)
    lng = sb("lngc", [128, 8], F32)
    lnb = sb("lnbc", [128, 8], F32)
    hlng = sb("hlng", [128, 8], F32)
    hlnb = sb("hlnb", [128, 8], F32)
    mask = sb("maskM", [128, 32], F32)
    onesN = sb("onesN", [128, 128], BF16)
    c16 = [sb("c16_%d" % i, [128, 512], BF16) for i in range(2)]
    s16 = [sb("s16_%d" % i, [128, 512], BF16) for i in range(2)]
    xblk = [sb("xblk%d" % i, [128, 512], F32) for i in range(2)]

    S.op("pool", lambda h: h.memset(onesN[:], 1.0 / 1024), writes=["onesN"])
    S.dma("sp", lambda h: h.dma_start(out=dwT[:], in_=dr["dwT"]), writes=["dwT"])
    S.op("dve", lambda h: h.tensor_scalar(out=dwT[:], in0=dwT[:], scalar1=0.5, scalar2=None, op0=ALU.mult),
         reads=["dwT"], writes=["dwT"])
    S.dma("sp", lambda h: h.dma_start(out=dwb[:], in_=dr["dwb_col"]), writes=["dwb"])
    S.dma("sp", lambda h: h.dma_start(out=lng[:], in_=dr["elng_col"]), writes=["lng"])
    S.dma("sp", lambda h: h.dma_start(out=lnb[:], in_=dr["elnb_col"]), writes=["lnb"])
    S.op("dve", lambda h: h.tensor_scalar(out=hlng[:], in0=lng[:], scalar1=0.5, scalar2=None, op0=ALU.mult),
         reads=["lng"], writes=["hlng"])
    S.op("dve", lambda h: h.tensor_scalar(out=hlnb[:], in0=lnb[:], scalar1=0.5, scalar2=None, op0=ALU.mult),
         reads=["lnb"], writes=["hlnb"])

    def consume(blk, rb, rk):
        _bcast_rows(S, C, rb, rk, gate_bc[:, blk * 512:(blk + 1) * 512], "gate_bc")
    _mod_section(S, nc, C, es, dr["ada_w0"], dr["ada_b0"], scbl, scblk, 2 * D, 4, consume, "gate")

    w_in = dr["ev_w_in"].rearrange("(k p) n -> p k n", p=128)
    w_out = dr["ev_w_out"].rearrange("(k p) n -> p k n", p=128)
    xown = dr["xown"].rearrange("(t p) d -> t p d", p=128)
    xhalo = dr["xhalo"].rearrange("(t p) d -> t p d", p=128)
    x1t = x1_ap.rearrange("(t p) d -> t p d", p=128)

    wctr = [0]
    pctr = [0]
    tc = [0]

    wblocks = []
    for _g in range(NG):
        wblocks += [(w_in, 1536), (w_in, 2048)]
        wblocks += [(w_in, 2560), (w_in, 3584), (w_in, 3072), (w_in, 4096)]
        wblocks += [(w_in, 4608), (w_in, 5120)]
        wblocks += [(w_out, ob * 512) for ob in range(4)]
    C.wctr = 0
    ws = wstream(S, C, wblocks, NWB, hold=2)

    def load_w(src_view, c0):
        wb, wk = next(ws)
        return wb, wk

    def bias_cols(wb, wk, base):
        ps = C.psF[3]
        pk = ("psF", 3)
        for k in range(NCH):
            S.op("pe", lambda h, k=k: h.matmul(ps[0:1, :], lhsT=shT0[:, k:k + 1], rhs=wb[:, k, :],
                                               start=(k == 0), stop=(k == NCH - 1)),
                 reads=[wk, "shT0"], writes=[pk])
        rb = C.rowb[0]
        rk = ("rowb", 0)
        S.op("act", lambda h: h.activation(out=rb[0:1, :], in_=ps[0:1, :], func=AF.Copy), reads=[pk], writes=[rk])
        for j in range(4):
            S.op("pe", lambda h, j=j: h.matmul(ps[:, j:j + 1], lhsT=rb[0:1, j * 128:(j + 1) * 128],
                                               rhs=C.ones_f[0:1, 0:1], start=True, stop=True),
                 reads=[rk, "ones_f"], writes=[pk])
        S.op("act", lambda h: h.activation(out=bcol[:, base:base + 4], in_=ps[:, 0:4], func=AF.Copy),
             reads=[pk], writes=[("bcol", base)])
        S.op("dve", lambda h: h.tensor_scalar(out=hbcol[:, base:base + 4], in0=bcol[:, base:base + 4], scalar1=0.5,
                                             scalar2=None, op0=ALU.mult), reads=[("bcol", base)], writes=[("hbcol", base)])

    def projT(wb, wk, cl, rhs_fn, rkeys, n):
        i = pctr[0] % 3
        pctr[0] += 1
        ps = C.psF[i]
        pk = ("psF", i)
        for k in range(NCH):
            S.op("pe", lambda h, k=k: h.matmul(ps[:, 0:n], lhsT=wb[:, k, cl * 128:(cl + 1) * 128], rhs=rhs_fn(k),
                                               start=(k == 0), stop=(k == NCH - 1)),
                 reads=[wk] + list(rkeys), writes=[pk])
        return ps, pk

    def silu2(ps, pk, n, bidx):
        i = tc[0] % 2
        tc[0] += 1
        a, b_ = t1[i], t2[i]
        ka, kb = ("mt1", i), ("mt2", i)
        bb = (bidx // 4) * 4
        S.op("act", lambda h: h.activation(out=b_[:, 0:n], in_=ps[:, 0:n], func=AF.Identity,
                                          bias=bcol[:, bidx:bidx + 1]), reads=[pk, ("bcol", bb)], writes=[kb])
        S.op("act", lambda h: h.activation(out=a[:, 0:n], in_=b_[:, 0:n], func=AF.Tanh, scale=0.5),
             reads=[kb], writes=[ka])
        S.op("dve", lambda h: h.scalar_tensor_tensor(out=b_[:, 0:n], in0=a[:, 0:n], scalar=1.0, in1=b_[:, 0:n],
                                                    op0=ALU.add, op1=ALU.mult), reads=[ka, kb], writes=[kb])
        return b_, kb

    out_dmas = []
    for gq in range(NG):
        for tl in range(G):
            tg = gq * G + tl
            S.dma("sp", lambda h, tg=tg: h.dma_start(out=xt[:], in_=xown[tg]), writes=["xt"])
            _norm_transpose(S, C, xt[:], ["xt"], s_bc, ["s_bc"],
                            lambda half, tl=tl: hTo[:, half * 8:(half + 1) * 8, tl * 128:(tl + 1) * 128],
                            [("hTo", tl)], 0)
        S.dma("sp", lambda h, gq=gq: h.dma_start(out=xt[:], in_=xhalo[gq]), writes=["xt"])
        _norm_transpose(S, C, xt[:], ["xt"], s_bc, ["s_bc"],
                        lambda half: hTh[:, half * 8:(half + 1) * 8, :], ["hTh"], 0)
        S.dma("sp", lambda h, gq=gq: h.dma_start(out=mask[:], in_=dr["halo_mask"][gq]), writes=["mask"])
        hkeys = [("hTo", t) for t in range(G)]
        if getattr(C, "upto", None) == "hT":
            return out_dmas
        for wbk in range(2):
            wb, wk = load_w(w_in, 1536 + wbk * 512)
            bias_cols(wb, wk, wbk * 4)
            for cl in range(4):
                c = wbk * 4 + cl
                u_ = getattr(C, "upto", None)
                if u_ == "za_w":
                    continue
                ps, pk = projT(wb, wk, cl, lambda k: hTo[:, k, :], hkeys, 512)
                if u_ == "za_p":
                    continue
                sres, sk = silu2(ps, pk, 512, c)
                if u_ in ("za_s", "za_s1", "za_s2"):
                    continue
                S.op("dve", lambda h, c=c, sres=sres, gq=gq: h.scalar_tensor_tensor(
                    out=QT[:, c, gq * 512:(gq + 1) * 512], in0=QT[:, c, gq * 512:(gq + 1) * 512], scalar=0.5,
                    in1=sres[:, :], op0=ALU.mult, op1=ALU.mult),
                    reads=[sk, ("QT", c, gq)], writes=[("QT", c, gq)])
        if getattr(C, "upto", None) in ("za", "za_w", "za_p", "za_s", "za_s1", "za_s2"):
            return out_dmas
        for wbk in range(2):
            wa, wak = load_w(w_in, 2560 + wbk * 512)
            bias_cols(wa, wak, 8 + wbk * 4)
            wbb, wbbk = load_w(w_in, 3584 + wbk * 512)
            bias_cols(wbb, wbbk, 16 + wbk * 4)
            for cl in range(4):
                c = wbk * 4 + cl
                yb = y[c % 2]
                yk = ("y", c % 2)
                for part in range(2):
                    n = 512 if part == 0 else 32
                    rf = (lambda k: hTo[:, k, :]) if part == 0 else (lambda k: hTh[:, k, 0:32])
                    rk_ = hkeys if part == 0 else ["hTh"]
                    psa, pka = projT(wa, wak, cl, rf, rk_, n)
                    psb, pkb = projT(wbb, wbbk, cl, rf, rk_, n)
                    i = tc[0] % 2
                    tc[0] += 1
                    a_, b_ = t1[i], t2[i]
                    ka, kb = ("mt1", i), ("mt2", i)
                    S.op("act", lambda h, a_=a_, psb=psb, n=n, c=c: h.activation(
                        out=a_[:, 0:n], in_=psb[:, 0:n], func=AF.Tanh, scale=0.5, bias=hbcol[:, 16 + c:17 + c]),
                        reads=[pkb, ("hbcol", 16 + wbk * 4)], writes=[ka])
                    S.op("act", lambda h, b_=b_, psa=psa, n=n, c=c: h.activation(
                        out=b_[:, 0:n], in_=psa[:, 0:n], func=AF.Identity, bias=bcol[:, 8 + c:9 + c]),
                        reads=[pka, ("bcol", 8 + wbk * 4)], writes=[kb])
                    if part == 0:
                        S.op("dve", lambda h, a_=a_, b_=b_, yb=yb: h.scalar_tensor_tensor(
                            out=yb[:, 15:527], in0=a_[:, :], scalar=1.0, in1=b_[:, :], op0=ALU.add, op1=ALU.mult),
                            reads=[ka, kb], writes=[yk])
                    else:
                        S.op("dve", lambda h, a_=a_, b_=b_: h.scalar_tensor_tensor(
                            out=yh[:, :], in0=a_[:, 0:32], scalar=1.0, in1=b_[:, 0:32], op0=ALU.add, op1=ALU.mult),
                            reads=[ka, kb], writes=["yh"])
                        S.op("dve", lambda h, yb=yb: h.tensor_tensor(out=yb[:, 0:15], in0=yh[:, 0:15], in1=mask[:, 0:15],
                                                                    op=ALU.mult), reads=["yh", "mask"], writes=[yk])
                        S.op("dve", lambda h, yb=yb: h.tensor_tensor(out=yb[:, 527:542], in0=yh[:, 15:30],
                                                                    in1=mask[:, 15:30], op=ALU.mult),
                             reads=["yh", "mask"], writes=[yk])
                eng = "dve"
                ck = ("cT", c)
                S.op(eng, lambda h, c=c, yb=yb: h.tensor_scalar(out=cT[:, c, :], in0=yb[:, 0:512],
                                                              scalar1=dwT[:, c, 0:1], scalar2=dwb[:, c:c + 1],
                                                              op0=ALU.mult, op1=ALU.add),
                     reads=[yk, "dwT", "dwb"], writes=[ck])
                for j in range(1, 31):
                    S.op(eng, lambda h, c=c, yb=yb, j=j: h.scalar_tensor_tensor(
                        out=cT[:, c, :], in0=yb[:, j:j + 512], scalar=dwT[:, c, j:j + 1], in1=cT[:, c, :],
                        op0=ALU.mult, op1=ALU.add), reads=[yk, "dwT", ck], writes=[ck])
                i = tc[0] % 2
                tc[0] += 1
                cb, sq = c16[i], s16[i]
                S.op("act", lambda h, c=c, cb=cb: h.activation(out=cb[:], in_=cT[:, c, :], func=AF.Copy),
                     reads=[ck], writes=[("c16", i)])
                S.op("act", lambda h, c=c, sq=sq: h.activation(out=sq[:], in_=cT[:, c, :], func=AF.Square),
                     reads=[ck], writes=[("s16", i)])
                S.op("pe", lambda h, c=c, cb=cb: h.matmul(C.psF[4][:, :], lhsT=onesN[:], rhs=cb[:],
                                                          start=(c == 0), stop=(c == 7)),
                     reads=[("c16", i), "onesN"], writes=[("psF", 4)])
                S.op("pe", lambda h, c=c, sq=sq: h.matmul(C.psF[5][:, :], lhsT=onesN[:], rhs=sq[:],
                                                          start=(c == 0), stop=(c == 7)),
                     reads=[("s16", i), "onesN"], writes=[("psF", 5)])
        if getattr(C, "upto", None) == "conv":
            return out_dmas
        S.op("act", lambda h: h.activation(out=mean_sb[:], in_=C.psF[4][:, :], func=AF.Copy),
             reads=[("psF", 4)], writes=["mean_sb"])
        S.op("dve", lambda h: h.tensor_tensor(out=rstd_sb[:], in0=mean_sb[:], in1=mean_sb[:], op=ALU.mult),
             reads=["mean_sb"], writes=["rstd_sb"])
        S.op("dve", lambda h: h.tensor_tensor(out=rstd_sb[:], in0=C.psF[5][:, :], in1=rstd_sb[:], op=ALU.subtract),
             reads=[("psF", 5), "rstd_sb"], writes=["rstd_sb"])
        S.op("dve", lambda h: h.tensor_scalar(out=rstd_sb[:], in0=rstd_sb[:], scalar1=EPS, scalar2=None, op0=ALU.add),
             reads=["rstd_sb"], writes=["rstd_sb"])
        S.op("pool", lambda h: h.tensor_tensor(out=rstd_sb[:], in0=rstd_sb[:], in1=C.mhalf[:, :], op=ALU.pow),
             reads=["rstd_sb", "mhalf"], writes=["rstd_sb"])
        if getattr(C, "upto", None) == "ln":
            return out_dmas
        for wbk in range(2):
            wb, wk = load_w(w_in, 4608 + wbk * 512)
            bias_cols(wb, wk, 24 + wbk * 4)
            for cl in range(4):
                c = wbk * 4 + cl
                ck = ("cT", c)
                S.op("dve", lambda h, c=c: h.tensor_tensor(out=cT[:, c, :], in0=cT[:, c, :], in1=mean_sb[:], op=ALU.subtract),
                     reads=[ck, "mean_sb"], writes=[ck])
                S.op("dve", lambda h, c=c: h.tensor_tensor(out=cT[:, c, :], in0=cT[:, c, :], in1=rstd_sb[:], op=ALU.mult),
                     reads=[ck, "rstd_sb"], writes=[ck])
                i = tc[0] % 2
                tc[0] += 1
                a_ = t3[i]
                ka = ("mt3", i)
                S.op("act", lambda h, c=c, a_=a_: h.activation(out=a_[:], in_=cT[:, c, :], func=AF.Tanh,
                                                             scale=hlng[:, c:c + 1], bias=hlnb[:, c:c + 1]),
                     reads=[ck, "hlng", "hlnb"], writes=[ka])
                S.op("dve", lambda h, c=c: h.tensor_scalar(out=cT[:, c, :], in0=cT[:, c, :], scalar1=lng[:, c:c + 1],
                                                          scalar2=lnb[:, c:c + 1], op0=ALU.mult, op1=ALU.add),
                     reads=[ck, "lng", "lnb"], writes=[ck])
                S.op("dve", lambda h, c=c, a_=a_: h.scalar_tensor_tensor(out=cT[:, c, :], in0=a_[:], scalar=1.0,
                                                                       in1=cT[:, c, :], op0=ALU.add, op1=ALU.mult),
                     reads=[ka, ck], writes=[ck])
                ps, pk = projT(wb, wk, cl, lambda k: hTo[:, k, :], hkeys, 512)
                sres, sk = silu2(ps, pk, 512, 24 + c)
                S.op("dve", lambda h, c=c, sres=sres: h.scalar_tensor_tensor(
                    out=mixc[:, c, :], in0=cT[:, c, :], scalar=0.25, in1=sres[:, :], op0=ALU.mult, op1=ALU.mult),
                    reads=[sk, ck], writes=[("mixc", c)])
        if getattr(C, "upto", None) == "zb":
            return out_dmas
        mkeys = [("mixc", c) for c in range(8)] + [("QT", c, gq) for c in range(8)]
        for ob in range(4):
            wb, wk = load_w(w_out, ob * 512)
            for tl in range(G):
                tg = gq * G + tl
                i = pctr[0] % 3
                pctr[0] += 1
                ps = C.psF[i]
                pk = ("psF", i)
                for k in range(NCH):
                    lhs = QT[:, k, tg * 128:(tg + 1) * 128] if k < 8 else mixc[:, k - 8, tl * 128:(tl + 1) * 128]
                    S.op("pe", lambda h, k=k, lhs=lhs, ps=ps, wb=wb: h.matmul(ps[:, :], lhsT=lhs, rhs=wb[:, k, :],
                                                                             start=(k == 0), stop=(k == NCH - 1)),
                         reads=mkeys + [wk], writes=[pk])
                j = tc[0] % 2
                tc[0] += 1
                xb = xblk[j]
                xk = ("xblk", j)
                tt = t3[j]
                S.dma("sp", lambda h, xb=xb, tg=tg, ob=ob: h.dma_start(out=xb[:], in_=xown[tg][:, ob * 512:(ob + 1) * 512]),
                      writes=[xk])
                S.op("dve", lambda h, tt=tt, ps=ps, ob=ob: h.tensor_tensor(
                    out=tt[:], in0=ps[:, :], in1=gate_bc[:, ob * 512:(ob + 1) * 512], op=ALU.mult),
                    reads=[pk, "gate_bc"], writes=[("mt3", j)])
                S.op("pool", lambda h, tt=tt, xb=xb: h.tensor_tensor(out=xb[:], in0=xb[:], in1=tt[:], op=ALU.add),
                     reads=[("mt3", j), xk], writes=[xk])
                out_dmas.append(S.dma("sp", lambda h, xb=xb, tg=tg, ob=ob: h.dma_start(
                    out=x1t[tg][:, ob * 512:(ob + 1) * 512], in_=xb[:]), reads=[xk]))
    return out_dmas


L0_INPUTS = {
    "xall": None, "xown": None, "rope_all": None, "rope_own": None, "xhalo": None, "halo_mask": None,
    "c_col": (128, NCH), "cctx_col": (128, NCH), "ada_w0": (D, 3 * D), "ada_b0": (1, 3 * D),
    "norm_g0_bc": (128, D), "ev_w_in": (D, 5632), "ev_w_out": (D, D), "kn_bc": (128, 128), "qn_bc": (128, 128),
    "dwT": (128, 8, 31), "dwb_col": (128, 8), "elng_col": (128, 8), "elnb_col": (128, 8),
}
L1_INPUTS = {
    "ada_w1": (D, 3 * D), "ada_b1": (1, 3 * D), "norm_g1_bc": (128, D), "od_w_in": (D, 3 * D), "od_w_out": (D, D),
    "lng_bc": (128, D), "lnb_bc": (128, D), "wsT": (128, 8, 128), "bs_col": (128, 8), "fg_bc": (128, D),
}


def build_full(NTK=64, NTO=16, stop=None, upto=None):
    nc = bass.Bass("TRN2", target_bir_lowering=False)
    dr = {}
    shapes = dict(L0_INPUTS)
    shapes.update(L1_INPUTS)
    shapes["xall"] = ((NTK + 2) * 128, D)
    shapes["xown"] = (NTO * 128, D)
    shapes["rope_all"] = (NTK * 128, 128)
    shapes["rope_own"] = (NTO * 128, 128)
    shapes["xhalo"] = ((NTO // 4) * 128, D)
    shapes["halo_mask"] = (NTO // 4, 128, 32)
    for name, shp in shapes.items():
        dr[name] = nc.dram_tensor(name, list(shp), F32, kind="ExternalInput").ap()
    x1s = nc.dram_tensor("x1s", [NTO * 128, D], F32, kind="Internal").ap()
    out = nc.dram_tensor("out", [NTO * 128, D], F32, kind="ExternalOutput").ap()
    with contextlib.ExitStack() as top:
        S = Sched(nc, top)
        C = Ctx()
        C.upto = upto
        C.psF = [top.enter_context(nc.psum_tensor("psF%d" % i, [128, 512], F32)) for i in range(6)]
        C.psT = [top.enter_context(nc.psum_tensor("psT%d" % i, [128, 1024], BF16)) for i in range(2)]
        _common_consts(nc, S, top, C)
        with contextlib.ExitStack() as es0:
            sb = lambda n, s, d: es0.enter_context(nc.sbuf_tensor("t_" + n, s, d))
            QT = sb("QT", [128, 8, NTO * 128], BF16)
            s_bc = sb("s_bc0", [128, D], F32)
            shT0 = sb("shT0", [128, NCH], BF16)
            scbl, scblk = _silu_cols(S, nc, C, es0, dr["c_col"], "lat")
            with contextlib.ExitStack() as esA:
                emit_l0_attn(S, nc, C, esA, dr, QT, s_bc, shT0, scbl, scblk, NTK, NTO)
                if stop == "attn":
                    dbg = nc.dram_tensor("dbg", [128, 8 * NTO * 128], BF16, kind="ExternalOutput").ap()
                    S.dma("sp", lambda h: h.dma_start(out=dbg, in_=QT[:, :, :].rearrange("p a b -> p (a b)")),
                          reads=[("QT", c, q) for c in range(8) for q in range(NTO // 4)])
                S.emit_block()
                if stop == "attn":
                    return nc
            with contextlib.ExitStack() as esM:
                emit_l0_mix(S, nc, C, esM, dr, QT, s_bc, shT0, scbl, scblk, NTO, x1s if stop != "mix" else out)
                S.emit_block()
                if stop == "mix":
                    return nc
        with contextlib.ExitStack() as es1:
            emit_l1(S, nc, C, es1, NTO, 4, x1s, out, dr)
            S.emit_block()
    return nc


def rope_table(n):
    rows = n // 64
    row = np.repeat(np.arange(rows, dtype=np.float32), 64)
    col = np.tile(np.arange(64, dtype=np.float32), rows)
    inv = np.power(np.float32(10000.0), np.arange(32, dtype=np.float32) * np.float32(-2.0 / 64)).astype(np.float32)
    ang = np.concatenate([row[:, None] * inv, col[:, None] * inv], axis=-1).astype(np.float32)
    return np.concatenate([np.cos(ang), np.sin(ang)], axis=-1).astype(np.float32)


def host_inputs(inp, b, T0, NTO, seq):
    f = lambda a: np.ascontiguousarray(np.asarray(a, dtype=np.float32))
    bc = lambda v, n=128: np.ascontiguousarray(np.broadcast_to(f(v)[None, :], (n, np.asarray(v).shape[0])))
    col = lambda v, k: np.ascontiguousarray(f(v).reshape(k, 128).T)
    x = f(inp["x"][b][:seq])
    T = NTO * 128
    rope = rope_table(seq)
    m = {}
    m["xall"] = np.ascontiguousarray(np.concatenate([x, f(inp["ctx"][b])], axis=0))
    m["xown"] = np.ascontiguousarray(x[T0:T0 + T])
    m["rope_all"] = rope
    m["rope_own"] = np.ascontiguousarray(rope[T0:T0 + T])
    NG = NTO // 4
    xh = np.zeros((NG * 128, D), np.float32)
    hm = np.zeros((NG, 128, 32), np.float32)
    for g in range(NG):
        base = T0 + g * 512
        for j in range(15):
            t = base - 15 + j
            if 0 <= t < seq:
                xh[g * 128 + j] = x[t]
                hm[g, :, j] = 1.0
            t = base + 512 + j
            if 0 <= t < seq:
                xh[g * 128 + 15 + j] = x[t]
                hm[g, :, 15 + j] = 1.0
    m["xhalo"] = xh
    m["halo_mask"] = hm
    m["c_col"] = col(inp["c"][b], NCH)
    m["cctx_col"] = col(inp["c_ctx"], NCH)
    m["ada_w0"] = f(inp["ada_w"][0])
    m["ada_b0"] = f(inp["ada_b"][0])[None, :]
    m["norm_g0_bc"] = bc(inp["norm_g"][0])
    m["ev_w_in"] = f(inp["ev_w_in"][0])
    m["ev_w_out"] = f(inp["ev_w_out"][0])
    m["kn_bc"] = bc(inp["ev_k_norm"][0])
    m["qn_bc"] = bc(inp["ev_q_norm"][0])
    m["dwT"] = np.ascontiguousarray(np.transpose(f(inp["ev_dw_w"][0]).reshape(31, 8, 128), (2, 1, 0)))
    m["dwb_col"] = col(inp["ev_dw_b"][0], 8)
    m["elng_col"] = col(inp["ev_ln_g"][0], 8)
    m["elnb_col"] = col(inp["ev_ln_b"][0], 8)
    m.update(l1_host_inputs(inp["c"][b], inp["ada_w"][1], inp["ada_b"][1], inp["norm_g"][1], inp["od_w_in"][0],
                            inp["od_ln_g"][0], inp["od_ln_b"][0], inp["od_ws"][0], inp["od_bs"][0],
                            inp["od_w_out"][0], inp["final_g"]))
    return m


_NC_CACHE = {}


def kernel(**inputs):
    inp = {k: np.asarray(v) for k, v in inputs.items()}
    B, seq, _ = inp["x"].shape
    ncores = 8
    per = ncores // B
    T = seq // per
    NTO = T // 128
    NTK = seq // 128
    key = (NTK, NTO)
    if key not in _NC_CACHE:
        _NC_CACHE[key] = build_full(NTK, NTO)
    nc = _NC_CACHE[key]
    in_maps = []
    for i in range(ncores):
        b, q = divmod(i, per)
        in_maps.append(host_inputs(inp, b, q * T, NTO, seq))
    res = run_bass_kernel_spmd(nc, in_maps, core_ids=list(range(ncores)))
    out = np.zeros((B, seq, D), np.float32)
    for i in range(ncores):
        b, q = divmod(i, per)
        out[b, q * T:(q + 1) * T] = res.results[i]["out"]
    return out
```

```python
import contextlib
import numpy as np
import concourse.bass as bass
import concourse.mybir as mybir
from concourse.bass_utils import run_bass_kernel_spmd

F32 = mybir.dt.float32
BF16 = mybir.dt.bfloat16
AF = mybir.ActivationFunctionType
ALU = mybir.AluOpType
AX = mybir.AxisListType

ENGS = ("pe", "act", "dve", "pool", "sp")
D = 2048
NCH = 16
SEQ = 8192
CTX = 256
EPS = 1e-6
GC = 0.7978845608028654
GA = 0.044715


class _Op:
    __slots__ = ("eng", "fn", "deps", "stream", "idx", "signal", "is_dma", "semval")

    def __init__(self, eng, fn, deps, stream, is_dma):
        self.eng = eng
        self.fn = fn
        self.deps = deps
        self.stream = stream
        self.idx = None
        self.signal = False
        self.is_dma = is_dma
        self.semval = None


class Sched:
    def __init__(self, nc, es, n_dma_sems=None):
        self.nc = nc
        nd = n_dma_sems or {"sp": 8, "pool": 6, "act": 2}
        self.dma_pool = {q: ["dma_%s_%d" % (q, i) for i in range(n)] for q, n in nd.items()}
        self.dma_rr = {q: 0 for q in nd}
        self.stream_names = list(ENGS) + [s for q in self.dma_pool for s in self.dma_pool[q]]
        self.sems = {s: es.enter_context(nc.semaphore("s_" + s)) for s in self.stream_names}
        self.semcount = {s: 0 for s in self.stream_names}
        self.streams = {s: [] for s in self.stream_names}
        self.pending = {e: [] for e in ENGS}
        self.seen = {e: {} for e in ENGS}
        self.bufs = {}
        self.nops = 0

    def _bs(self, k):
        st = self.bufs.get(k)
        if st is None:
            st = [None, {}]
            self.bufs[k] = st
        return st

    def _collect(self, reads, writes):
        deps = []
        for k in reads:
            st = self._bs(k)
            if st[0] is not None:
                deps.append(st[0])
        for k in writes:
            st = self._bs(k)
            if st[0] is not None:
                deps.append(st[0])
            deps.extend(st[1].values())
        return deps

    def _commit(self, op, reads, writes):
        for k in reads:
            st = self._bs(k)
            cur = st[1].get(op.stream)
            if cur is None or cur.idx < op.idx:
                st[1][op.stream] = op
        for k in writes:
            st = self._bs(k)
            st[0] = op
            st[1] = {}

    def op(self, eng, fn, reads=(), writes=()):
        deps = self._collect(reads, writes)
        o = _Op(eng, fn, deps, eng, False)
        o.idx = len(self.streams[eng])
        self.streams[eng].append(o)
        self.pending[eng].append(o)
        self._commit(o, reads, writes)
        self.nops += 1
        return o

    def dma(self, q, fn, reads=(), writes=()):
        pool = self.dma_pool[q]
        s = pool[self.dma_rr[q] % len(pool)]
        self.dma_rr[q] += 1
        deps = self._collect(reads, writes)
        if self.streams[s]:
            deps.append(self.streams[s][-1])
        o = _Op(q, fn, deps, s, True)
        o.idx = len(self.streams[s])
        self.streams[s].append(o)
        self.pending[q].append(o)
        self._commit(o, reads, writes)
        self.nops += 1
        return o

    def emit_block(self):
        nc = self.nc
        plan = {e: [] for e in ENGS}
        for e in ENGS:
            seen = self.seen[e]
            for o in self.pending[e]:
                need = {}
                for d in o.deps:
                    if d.stream == "pe" and e == "pe":
                        continue
                    if seen.get(d.stream, -1) >= d.idx:
                        continue
                    if d.stream not in need or need[d.stream].idx < d.idx:
                        need[d.stream] = d
                for sname, d in need.items():
                    seen[sname] = d.idx
                    d.signal = True
                plan[e].append((o, list(need.values())))
        lasts = []
        for s in self.stream_names:
            lst = self.streams[s]
            if lst and lst[-1].semval is None:
                lst[-1].signal = True
            if lst:
                lasts.append(lst[-1])
        newops = []
        for e in ENGS:
            newops.extend(self.pending[e])
        for s in self.stream_names:
            for o in self.streams[s]:
                if o.semval is not None:
                    continue
                if o.is_dma:
                    self.semcount[s] += 16
                    o.semval = self.semcount[s]
                elif o.signal:
                    self.semcount[s] += 1
                    o.semval = self.semcount[s]
                else:
                    o.semval = -1
        sems = self.sems
        with nc.Block() as block:
            def run(e):
                def body(h):
                    for o, waits in plan[e]:
                        for d in waits:
                            h.wait_ge(sems[d.stream], d.semval)
                        ins = o.fn(h)
                        if o.is_dma:
                            ins.then_inc(sems[o.stream], 16)
                        elif o.signal:
                            ins.then_inc(sems[o.stream], 1)
                    for d in lasts:
                        if self.seen[e].get(d.stream, -1) < d.idx:
                            h.wait_ge(sems[d.stream], d.semval)
                            self.seen[e][d.stream] = d.idx
                return body

            block.tensor(run("pe"))
            block.scalar(run("act"))
            block.vector(run("dve"))
            block.gpsimd(run("pool"))
            block.sync(run("sp"))
        self.pending = {e: [] for e in ENGS}
        self.bufs = {}


class Ctx:
    pass


def _common_consts(nc, S, es, C):
    sb = lambda n, s, d: es.enter_context(nc.sbuf_tensor("t_" + n, s, d))
    C.ident = sb("ident", [128, 128], BF16)
    C.ones_bf = sb("ones_bf", [1, 128], BF16)
    C.ones_f = sb("ones_f", [1, 128], F32)
    C.mhalf = sb("mhalf", [128, 512], F32)
    S.op("pool", lambda h: h.memset(C.ident[:], 1.0), writes=["ident"])
    S.op("pool", lambda h: h.affine_select(out=C.ident[:], in_=C.ident[:], pattern=[[-1, 128]],
                                          compare_op=ALU.is_equal, fill=0.0, base=0, channel_multiplier=1),
         reads=["ident"], writes=["ident"])
    S.op("pool", lambda h: h.memset(C.ones_bf[:], 1.0), writes=["ones_bf"])
    S.op("pool", lambda h: h.memset(C.ones_f[:], 1.0), writes=["ones_f"])
    S.op("pool", lambda h: h.memset(C.mhalf[:], -0.5), writes=["mhalf"])


def _rstd(S, C, out_ap, in_ap, mult, add, rkeys, wkey, n=1):
    tmpk = ("rstd_tmp", wkey)
    S.op("dve", lambda h: h.tensor_scalar(out=out_ap, in0=in_ap, scalar1=mult, scalar2=add,
                                         op0=ALU.mult, op1=ALU.add), reads=rkeys, writes=[tmpk, wkey])
    S.op("pool", lambda h: h.tensor_tensor(out=out_ap, in0=out_ap, in1=C.mhalf[:, 0:n], op=ALU.pow),
         reads=[tmpk, "mhalf"], writes=[wkey])


def wstream(S, C, blocks, nbuf=2, hold=1):
    n = len(blocks)
    base = getattr(C, "wctr", 0)
    C.wctr = base + n
    issued = [0]

    def issue(j):
        i = (base + j) % nbuf
        view, c0 = blocks[j]
        wb = C.wbuf[i]
        S.dma("pool", lambda h: h.dma_start(out=wb[:], in_=view[:, :, c0:c0 + 512]), writes=[("wbuf", i)])

    for j in range(n):
        while issued[0] < min(n, j + nbuf - hold + 1):
            issue(issued[0])
            issued[0] += 1
        i = (base + j) % nbuf
        yield C.wbuf[i], ("wbuf", i)


def _mod_rows(S, nc, C, es, ada_w_ap, ada_b_ap, ccol_ap, ncols, tag, consume):
    sb = lambda n, s, d: es.enter_context(nc.sbuf_tensor("t_" + n, s, d))
    cc = sb("cc_" + tag, [128, NCH], F32)
    ct = sb("ct_" + tag, [128, NCH], F32)
    scb = sb("scb_" + tag, [128, NCH], BF16)
    S.dma("sp", lambda h: h.dma_start(out=cc[:], in_=ccol_ap), writes=["cc" + tag])
    S.op("act", lambda h: h.activation(out=ct[:], in_=cc[:], func=AF.Tanh, scale=0.5),
         reads=["cc" + tag], writes=["ct" + tag])
    S.op("dve", lambda h: h.scalar_tensor_tensor(out=ct[:], in0=ct[:], scalar=1.0, in1=cc[:],
                                                op0=ALU.add, op1=ALU.mult),
         reads=["ct" + tag, "cc" + tag], writes=["ct" + tag])
    S.op("dve", lambda h: h.tensor_scalar(out=scb[:], in0=ct[:], scalar1=0.5, scalar2=None, op0=ALU.mult),
         reads=["ct" + tag], writes=["scb" + tag])
    wv = ada_w_ap.rearrange("(k p) n -> p k n", p=128)
    for blk in range(ncols // 512):
        wb = C.wbuf[blk % 2]
        wk = ("wbuf", blk % 2)
        S.dma("pool", lambda h, wb=wb, blk=blk: h.dma_start(out=wb[:], in_=wv[:, :, blk * 512:(blk + 1) * 512]),
              writes=[wk])
        rb = C.rowb[blk % 2]
        rk = ("rowb", blk % 2)
        S.dma("sp", lambda h, rb=rb, blk=blk: h.dma_start(out=rb[0:1, :], in_=ada_b_ap[0:1, blk * 512:(blk + 1) * 512]),
              writes=[rk])
        ps = C.psF[blk % 2]
        pk = ("psF", blk % 2)
        for k in range(NCH):
            S.op("pe", lambda h, ps=ps, wb=wb, k=k: h.matmul(ps[0:1, :], lhsT=scb[:, k:k + 1], rhs=wb[:, k, :],
                                                             start=(k == 0), stop=(k == NCH - 1)),
                 reads=["scb" + tag, wk], writes=[pk])
        S.op("dve", lambda h, ps=ps, rb=rb: h.tensor_tensor(out=rb[0:1, :], in0=ps[0:1, :], in1=rb[0:1, :], op=ALU.add),
             reads=[pk, rk], writes=[rk])
        consume(blk, rb, rk)


def _bcast_rows(S, C, rb, rk, dst_ap, dkey, post=None):
    ps = C.psF[2]
    pk = ("psF", 2)
    S.op("pe", lambda h: h.matmul(ps[:, :], lhsT=C.ones_f[0:1, :], rhs=rb[0:1, :], start=True, stop=True),
         reads=[rk, "ones_f"], writes=[pk])
    if post is None:
        S.op("act", lambda h: h.activation(out=dst_ap, in_=ps[:, :], func=AF.Copy), reads=[pk], writes=[dkey])
    else:
        post(ps, pk)


def _row_to_cols(S, C, rb, rk, dst_ap, dkey, base):
    ps = C.psF[3]
    pk = ("psF", 3)
    for j in range(4):
        S.op("pe", lambda h, j=j: h.matmul(ps[:, j:j + 1], lhsT=rb[0:1, j * 128:(j + 1) * 128], rhs=C.ones_f[0:1, 0:1],
                                           start=True, stop=True),
             reads=[rk, "ones_f"], writes=[pk])
    S.op("act", lambda h: h.activation(out=dst_ap, in_=ps[:, 0:4], func=AF.Copy), reads=[pk], writes=[dkey])


def _norm_scale(S, C, xsrc_ap, xkeys, sbc, sbc_keys, tagi):
    i2 = tagi % 2
    st = C.stat
    sk = ("stat_n", i2)
    junk = C.junk if C.junk is not None else C.hn[i2]
    S.op("act", lambda h: h.activation(out=junk[:], in_=xsrc_ap, func=AF.Square, accum_out=st[:, i2:i2 + 1]),
         reads=list(xkeys), writes=["junk", ("hn", i2), sk])
    _rstd(S, C, st[:, 2 + i2:3 + i2], st[:, i2:i2 + 1], 1.0 / D, EPS, [sk], ("stat_r", i2))
    hn = C.hn[i2]
    hk = ("hn", i2)
    S.op("dve", lambda h: h.scalar_tensor_tensor(out=hn[:], in0=xsrc_ap, scalar=st[:, 2 + i2:3 + i2], in1=sbc[:],
                                                op0=ALU.mult, op1=ALU.mult),
         reads=list(xkeys) + [("stat_r", i2)] + list(sbc_keys), writes=[hk])


def _transposes(S, C, hT_dst_fn, hT_keys, tagi):
    i2 = tagi % 2
    hn = C.hn[i2]
    hk = ("hn", i2)
    for half in range(2):
        pT = C.psT[half]
        pk = ("psT", half)
        for c in range(8):
            cc = half * 8 + c
            S.op("pe", lambda h, pT=pT, c=c, cc=cc: h.transpose(out=pT[:, c * 128:(c + 1) * 128],
                                                               in_=hn[:, cc * 128:(cc + 1) * 128], identity=C.ident[:]),
                 reads=[hk, "ident"], writes=[pk])
        dst = hT_dst_fn(half)
        src = pT[:, :].rearrange("p (c t) -> p c t", c=8)
        if half == 0:
            S.op("act", lambda h, dst=dst, src=src: h.activation(out=dst, in_=src, func=AF.Copy),
                 reads=[pk], writes=list(hT_keys))
        else:
            S.op("dve", lambda h, dst=dst, src=src: h.tensor_copy(out=dst, in_=src),
                 reads=[pk], writes=list(hT_keys))


def _norm_transpose(S, C, xsrc_ap, xkeys, sbc, sbc_keys, hT_dst_fn, hT_keys, tagi):
    _norm_scale(S, C, xsrc_ap, xkeys, sbc, sbc_keys, tagi)
    _transposes(S, C, hT_dst_fn, hT_keys, tagi)


def emit_l1(S, nc, C, es, NT, G, x1_ap, out_ap, dr):
    sb = lambda n, s, d: es.enter_context(nc.sbuf_tensor("t_" + n, s, d))
    GT = G * 128
    C.hn = [sb("hn%d" % i, [128, D], BF16) for i in range(2)]
    C.junk = sb("junk", [128, D], BF16)
    C.stat = sb("stat", [128, 32], F32)
    hT = sb("hT", [128, NCH, GT], BF16)
    mixT = sb("mixT", [128, NCH, GT], BF16)
    A = [sb("A%d" % i, [128, D], F32) for i in range(G)]
    C.wbuf = [sb("wbuf%d" % i, [128, NCH, 512], BF16) for i in range(2)]
    C.rowb = [sb("rowb%d" % i, [1, 512], F32) for i in range(2)]
    s_bc = sb("s_bc", [128, D], F32)
    gate_bc = sb("gate_bc", [128, D], F32)
    lng_bc = sb("lng_bc", [128, D], F32)
    lnb_bc = sb("lnb_bc", [128, D], F32)
    fg_bc = sb("fg_bc", [128, D], F32)
    shT = sb("shT", [128, NCH], BF16)
    vhat = sb("vhat", [128, D], BF16)
    tmp1 = [sb("tmp1_%d" % i, [128, 512], F32) for i in range(2)]
    tmp2 = [sb("tmp2_%d" % i, [128, 512], F32) for i in range(2)]
    mixblk = [sb("mixblk%d" % i, [128, 512], BF16) for i in range(2)]
    brow = [sb("brow%d" % i, [1, 512], BF16) for i in range(2)]
    wsT = sb("wsT", [128, 8, 128], BF16)
    bscol = sb("bscol", [128, 8], F32)
    st2 = sb("st2", [128, 16], F32)

    Ak = lambda t: [("A", t, b) for b in range(4)]

    S.dma("sp", lambda h: h.dma_start(out=s_bc[:], in_=dr["norm_g1_bc"]), writes=["s_bc"])
    S.dma("sp", lambda h: h.dma_start(out=lng_bc[:], in_=dr["lng_bc"]), writes=["lng_bc"])
    S.dma("sp", lambda h: h.dma_start(out=lnb_bc[:], in_=dr["lnb_bc"]), writes=["lnb_bc"])
    S.dma("sp", lambda h: h.dma_start(out=fg_bc[:], in_=dr["fg_bc"]), writes=["fg_bc"])
    S.dma("sp", lambda h: h.dma_start(out=bscol[:], in_=dr["bs_col"]), writes=["bscol"])
    S.dma("pool", lambda h: h.dma_start(out=wsT[:], in_=dr["wsT"]), writes=["wsT"])

    def consume(blk, rb, rk):
        sec, j = divmod(blk, 4)
        cols = slice(j * 512, (j + 1) * 512)
        if sec == 0:
            _row_to_cols(S, C, rb, rk, shT[:, j * 4:(j + 1) * 4], "shT", j * 4)
        elif sec == 1:
            def post(ps, pk):
                S.op("dve", lambda h: h.scalar_tensor_tensor(out=s_bc[:, cols], in0=ps[:, :], scalar=1.0,
                                                            in1=s_bc[:, cols], op0=ALU.add, op1=ALU.mult),
                     reads=[pk, "s_bc"], writes=["s_bc"])
            _bcast_rows(S, C, rb, rk, None, None, post)
        else:
            _bcast_rows(S, C, rb, rk, gate_bc[:, cols], "gate_bc")

    _mod_rows(S, nc, C, es, dr["ada_w1"], dr["ada_b1"], dr["c_col"], 3 * D, "l1", consume)

    w_in = dr["od_w_in"].rearrange("(k p) n -> p k n", p=128)
    w_out = dr["od_w_out"].rearrange("(k p) n -> p k n", p=128)
    x1t = x1_ap.rearrange("(t p) d -> t p d", p=128)
    outt = out_ap.rearrange("(t p) d -> t p d", p=128)

    wctr = [0]
    psctr = [0]
    tctr = [0]
    out_dmas = []

    wblocks = []
    for _g in range(NT // G):
        wblocks += [(w_in, D + vb * 512) for vb in range(4)]
        for kb in range(4):
            wblocks += [(w_in, kb * 512), (w_in, 2 * D + kb * 512)]
        wblocks += [(w_out, ob * 512) for ob in range(4)]
    C.wctr = 0
    ws = wstream(S, C, wblocks, 2)

    def load_w(src_view, c0):
        wctr[0] += 1
        return next(ws)

    def bias_row(wb, wk):
        i = wctr[0] % 2
        ps = C.psF[4 + i]
        pk = ("psF", 4 + i)
        for k in range(NCH):
            S.op("pe", lambda h, k=k: h.matmul(ps[0:1, :], lhsT=shT[:, k:k + 1], rhs=wb[:, k, :],
                                               start=(k == 0), stop=(k == NCH - 1)),
                 reads=["shT", wk], writes=[pk])
        br = brow[i]
        S.op("act", lambda h: h.activation(out=br[0:1, :], in_=ps[0:1, :], func=AF.Copy),
             reads=[pk], writes=[("brow", i)])
        return br, ("brow", i)

    def proj_tile(lhs_buf, lhs_keys, tl, wb, wk, br, bk):
        i = psctr[0] % 4
        psctr[0] += 1
        ps = C.psF[i]
        pk = ("psF", i)
        for k in range(NCH):
            S.op("pe", lambda h, k=k: h.matmul(ps[:, :], lhsT=lhs_buf[:, k, tl * 128:(tl + 1) * 128], rhs=wb[:, k, :],
                                               start=(k == 0), stop=(k == NCH - 1 and br is None)),
                 reads=list(lhs_keys) + [wk], writes=[pk])
        if br is not None:
            S.op("pe", lambda h: h.matmul(ps[:, :], lhsT=C.ones_bf[0:1, :], rhs=br[0:1, :], start=False, stop=True),
                 reads=["ones_bf", bk], writes=[pk])
        return ps, pk

    def gelu2(ps, pk, out_fn, outkeys, extra_reads=()):
        i = tctr[0] % 2
        tctr[0] += 1
        t1, t2 = tmp1[i], tmp2[i]
        k1, k2 = ("tmp1", i), ("tmp2", i)
        S.op("act", lambda h: h.activation(out=t1[:], in_=ps[:, :], func=AF.Square, scale=float(np.sqrt(GA))),
             reads=[pk], writes=[k1])
        S.op("dve", lambda h: h.scalar_tensor_tensor(out=t2[:], in0=t1[:], scalar=1.0, in1=ps[:, :],
                                                    op0=ALU.add, op1=ALU.mult), reads=[k1, pk], writes=[k2])
        S.op("act", lambda h: h.activation(out=t1[:], in_=t2[:], func=AF.Tanh, scale=GC),
             reads=[k2], writes=[k1])
        return t1, k1, t2, k2

    for grp in range(NT // G):
        for tl in range(G):
            tg = grp * G + tl
            S.dma("sp", lambda h, tl=tl, tg=tg: h.dma_start(out=A[tl][:], in_=x1t[tg]), writes=Ak(tl))
            _norm_transpose(S, C, A[tl][:], Ak(tl), s_bc, ["s_bc"],
                            lambda half, tl=tl: hT[:, half * 8:(half + 1) * 8, tl * 128:(tl + 1) * 128],
                            [("hT", tl)], tg)
        hT_keys = [("hT", t) for t in range(G)]
        for vb in range(4):
            wb, wk = load_w(w_in, D + vb * 512)
            br, bk = bias_row(wb, wk)
            for tl in range(G):
                ps, pk = proj_tile(hT, [("hT", tl)], tl, wb, wk, br, bk)
                t1, k1, t2, k2 = gelu2(ps, pk, None, None)
                S.op("dve", lambda h, t1=t1, ps=ps, tl=tl, vb=vb: h.scalar_tensor_tensor(
                    out=A[tl][:, vb * 512:(vb + 1) * 512], in0=t1[:], scalar=1.0, in1=ps[:, :],
                    op0=ALU.add, op1=ALU.mult), reads=[k1, pk], writes=[("A", tl, vb)])
        for tl in range(G):
            i2 = tl % 2
            b0 = i2 * 8
            S.op("act", lambda h, tl=tl, b0=b0: h.activation(out=C.junk[:], in_=A[tl][:], func=AF.Identity,
                                                           accum_out=st2[:, b0:b0 + 1]),
                 reads=Ak(tl), writes=["junk", ("st2s", i2)])
            S.op("act", lambda h, tl=tl, b0=b0: h.activation(out=C.junk[:], in_=A[tl][:], func=AF.Square,
                                                           accum_out=st2[:, b0 + 1:b0 + 2]),
                 reads=Ak(tl), writes=["junk", ("st2q", i2)])
            S.op("dve", lambda h, b0=b0: h.tensor_scalar(out=st2[:, b0 + 2:b0 + 3], in0=st2[:, b0:b0 + 1],
                                                        scalar1=1.0 / D, scalar2=None, op0=ALU.mult),
                 reads=[("st2s", i2)], writes=[("st2m", i2)])
            S.op("dve", lambda h, b0=b0: h.tensor_tensor(out=st2[:, b0 + 3:b0 + 4], in0=st2[:, b0 + 2:b0 + 3],
                                                        in1=st2[:, b0 + 2:b0 + 3], op=ALU.mult),
                 reads=[("st2m", i2)], writes=[("st2mm", i2)])
            S.op("dve", lambda h, b0=b0: h.scalar_tensor_tensor(out=st2[:, b0 + 4:b0 + 5], in0=st2[:, b0 + 1:b0 + 2],
                                                               scalar=1.0 / D, in1=st2[:, b0 + 3:b0 + 4],
                                                               op0=ALU.mult, op1=ALU.subtract),
                 reads=[("st2q", i2), ("st2mm", i2)], writes=[("st2v", i2)])
            _rstd(S, C, st2[:, b0 + 5:b0 + 6], st2[:, b0 + 4:b0 + 5], 0.25, EPS, [("st2v", i2)], ("st2r", i2))
            S.op("dve", lambda h, b0=b0: h.tensor_scalar(out=st2[:, b0 + 5:b0 + 6], in0=st2[:, b0 + 5:b0 + 6],
                                                        scalar1=0.5, scalar2=None, op0=ALU.mult),
                 reads=[("st2r", i2)], writes=[("st2r", i2)])
            S.op("dve", lambda h, b0=b0: h.scalar_tensor_tensor(out=st2[:, b0 + 6:b0 + 7], in0=st2[:, b0 + 2:b0 + 3],
                                                               scalar=-1.0, in1=st2[:, b0 + 5:b0 + 6],
                                                               op0=ALU.mult, op1=ALU.mult),
                 reads=[("st2m", i2), ("st2r", i2)], writes=[("st2nb", i2)])
            S.op("act", lambda h, tl=tl, b0=b0: h.activation(out=A[tl][:], in_=A[tl][:], func=AF.Identity,
                                                           scale=st2[:, b0 + 5:b0 + 6], bias=st2[:, b0 + 6:b0 + 7]),
                 reads=Ak(tl) + [("st2r", i2), ("st2nb", i2)], writes=Ak(tl))
            S.op("dve", lambda h, tl=tl: h.tensor_tensor(out=A[tl][:], in0=A[tl][:], in1=lng_bc[:], op=ALU.mult),
                 reads=Ak(tl) + ["lng_bc"], writes=Ak(tl))
            S.op("dve", lambda h, tl=tl: h.tensor_tensor(out=vhat[:], in0=A[tl][:], in1=lnb_bc[:], op=ALU.add),
                 reads=Ak(tl) + ["lnb_bc"], writes=["vhat"])
            for gp in range(4):
                ps = C.psF[4 + gp % 2]
                pk = ("psF", 4 + gp % 2)
                for j in range(2):
                    g = gp * 2 + j
                    S.op("pe", lambda h, ps=ps, g=g, j=j: h.matmul(ps[:, j * 256:(j + 1) * 256], lhsT=wsT[:, g, :],
                                                                  rhs=vhat[:, g * 256:(g + 1) * 256], start=True, stop=True),
                         reads=["wsT", "vhat"], writes=[pk])
                for j in range(2):
                    g = gp * 2 + j
                    S.op("act", lambda h, ps=ps, g=g, j=j, tl=tl: h.activation(
                        out=A[tl][:, g * 256:(g + 1) * 256], in_=ps[:, j * 256:(j + 1) * 256], func=AF.Identity,
                        bias=bscol[:, g:g + 1]), reads=[pk, "bscol"], writes=[("A", tl, gp)])
        for kb in range(4):
            wb, wk = load_w(w_in, kb * 512)
            br, bk = bias_row(wb, wk)
            for tl in range(G):
                ps, pk = proj_tile(hT, [("hT", tl)], tl, wb, wk, br, bk)
                t1, k1, t2, k2 = gelu2(ps, pk, None, None)
                S.op("dve", lambda h, t1=t1, t2=t2, ps=ps: h.scalar_tensor_tensor(
                    out=t2[:], in0=t1[:], scalar=1.0, in1=ps[:, :], op0=ALU.add, op1=ALU.mult),
                    reads=[k1, pk], writes=[k2])
                S.op("pool", lambda h, t2=t2, tl=tl, kb=kb: h.tensor_tensor(
                    out=A[tl][:, kb * 512:(kb + 1) * 512], in0=A[tl][:, kb * 512:(kb + 1) * 512], in1=t2[:], op=ALU.mult),
                    reads=[k2, ("A", tl, kb)], writes=[("A", tl, kb)])
            wb, wk = load_w(w_in, 2 * D + kb * 512)
            br, bk = bias_row(wb, wk)
            for tl in range(G):
                ps, pk = proj_tile(hT, [("hT", tl)], tl, wb, wk, br, bk)
                i = tctr[0] % 2
                tctr[0] += 1
                t1, t2, mb = tmp1[i], tmp2[i], mixblk[i]
                k1, k2, mk = ("tmp1", i), ("tmp2", i), ("mixblk", i)
                S.op("act", lambda h, t1=t1, ps=ps: h.activation(out=t1[:], in_=ps[:, :], func=AF.Tanh, scale=0.5),
                     reads=[pk], writes=[k1])
                S.op("dve", lambda h, t1=t1, t2=t2, ps=ps: h.scalar_tensor_tensor(
                    out=t2[:], in0=t1[:], scalar=1.0, in1=ps[:, :], op0=ALU.add, op1=ALU.mult),
                    reads=[k1, pk], writes=[k2])
                S.op("dve", lambda h, t2=t2, mb=mb, tl=tl, kb=kb: h.scalar_tensor_tensor(
                    out=mb[:], in0=A[tl][:, kb * 512:(kb + 1) * 512], scalar=0.25, in1=t2[:],
                    op0=ALU.mult, op1=ALU.mult), reads=[k2, ("A", tl, kb)], writes=[mk])
                pT = C.psT[i]
                ptk = ("psT", i)
                for j in range(4):
                    S.op("pe", lambda h, pT=pT, mb=mb, j=j: h.transpose(out=pT[:, j * 128:(j + 1) * 128],
                                                                       in_=mb[:, j * 128:(j + 1) * 128], identity=C.ident[:]),
                         reads=[mk, "ident"], writes=[ptk])
                S.op("act", lambda h, pT=pT, tl=tl, kb=kb: h.activation(
                    out=mixT[:, kb * 4:(kb + 1) * 4, tl * 128:(tl + 1) * 128],
                    in_=pT[:, 0:512].rearrange("p (c t) -> p c t", c=4), func=AF.Copy),
                    reads=[ptk], writes=[("mixT", tl, kb)])
        for tl in range(G):
            tg = grp * G + tl
            S.dma("sp", lambda h, tl=tl, tg=tg: h.dma_start(out=A[tl][:], in_=x1t[tg]), writes=Ak(tl))
        for ob in range(4):
            wb, wk = load_w(w_out, ob * 512)
            for tl in range(G):
                ps, pk = proj_tile(mixT, [("mixT", tl, b) for b in range(4)], tl, wb, wk, None, None)
                i = tctr[0] % 2
                tctr[0] += 1
                t2 = tmp2[i]
                k2 = ("tmp2", i)
                S.op("dve", lambda h, t2=t2, ps=ps, ob=ob: h.tensor_tensor(
                    out=t2[:], in0=ps[:, :], in1=gate_bc[:, ob * 512:(ob + 1) * 512], op=ALU.mult),
                    reads=[pk, "gate_bc"], writes=[k2])
                S.op("pool", lambda h, t2=t2, tl=tl, ob=ob: h.tensor_tensor(
                    out=A[tl][:, ob * 512:(ob + 1) * 512], in0=A[tl][:, ob * 512:(ob + 1) * 512], in1=t2[:], op=ALU.add),
                    reads=[k2, ("A", tl, ob)], writes=[("A", tl, ob)])
        for tl in range(G):
            tg = grp * G + tl
            i2 = tl % 2
            S.op("act", lambda h, tl=tl, i2=i2: h.activation(out=C.junk[:], in_=A[tl][:], func=AF.Square,
                                                           accum_out=C.stat[:, 8 + i2:9 + i2]),
                 reads=Ak(tl), writes=["junk", ("fst", i2)])
            _rstd(S, C, C.stat[:, 10 + i2:11 + i2], C.stat[:, 8 + i2:9 + i2], 1.0 / D, EPS, [("fst", i2)], ("fsr", i2))
            S.op("dve", lambda h, tl=tl, i2=i2: h.scalar_tensor_tensor(
                out=A[tl][:], in0=A[tl][:], scalar=C.stat[:, 10 + i2:11 + i2], in1=fg_bc[:],
                op0=ALU.mult, op1=ALU.mult), reads=Ak(tl) + [("fsr", i2), "fg_bc"], writes=Ak(tl))
            out_dmas.append(S.dma("sp", lambda h, tl=tl, tg=tg: h.dma_start(out=outt[tg], in_=A[tl][:]), reads=Ak(tl)))
    return out_dmas


def build_l1_only(NT=16, G=4):
    nc = bass.Bass("TRN2", target_bir_lowering=False)
    T = NT * 128
    dr = {}
    def din(name, shape):
        dr[name] = nc.dram_tensor(name, list(shape), F32, kind="ExternalInput").ap()
    din("x1", (T, D))
    din("c_col", (128, NCH))
    din("ada_w1", (D, 3 * D))
    din("ada_b1", (1, 3 * D))
    din("norm_g1_bc", (128, D))
    din("od_w_in", (D, 3 * D))
    din("od_w_out", (D, D))
    din("lng_bc", (128, D))
    din("lnb_bc", (128, D))
    din("wsT", (128, 8, 128))
    din("bs_col", (128, 8))
    din("fg_bc", (128, D))
    out = nc.dram_tensor("out", [T, D], F32, kind="ExternalOutput").ap()
    with contextlib.ExitStack() as es_top:
        S = Sched(nc, es_top)
        C = Ctx()
        C.psF = [es_top.enter_context(nc.psum_tensor("psF%d" % i, [128, 512], F32)) for i in range(6)]
        C.psT = [es_top.enter_context(nc.psum_tensor("psT%d" % i, [128, 1024], BF16)) for i in range(2)]
        _common_consts(nc, S, es_top, C)
        with contextlib.ExitStack() as es1:
            emit_l1(S, nc, C, es1, NT, G, dr["x1"], out, dr)
            S.emit_block()
    return nc


def l1_host_inputs(c_b, ada_w1, ada_b1, norm_g1, od_w_in, od_ln_g, od_ln_b, od_ws, od_bs, od_w_out, final_g):
    bc = lambda v: np.ascontiguousarray(np.broadcast_to(np.asarray(v, np.float32)[None, :], (128, v.shape[0])))
    return {
        "c_col": np.ascontiguousarray(np.asarray(c_b, np.float32).reshape(NCH, 128).T),
        "ada_w1": np.ascontiguousarray(ada_w1, dtype=np.float32),
        "ada_b1": np.ascontiguousarray(np.asarray(ada_b1, np.float32)[None, :]),
        "norm_g1_bc": bc(norm_g1),
        "od_w_in": np.ascontiguousarray(od_w_in, dtype=np.float32),
        "od_w_out": np.ascontiguousarray(od_w_out, dtype=np.float32),
        "lng_bc": bc(od_ln_g),
        "lnb_bc": bc(od_ln_b),
        "wsT": np.ascontiguousarray(np.transpose(np.asarray(od_ws, np.float32), (2, 0, 1))),
        "bs_col": np.ascontiguousarray(np.asarray(od_bs, np.float32).T),
        "fg_bc": bc(final_g),
    }


def _mod_section(S, nc, C, es, ada_w_ap, ada_b_ap, scb, scbk, c0, nblk, consume, tag):
    wv = ada_w_ap.rearrange("(k p) n -> p k n", p=128)
    for blk in range(nblk):
        cs = c0 + blk * 512
        wb = C.wbuf[blk % 2]
        wk = ("wbuf", blk % 2)
        S.dma("pool", lambda h, wb=wb, cs=cs: h.dma_start(out=wb[:], in_=wv[:, :, cs:cs + 512]), writes=[wk])
        rb = C.rowb[blk % 2]
        rk = ("rowb", blk % 2)
        S.dma("sp", lambda h, rb=rb, cs=cs: h.dma_start(out=rb[0:1, :], in_=ada_b_ap[0:1, cs:cs + 512]), writes=[rk])
        ps = C.psF[blk % 2]
        pk = ("psF", blk % 2)
        for k in range(NCH):
            S.op("pe", lambda h, ps=ps, wb=wb, k=k: h.matmul(ps[0:1, :], lhsT=scb[:, k:k + 1], rhs=wb[:, k, :],
                                                             start=(k == 0), stop=(k == NCH - 1)),
                 reads=[scbk, wk], writes=[pk])
        S.op("dve", lambda h, ps=ps, rb=rb: h.tensor_tensor(out=rb[0:1, :], in0=ps[0:1, :], in1=rb[0:1, :], op=ALU.add),
             reads=[pk, rk], writes=[rk])
        consume(blk, rb, rk)


def _silu_cols(S, nc, C, es, ccol_ap, tag):
    sb = lambda n, s, d: es.enter_context(nc.sbuf_tensor("t_" + n, s, d))
    cc = sb("cc_" + tag, [128, NCH], F32)
    ct = sb("ct_" + tag, [128, NCH], F32)
    scb = sb("scb_" + tag, [128, NCH], BF16)
    S.dma("sp", lambda h: h.dma_start(out=cc[:], in_=ccol_ap), writes=["cc" + tag])
    S.op("act", lambda h: h.activation(out=ct[:], in_=cc[:], func=AF.Tanh, scale=0.5),
         reads=["cc" + tag], writes=["ct" + tag])
    S.op("dve", lambda h: h.scalar_tensor_tensor(out=ct[:], in0=ct[:], scalar=1.0, in1=cc[:],
                                                op0=ALU.add, op1=ALU.mult),
         reads=["ct" + tag, "cc" + tag], writes=["ct" + tag])
    S.op("dve", lambda h: h.tensor_scalar(out=scb[:], in0=ct[:], scalar1=0.5, scalar2=None, op0=ALU.mult),
         reads=["ct" + tag], writes=["scb" + tag])
    return scb, "scb" + tag


def _rope(S, eng, src, dst, rt, H, tmps, rkeys, wkeys, tkey):
    sv = src.rearrange("p (h i two) -> p h i two", h=H, two=2)
    dv = dst.rearrange("p (h i two) -> p h i two", h=H, two=2)
    e, o = sv[:, :, :, 0], sv[:, :, :, 1]
    cosb = rt[:, 0:64].unsqueeze(1).broadcast_to([128, H, 64])
    sinb = rt[:, 64:128].unsqueeze(1).broadcast_to([128, H, 64])
    ta = tmps[0].rearrange("p (h i) -> p h i", h=H)
    tb = tmps[1].rearrange("p (h i) -> p h i", h=H)
    ka, kb = (tkey, 0), (tkey, 1)
    S.op(eng, lambda h: h.tensor_tensor(out=ta, in0=e, in1=cosb, op=ALU.mult), reads=rkeys, writes=[ka])
    S.op(eng, lambda h: h.tensor_tensor(out=tb, in0=o, in1=sinb, op=ALU.mult), reads=rkeys, writes=[kb])
    S.op(eng, lambda h: h.tensor_tensor(out=dv[:, :, :, 0], in0=ta, in1=tb, op=ALU.subtract),
         reads=[ka, kb], writes=wkeys)
    S.op(eng, lambda h: h.tensor_tensor(out=ta, in0=e, in1=sinb, op=ALU.mult), reads=rkeys + wkeys, writes=[ka])
    S.op(eng, lambda h: h.tensor_tensor(out=tb, in0=o, in1=cosb, op=ALU.mult), reads=rkeys, writes=[kb])
    S.op(eng, lambda h: h.tensor_tensor(out=dv[:, :, :, 1], in0=ta, in1=tb, op=ALU.add),
         reads=[ka, kb], writes=wkeys)


def emit_l0_attn(S, nc, C, es, dr, QT, s_bc, shT0, scbl, scblk, NTK, NTO):
    sb = lambda n, s, d: es.enter_context(nc.sbuf_tensor("t_" + n, s, d))
    NKT = NTK + 2
    KT = sb("KT", [128, 2, NKT * 128], BF16)
    V = sb("V", [128, NKT, 256], BF16)
    C.wbuf = [sb("wbufA%d" % i, [128, NCH, 512], BF16) for i in range(2)]
    C.rowb = [sb("rowbA%d" % i, [1, 512], F32) for i in range(2)]
    C.hn = [sb("hnA%d" % i, [128, D], BF16) for i in range(2)]
    C.junk = None
    C.stat = sb("statA", [128, 32], F32)
    xt = [sb("xtA%d" % i, [128, D], F32) for i in range(2)]
    hTt = [sb("hTt%d" % i, [128, NCH, 128], BF16) for i in range(2)]
    rt = [sb("rt%d" % i, [128, 128], F32) for i in range(3)]
    kqsq = [sb("kqsq%d" % i, [128, 512], F32) for i in range(2)]
    kqf = [sb("kqf%d" % i, [128, 512], F32) for i in range(2)]
    kqb = [sb("kqb%d" % i, [128, 512], BF16) for i in range(2)]
    rtmp = [sb("rtmp%d" % i, [128, 256], F32) for i in range(4)]
    kst = sb("kst", [128, 32], F32)
    knb = sb("knb", [128, 128], F32)
    qnb = sb("qnb", [128, 128], F32)
    shTc = sb("shTc", [128, NCH], BF16)
    brow = [sb("browA%d" % i, [1, 512], BF16) for i in range(3)]
    PT = [sb("PT%d" % i, [128, 512], BF16) for i in range(4)]
    rec = sb("rec", [128, 512], F32)
    ones128 = sb("ones128", [128, 128], BF16)
    S.op("pool", lambda h: h.memset(ones128[:], 1.0), writes=["ones128"])
    S.dma("sp", lambda h: h.dma_start(out=knb[:], in_=dr["kn_bc"]), writes=["knb"])
    S.dma("sp", lambda h: h.dma_start(out=qnb[:], in_=dr["qn_bc"]), writes=["qnb"])

    w_in = dr["ev_w_in"].rearrange("(k p) n -> p k n", p=128)
    xall = dr["xall"].rearrange("(t p) d -> t p d", p=128)
    xown = dr["xown"].rearrange("(t p) d -> t p d", p=128)
    rall = dr["rope_all"].rearrange("(t p) d -> t p d", p=128)
    rown = dr["rope_own"].rearrange("(t p) d -> t p d", p=128)
    psK = C.psF[5][:, :].bitcast(BF16)
    psKk = ("psF", 5)

    def set_sbc(scb, scbk, shT_dst, shkey, tag):
        S.dma("sp", lambda h: h.dma_start(out=s_bc[:], in_=dr["norm_g0_bc"]), writes=["s_bc"])

        def consume(blk, rb, rk):
            sec, j = divmod(blk, 4)
            cols = slice(j * 512, (j + 1) * 512)
            if sec == 0:
                _row_to_cols(S, C, rb, rk, shT_dst[:, j * 4:(j + 1) * 4], shkey, j * 4)
            else:
                def post(ps, pk):
                    S.op("dve", lambda h: h.scalar_tensor_tensor(out=s_bc[:, cols], in0=ps[:, :], scalar=1.0,
                                                                in1=s_bc[:, cols], op0=ALU.add, op1=ALU.mult),
                         reads=[pk, "s_bc"], writes=["s_bc"])
                _bcast_rows(S, C, rb, rk, None, None, post)
        _mod_section(S, nc, C, es, dr["ada_w0"], dr["ada_b0"], scb, scbk, 0, 8, consume, tag)

    def bias_row(wb, wk, shT, shk, i):
        ps = C.psF[4]
        pk = ("psF", 4)
        for k in range(NCH):
            S.op("pe", lambda h, k=k: h.matmul(ps[0:1, :], lhsT=shT[:, k:k + 1], rhs=wb[:, k, :],
                                               start=(k == 0), stop=(k == NCH - 1)),
                 reads=[shk, wk], writes=[pk])
        br = brow[i]
        S.op("act", lambda h: h.activation(out=br[0:1, :], in_=ps[0:1, :], func=AF.Copy),
             reads=[pk], writes=[("browA", i)])
        return br, ("browA", i)

    pctr = [0]

    def stage_A1(i, src_ap, rope_ap):
        p = i % 2
        S.dma("sp", lambda h: h.dma_start(out=xt[p][:], in_=src_ap), writes=[("xt", p)])
        if rope_ap is not None:
            r3 = i % 3
            S.dma("sp", lambda h: h.dma_start(out=rt[r3][:], in_=rope_ap), writes=[("rt", r3)])
        _norm_scale(S, C, xt[p][:], [("xt", p)], s_bc, ["s_bc"], i)

    def stage_A2(i):
        p = i % 2
        _transposes(S, C, lambda half: hTt[p][:, half * 8:(half + 1) * 8, :], [("hTt", p)], i)

    def tile_proj(i, wb, wk, br, bk):
        p = i % 2
        j = pctr[0] % 4
        pctr[0] += 1
        ps = C.psF[j]
        pk = ("psF", j)
        for k in range(NCH):
            S.op("pe", lambda h, k=k: h.matmul(ps[:, :], lhsT=hTt[p][:, k, :], rhs=wb[:, k, :],
                                               start=(k == 0), stop=False),
                 reads=[("hTt", p), wk], writes=[pk])
        S.op("pe", lambda h: h.matmul(ps[:, :], lhsT=C.ones_bf[0:1, :], rhs=br[0:1, :], start=False, stop=True),
             reads=["ones_bf", bk], writes=[pk])
        return ps, pk

    def headnorm(j, ps, pk, H, gbc, gk):
        q = j % 2
        W = H * 128
        c0 = q * 16
        S.op("act", lambda h: h.activation(out=kqsq[q][:, 0:W], in_=ps[:, 0:W], func=AF.Square),
             reads=[pk], writes=[("kqsq", q)])
        S.op("dve", lambda h: h.tensor_reduce(out=kst[:, c0:c0 + H],
                                             in_=kqsq[q][:, 0:W].rearrange("p (h d) -> p h d", h=H),
                                             axis=AX.X, op=ALU.add), reads=[("kqsq", q)], writes=[("kst0", q)])
        _rstd(S, C, kst[:, c0 + 8:c0 + 8 + H], kst[:, c0:c0 + H], 1.0 / 128, EPS, [("kst0", q)], ("kst1", q), n=H)
        for hh in range(H):
            S.op("dve", lambda h, hh=hh: h.scalar_tensor_tensor(
                out=kqf[q][:, hh * 128:(hh + 1) * 128], in0=ps[:, hh * 128:(hh + 1) * 128],
                scalar=kst[:, c0 + 8 + hh:c0 + 9 + hh], in1=gbc[:], op0=ALU.mult, op1=ALU.mult),
                reads=[pk, ("kst1", q), gk], writes=[("kqf", q)])

    def kv_B1(i, wkv, wkvk, br, bk, has_rope, kpos):
        p = i % 2
        ps, pk = tile_proj(i, wkv, wkvk, br, bk)
        S.op("act", lambda h: h.activation(out=V[:, kpos, :], in_=ps[:, 256:512], func=AF.Copy),
             reads=[pk], writes=[("V", kpos)])
        headnorm(i, ps, pk, 2, knb, "knb")
        if has_rope:
            _rope(S, "pool", kqf[p][:, 0:256], kqb[p][:, 0:256], rt[i % 3], 2,
                  [rtmp[2 * p][:, 0:128], rtmp[2 * p + 1][:, 0:128]], [("kqf", p), ("rt", i % 3)], [("kqb", p)],
                  ("rtmpk", p))
        else:
            S.op("dve", lambda h: h.tensor_copy(out=kqb[p][:, 0:256], in_=kqf[p][:, 0:256]),
                 reads=[("kqf", p)], writes=[("kqb", p)])

    def kv_B2(i, kpos):
        p = i % 2
        for hh in range(2):
            S.op("pe", lambda h, hh=hh: h.transpose(out=psK[:, p * 512 + hh * 128:p * 512 + (hh + 1) * 128],
                                                   in_=kqb[p][:, hh * 128:(hh + 1) * 128], identity=C.ident[:]),
                 reads=[("kqb", p), "ident"], writes=[psKk])
        S.op("act", lambda h: h.activation(out=KT[:, :, kpos * 128:(kpos + 1) * 128],
                                          in_=psK[:, p * 512:p * 512 + 256].rearrange("p (g t) -> p g t", g=2),
                                          func=AF.Copy),
             reads=[psKk], writes=[("KT", kpos)])

    def kv_pipeline(tiles, wkv, wkvk, br, bk, has_rope, i0):
        n = len(tiles)
        stage_A1(i0, tiles[0][0], tiles[0][1])
        if n > 1:
            stage_A1(i0 + 1, tiles[1][0], tiles[1][1])
        stage_A2(i0)
        for t in range(n):
            if t + 2 < n:
                stage_A1(i0 + t + 2, tiles[t + 2][0], tiles[t + 2][1])
            if t + 1 < n:
                stage_A2(i0 + t + 1)
            kv_B1(i0 + t, wkv, wkvk, br, bk, has_rope, tiles[t][2])
            if t >= 1:
                kv_B2(i0 + t - 1, tiles[t - 1][2])
        kv_B2(i0 + n - 1, tiles[n - 1][2])

    scbc, scbck = _silu_cols(S, nc, C, es, dr["cctx_col"], "ctx")
    set_sbc(scbc, scbck, shTc, "shTc", "ctx")
    wkv = C.wbuf[0]
    S.dma("pool", lambda h: h.dma_start(out=wkv[:], in_=w_in[:, :, 0:512]), writes=[("wbuf", 0)])
    brc, brck = bias_row(wkv, ("wbuf", 0), shTc, "shTc", 0)
    kv_pipeline([(xall[NTK + t], None, NTK + t) for t in range(2)], wkv, ("wbuf", 0), brc, brck, False, 0)
    set_sbc(scbl, scblk, shT0, "shT0", "lat")
    S.dma("pool", lambda h: h.dma_start(out=wkv[:], in_=w_in[:, :, 0:512]), writes=[("wbuf", 0)])
    brl, brlk = bias_row(wkv, ("wbuf", 0), shT0, "shT0", 1)
    wq = [C.wbuf[1], C.wbuf[0]]
    wqk = [("wbuf", 1), ("wbuf", 0)]
    S.dma("pool", lambda h: h.dma_start(out=wq[0][:], in_=w_in[:, :, 512:1024]), writes=[wqk[0]])
    kv_pipeline([(xall[t], rall[t], t) for t in range(NTK)], wkv, ("wbuf", 0), brl, brlk, True, 2)
    S.dma("pool", lambda h: h.dma_start(out=wq[1][:], in_=w_in[:, :, 1024:1536]), writes=[wqk[1]])
    brq = [bias_row(wq[0], wqk[0], shT0, "shT0", 0), bias_row(wq[1], wqk[1], shT0, "shT0", 2)]

    def q_B1(i, t, b):
        j = 2 * i + b
        ps, pk = tile_proj(i, wq[b], wqk[b], brq[b][0], brq[b][1])
        headnorm(j, ps, pk, 4, qnb, "qnb")
        q = j % 2
        _rope(S, "dve", kqf[q][:, 0:512], kqb[q][:, 0:512], rt[i % 3], 4,
              [rtmp[2 * q][:, 0:256], rtmp[2 * q + 1][:, 0:256]], [("kqf", q), ("rt", i % 3)], [("kqb", q)],
              ("rtmpk", q))

    def q_B2(i, t, b):
        j = 2 * i + b
        q = j % 2
        for hh in range(4):
            S.op("pe", lambda h, hh=hh: h.transpose(out=psK[:, q * 512 + hh * 128:q * 512 + (hh + 1) * 128],
                                                   in_=kqb[q][:, hh * 128:(hh + 1) * 128], identity=C.ident[:]),
                 reads=[("kqb", q), "ident"], writes=[psKk])
        S.op("act", lambda h: h.activation(
            out=QT[:, b * 4:(b + 1) * 4, t * 128:(t + 1) * 128],
            in_=psK[:, q * 512:(q + 1) * 512].rearrange("p (g t) -> p g t", g=4), func=AF.Copy),
            reads=[psKk], writes=[("QT", b * 4 + hh, t // 4) for hh in range(4)])

    i0 = 2 + NTK
    stage_A1(i0, xown[0], rown[0])
    if NTO > 1:
        stage_A1(i0 + 1, xown[1], rown[1])
    stage_A2(i0)
    pend = None
    for t in range(NTO):
        if t + 2 < NTO:
            stage_A1(i0 + t + 2, xown[t + 2], rown[t + 2])
        if t + 1 < NTO:
            stage_A2(i0 + t + 1)
        for b in range(2):
            q_B1(i0 + t, t, b)
            if pend is not None:
                q_B2(*pend)
            pend = (i0 + t, t, b)
    q_B2(*pend)
    NQB = NTO // 4
    blocks = [(g, hq, qb) for g in range(2) for hq in range(4) for qb in range(NQB)]
    steps = [(bi, kt) for bi in range(len(blocks)) for kt in range(NKT)]
    scale = 128 ** -0.5
    kkeys = [("KT", k) for k in range(NKT)]

    def emit_S(si):
        bi, kt = steps[si]
        g, hq, qb = blocks[bi]
        hd = g * 4 + hq
        ps = C.psF[si % 2]
        S.op("pe", lambda h: h.matmul(ps[:, :], lhsT=KT[:, g, kt * 128:(kt + 1) * 128],
                                      rhs=QT[:, hd, qb * 512:(qb + 1) * 512], start=True, stop=True),
             reads=[("KT", kt), ("QT", hd, qb)], writes=[("psF", si % 2)])
        p = PT[si % 4]
        S.op("act", lambda h: h.activation(out=p[:], in_=ps[:, :], func=AF.Exp, scale=scale),
             reads=[("psF", si % 2)], writes=[("PT", si % 4)])

    def emit_OR(si):
        bi, kt = steps[si]
        g, hq, qb = blocks[bi]
        hd = g * 4 + hq
        pO = C.psF[2 + bi % 2]
        pR = C.psF[4 + bi % 2]
        p = PT[si % 4]
        S.op("pe", lambda h: h.matmul(pO[:, :], lhsT=V[:, kt, g * 128:(g + 1) * 128], rhs=p[:],
                                      start=(kt == 0), stop=(kt == NKT - 1)),
             reads=[("V", kt), ("PT", si % 4)], writes=[("psF", 2 + bi % 2)])
        S.op("pe", lambda h: h.matmul(pR[:, :], lhsT=ones128[:], rhs=p[:],
                                      start=(kt == 0), stop=(kt == NKT - 1)),
             reads=["ones128", ("PT", si % 4)], writes=[("psF", 4 + bi % 2)])
        if kt == NKT - 1:
            S.op("dve", lambda h: h.reciprocal(out=rec[:], in_=pR[:, :]), reads=[("psF", 4 + bi % 2)], writes=["rec"])
            S.op("dve", lambda h: h.tensor_tensor(out=QT[:, hd, qb * 512:(qb + 1) * 512], in0=pO[:, :], in1=rec[:],
                                                 op=ALU.mult),
                 reads=[("psF", 2 + bi % 2), "rec"], writes=[("QT", hd, qb)])

    emit_S(0)
    for si in range(len(steps)):
        if si + 1 < len(steps):
            emit_S(si + 1)
        emit_OR(si)


def emit_l0_mix(S, nc, C, es, dr, QT, s_bc, shT0, scbl, scblk, NTO, x1_ap):
    sb = lambda n, s, d: es.enter_context(nc.sbuf_tensor("t_" + n, s, d))
    G = 4
    GT = 512
    NG = NTO // G
    NWB = 4
    C.wbuf = [sb("wbufM%d" % i, [128, NCH, 512], BF16) for i in range(NWB)]
    C.rowb = [sb("rowbM%d" % i, [1, 512], F32) for i in range(2)]
    C.hn = [sb("hnM", [128, D], BF16)] * 2
    C.junk = C.hn[0]
    C.stat = sb("statM", [128, 32], F32)
    xt = sb("xtM", [128, D], F32)
    hTo = sb("hTo", [128, NCH, GT], BF16)
    hTh = sb("hTh", [128, NCH, 128], BF16)
    gate_bc = sb("gate_bcM", [128, D], F32)
    mixc = sb("mixc", [128, 8, GT], BF16)
    cT = sb("cT", [128, 8, GT], F32)
    y = [sb("y%d" % i, [128, GT + 32], F32) for i in range(2)]
    yh = sb("yh", [128, 32], F32)
    t1 = [sb("mt1_%d" % i, [128, 512], F32) for i in range(2)]
    t2 = [sb("mt2_%d" % i, [128, 512], F32) for i in range(2)]
    t3 = [sb("mt3_%d" % i, [128, 512], F32) for i in range(2)]
    mean_sb = sb("mean_sb", [128, 512], F32)
    rstd_sb = sb("rstd_sb", [128, 512], F32)
    bcol = sb("bcol", [128, 32], F32)
    hbcol = sb("hbcol", [128, 32], F32)
    dwT = sb("dwT", [128, 8, 31], F32)
    dwb = sb("dwb", [128, 8], F32)
    lng = sb("lngc", [128, 8], F32)
    lnb = sb("lnbc", [128, 8], F32)
    hlng = sb("hlng", [128, 8], F32)
    hlnb = sb("hlnb", [128, 8], F32)
    mask = sb("maskM", [128, 32], F32)
    onesN = sb("onesN", [128, 128], BF16)
    c16 = [sb("c16_%d" % i, [128, 512], BF16) for i in range(2)]
    s16 = [sb("s16_%d" % i, [128, 512], BF16) for i in range(2)]
    xblk = [sb("xblk%d" % i, [128, 512], F32) for i in range(2)]

    S.op("pool", lambda h: h.memset(onesN[:], 1.0 / 1024), writes=["onesN"])
    S.dma("sp", lambda h: h.dma_start(out=dwT[:], in_=dr["dwT"]), writes=["dwT"])
    S.op("dve", lambda h: h.tensor_scalar(out=dwT[:], in0=dwT[:], scalar1=0.5, scalar2=None, op0=ALU.mult),
         reads=["dwT"], writes=["dwT"])
    S.dma("sp", lambda h: h.dma_start(out=dwb[:], in_=dr["dwb_col"]), writes=["dwb"])
    S.dma("sp", lambda h: h.dma_start(out=lng[:], in_=dr["elng_col"]), writes=["lng"])
    S.dma("sp", lambda h: h.dma_start(out=lnb[:], in_=dr["elnb_col"]), writes=["lnb"])
    S.op("dve", lambda h: h.tensor_scalar(out=hlng[:], in0=lng[:], scalar1=0.5, scalar2=None, op0=ALU.mult),
         reads=["lng"], writes=["hlng"])
    S.op("dve", lambda h: h.tensor_scalar(out=hlnb[:], in0=lnb[:], scalar1=0.5, scalar2=None, op0=ALU.mult),
         reads=["lnb"], writes=["hlnb"])

    def consume(blk, rb, rk):
        _bcast_rows(S, C, rb, rk, gate_bc[:, blk * 512:(blk + 1) * 512], "gate_bc")
    _mod_section(S, nc, C, es, dr["ada_w0"], dr["ada_b0"], scbl, scblk, 2 * D, 4, consume, "gate")

    w_in = dr["ev_w_in"].rearrange("(k p) n -> p k n", p=128)
    w_out = dr["ev_w_out"].rearrange("(k p) n -> p k n", p=128)
    xown = dr["xown"].rearrange("(t p) d -> t p d", p=128)
    xhalo = dr["xhalo"].rearrange("(t p) d -> t p d", p=128)
    x1t = x1_ap.rearrange("(t p) d -> t p d", p=128)

    wctr = [0]
    pctr = [0]
    tc = [0]

    wblocks = []
    for _g in range(NG):
        wblocks += [(w_in, 1536), (w_in, 2048)]
        wblocks += [(w_in, 2560), (w_in, 3584), (w_in, 3072), (w_in, 4096)]
        wblocks += [(w_in, 4608), (w_in, 5120)]
        wblocks += [(w_out, ob * 512) for ob in range(4)]
    C.wctr = 0
    ws = wstream(S, C, wblocks, NWB, hold=2)

    def load_w(src_view, c0):
        wb, wk = next(ws)
        return wb, wk

    def bias_cols(wb, wk, base):
        ps = C.psF[3]
        pk = ("psF", 3)
        for k in range(NCH):
            S.op("pe", lambda h, k=k: h.matmul(ps[0:1, :], lhsT=shT0[:, k:k + 1], rhs=wb[:, k, :],
                                               start=(k == 0), stop=(k == NCH - 1)),
                 reads=[wk, "shT0"], writes=[pk])
        rb = C.rowb[0]
        rk = ("rowb", 0)
        S.op("act", lambda h: h.activation(out=rb[0:1, :], in_=ps[0:1, :], func=AF.Copy), reads=[pk], writes=[rk])
        for j in range(4):
            S.op("pe", lambda h, j=j: h.matmul(ps[:, j:j + 1], lhsT=rb[0:1, j * 128:(j + 1) * 128],
                                               rhs=C.ones_f[0:1, 0:1], start=True, stop=True),
                 reads=[rk, "ones_f"], writes=[pk])
        S.op("act", lambda h: h.activation(out=bcol[:, base:base + 4], in_=ps[:, 0:4], func=AF.Copy),
             reads=[pk], writes=[("bcol", base)])
        S.op("dve", lambda h: h.tensor_scalar(out=hbcol[:, base:base + 4], in0=bcol[:, base:base + 4], scalar1=0.5,
                                             scalar2=None, op0=ALU.mult), reads=[("bcol", base)], writes=[("hbcol", base)])

    def projT(wb, wk, cl, rhs_fn, rkeys, n):
        i = pctr[0] % 3
        pctr[0] += 1
        ps = C.psF[i]
        pk = ("psF", i)
        for k in range(NCH):
            S.op("pe", lambda h, k=k: h.matmul(ps[:, 0:n], lhsT=wb[:, k, cl * 128:(cl + 1) * 128], rhs=rhs_fn(k),
                                               start=(k == 0), stop=(k == NCH - 1)),
                 reads=[wk] + list(rkeys), writes=[pk])
        return ps, pk

    def silu2(ps, pk, n, bidx):
        i = tc[0] % 2
        tc[0] += 1
        a, b_ = t1[i], t2[i]
        ka, kb = ("mt1", i), ("mt2", i)
        bb = (bidx // 4) * 4
        S.op("act", lambda h: h.activation(out=b_[:, 0:n], in_=ps[:, 0:n], func=AF.Identity,
                                          bias=bcol[:, bidx:bidx + 1]), reads=[pk, ("bcol", bb)], writes=[kb])
        S.op("act", lambda h: h.activation(out=a[:, 0:n], in_=b_[:, 0:n], func=AF.Tanh, scale=0.5),
             reads=[kb], writes=[ka])
        S.op("dve", lambda h: h.scalar_tensor_tensor(out=b_[:, 0:n], in0=a[:, 0:n], scalar=1.0, in1=b_[:, 0:n],
                                                    op0=ALU.add, op1=ALU.mult), reads=[ka, kb], writes=[kb])
        return b_, kb

    out_dmas = []
    for gq in range(NG):
        for tl in range(G):
            tg = gq * G + tl
            S.dma("sp", lambda h, tg=tg: h.dma_start(out=xt[:], in_=xown[tg]), writes=["xt"])
            _norm_transpose(S, C, xt[:], ["xt"], s_bc, ["s_bc"],
                            lambda half, tl=tl: hTo[:, half * 8:(half + 1) * 8, tl * 128:(tl + 1) * 128],
                            [("hTo", tl)], 0)
        S.dma("sp", lambda h, gq=gq: h.dma_start(out=xt[:], in_=xhalo[gq]), writes=["xt"])
        _norm_transpose(S, C, xt[:], ["xt"], s_bc, ["s_bc"],
                        lambda half: hTh[:, half * 8:(half + 1) * 8, :], ["hTh"], 0)
        S.dma("sp", lambda h, gq=gq: h.dma_start(out=mask[:], in_=dr["halo_mask"][gq]), writes=["mask"])
        hkeys = [("hTo", t) for t in range(G)]
        if getattr(C, "upto", None) == "hT":
            return out_dmas
        for wbk in range(2):
            wb, wk = load_w(w_in, 1536 + wbk * 512)
            bias_cols(wb, wk, wbk * 4)
            for cl in range(4):
                c = wbk * 4 + cl
                u_ = getattr(C, "upto", None)
                if u_ == "za_w":
                    continue
                ps, pk = projT(wb, wk, cl, lambda k: hTo[:, k, :], hkeys, 512)
                if u_ == "za_p":
                    continue
                sres, sk = silu2(ps, pk, 512, c)
                if u_ in ("za_s", "za_s1", "za_s2"):
                    continue
                S.op("dve", lambda h, c=c, sres=sres, gq=gq: h.scalar_tensor_tensor(
                    out=QT[:, c, gq * 512:(gq + 1) * 512], in0=QT[:, c, gq * 512:(gq + 1) * 512], scalar=0.5,
                    in1=sres[:, :], op0=ALU.mult, op1=ALU.mult),
                    reads=[sk, ("QT", c, gq)], writes=[("QT", c, gq)])
        if getattr(C, "upto", None) in ("za", "za_w", "za_p", "za_s", "za_s1", "za_s2"):
            return out_dmas
        for wbk in range(2):
            wa, wak = load_w(w_in, 2560 + wbk * 512)
            bias_cols(wa, wak, 8 + wbk * 4)
            wbb, wbbk = load_w(w_in, 3584 + wbk * 512)
            bias_cols(wbb, wbbk, 16 + wbk * 4)
            for cl in range(4):
                c = wbk * 4 + cl
                yb = y[c % 2]
                yk = ("y", c % 2)
                for part in range(2):
                    n = 512 if part == 0 else 32
                    rf = (lambda k: hTo[:, k, :]) if part == 0 else (lambda k: hTh[:, k, 0:32])
                    rk_ = hkeys if part == 0 else ["hTh"]
                    psa, pka = projT(wa, wak, cl, rf, rk_, n)
                    psb, pkb = projT(wbb, wbbk, cl, rf, rk_, n)
                    i = tc[0] % 2
                    tc[0] += 1
                    a_, b_ = t1[i], t2[i]
                    ka, kb = ("mt1", i), ("mt2", i)
                    S.op("act", lambda h, a_=a_, psb=psb, n=n, c=c: h.activation(
                        out=a_[:, 0:n], in_=psb[:, 0:n], func=AF.Tanh, scale=0.5, bias=hbcol[:, 16 + c:17 + c]),
                        reads=[pkb, ("hbcol", 16 + wbk * 4)], writes=[ka])
                    S.op("act", lambda h, b_=b_, psa=psa, n=n, c=c: h.activation(
                        out=b_[:, 0:n], in_=psa[:, 0:n], func=AF.Identity, bias=bcol[:, 8 + c:9 + c]),
                        reads=[pka, ("bcol", 8 + wbk * 4)], writes=[kb])
                    if part == 0:
                        S.op("dve", lambda h, a_=a_, b_=b_, yb=yb: h.scalar_tensor_tensor(
                            out=yb[:, 15:527], in0=a_[:, :], scalar=1.0, in1=b_[:, :], op0=ALU.add, op1=ALU.mult),
                            reads=[ka, kb], writes=[yk])
                    else:
                        S.op("dve", lambda h, a_=a_, b_=b_: h.scalar_tensor_tensor(
                            out=yh[:, :], in0=a_[:, 0:32], scalar=1.0, in1=b_[:, 0:32], op0=ALU.add, op1=ALU.mult),
                            reads=[ka, kb], writes=["yh"])
                        S.op("dve", lambda h, yb=yb: h.tensor_tensor(out=yb[:, 0:15], in0=yh[:, 0:15], in1=mask[:, 0:15],
                                                                    op=ALU.mult), reads=["yh", "mask"], writes=[yk])
                        S.op("dve", lambda h, yb=yb: h.tensor_tensor(out=yb[:, 527:542], in0=yh[:, 15:30],
                                                                    in1=mask[:, 15:30], op=ALU.mult),
                             reads=["yh", "mask"], writes=[yk])
                eng = "dve"
                ck = ("cT", c)
                S.op(eng, lambda h, c=c, yb=yb: h.tensor_scalar(out=cT[:, c, :], in0=yb[:, 0:512],
                                                              scalar1=dwT[:, c, 0:1], scalar2=dwb[:, c:c + 1],
                                                              op0=ALU.mult, op1=ALU.add),
                     reads=[yk, "dwT", "dwb"], writes=[ck])
                for j in range(1, 31):
                    S.op(eng, lambda h, c=c, yb=yb, j=j: h.scalar_tensor_tensor(
                        out=cT[:, c, :], in0=yb[:, j:j + 512], scalar=dwT[:, c, j:j + 1], in1=cT[:, c, :],
                        op0=ALU.mult, op1=ALU.add), reads=[yk, "dwT", ck], writes=[ck])
                i = tc[0] % 2
                tc[0] += 1
                cb, sq = c16[i], s16[i]
                S.op("act", lambda h, c=c, cb=cb: h.activation(out=cb[:], in_=cT[:, c, :], func=AF.Copy),
                     reads=[ck], writes=[("c16", i)])
                S.op("act", lambda h, c=c, sq=sq: h.activation(out=sq[:], in_=cT[:, c, :], func=AF.Square),
                     reads=[ck], writes=[("s16", i)])
                S.op("pe", lambda h, c=c, cb=cb: h.matmul(C.psF[4][:, :], lhsT=onesN[:], rhs=cb[:],
                                                          start=(c == 0), stop=(c == 7)),
                     reads=[("c16", i), "onesN"], writes=[("psF", 4)])
                S.op("pe", lambda h, c=c, sq=sq: h.matmul(C.psF[5][:, :], lhsT=onesN[:], rhs=sq[:],
                                                          start=(c == 0), stop=(c == 7)),
                     reads=[("s16", i), "onesN"], writes=[("psF", 5)])
        if getattr(C, "upto", None) == "conv":
            return out_dmas
        S.op("act", lambda h: h.activation(out=mean_sb[:], in_=C.psF[4][:, :], func=AF.Copy),
             reads=[("psF", 4)], writes=["mean_sb"])
        S.op("dve", lambda h: h.tensor_tensor(out=rstd_sb[:], in0=mean_sb[:], in1=mean_sb[:], op=ALU.mult),
             reads=["mean_sb"], writes=["rstd_sb"])
        S.op("dve", lambda h: h.tensor_tensor(out=rstd_sb[:], in0=C.psF[5][:, :], in1=rstd_sb[:], op=ALU.subtract),
             reads=[("psF", 5), "rstd_sb"], writes=["rstd_sb"])
        S.op("dve", lambda h: h.tensor_scalar(out=rstd_sb[:], in0=rstd_sb[:], scalar1=EPS, scalar2=None, op0=ALU.add),
             reads=["rstd_sb"], writes=["rstd_sb"])
        S.op("pool", lambda h: h.tensor_tensor(out=rstd_sb[:], in0=rstd_sb[:], in1=C.mhalf[:, :], op=ALU.pow),
             reads=["rstd_sb", "mhalf"], writes=["rstd_sb"])
        if getattr(C, "upto", None) == "ln":
            return out_dmas
        for wbk in range(2):
            wb, wk = load_w(w_in, 4608 + wbk * 512)
            bias_cols(wb, wk, 24 + wbk * 4)
            for cl in range(4):
                c = wbk * 4 + cl
                ck = ("cT", c)
                S.op("dve", lambda h, c=c: h.tensor_tensor(out=cT[:, c, :], in0=cT[:, c, :], in1=mean_sb[:], op=ALU.subtract),
                     reads=[ck, "mean_sb"], writes=[ck])
                S.op("dve", lambda h, c=c: h.tensor_tensor(out=cT[:, c, :], in0=cT[:, c, :], in1=rstd_sb[:], op=ALU.mult),
                     reads=[ck, "rstd_sb"], writes=[ck])
                i = tc[0] % 2
                tc[0] += 1
                a_ = t3[i]
                ka = ("mt3", i)
                S.op("act", lambda h, c=c, a_=a_: h.activation(out=a_[:], in_=cT[:, c, :], func=AF.Tanh,
                                                             scale=hlng[:, c:c + 1], bias=hlnb[:, c:c + 1]),
                     reads=[ck, "hlng", "hlnb"], writes=[ka])
                S.op("dve", lambda h, c=c: h.tensor_scalar(out=cT[:, c, :], in0=cT[:, c, :], scalar1=lng[:, c:c + 1],
                                                          scalar2=lnb[:, c:c + 1], op0=ALU.mult, op1=ALU.add),
                     reads=[ck, "lng", "lnb"], writes=[ck])
                S.op("dve", lambda h, c=c, a_=a_: h.scalar_tensor_tensor(out=cT[:, c, :], in0=a_[:], scalar=1.0,
                                                                       in1=cT[:, c, :], op0=ALU.add, op1=ALU.mult),
                     reads=[ka, ck], writes=[ck])
                ps, pk = projT(wb, wk, cl, lambda k: hTo[:, k, :], hkeys, 512)
                sres, sk = silu2(ps, pk, 512, 24 + c)
                S.op("dve", lambda h, c=c, sres=sres: h.scalar_tensor_tensor(
                    out=mixc[:, c, :], in0=cT[:, c, :], scalar=0.25, in1=sres[:, :], op0=ALU.mult, op1=ALU.mult),
                    reads=[sk, ck], writes=[("mixc", c)])
        if getattr(C, "upto", None) == "zb":
            return out_dmas
        mkeys = [("mixc", c) for c in range(8)] + [("QT", c, gq) for c in range(8)]
        for ob in range(4):
            wb, wk = load_w(w_out, ob * 512)
            for tl in range(G):
                tg = gq * G + tl
                i = pctr[0] % 3
                pctr[0] += 1
                ps = C.psF[i]
                pk = ("psF", i)
                for k in range(NCH):
                    lhs = QT[:, k, tg * 128:(tg + 1) * 128] if k < 8 else mixc[:, k - 8, tl * 128:(tl + 1) * 128]
                    S.op("pe", lambda h, k=k, lhs=lhs, ps=ps, wb=wb: h.matmul(ps[:, :], lhsT=lhs, rhs=wb[:, k, :],
                                                                             start=(k == 0), stop=(k == NCH - 1)),
                         reads=mkeys + [wk], writes=[pk])
                j = tc[0] % 2
                tc[0] += 1
                xb = xblk[j]
                xk = ("xblk", j)
                tt = t3[j]
                S.dma("sp", lambda h, xb=xb, tg=tg, ob=ob: h.dma_start(out=xb[:], in_=xown[tg][:, ob * 512:(ob + 1) * 512]),
                      writes=[xk])
                S.op("dve", lambda h, tt=tt, ps=ps, ob=ob: h.tensor_tensor(
                    out=tt[:], in0=ps[:, :], in1=gate_bc[:, ob * 512:(ob + 1) * 512], op=ALU.mult),
                    reads=[pk, "gate_bc"], writes=[("mt3", j)])
                S.op("pool", lambda h, tt=tt, xb=xb: h.tensor_tensor(out=xb[:], in0=xb[:], in1=tt[:], op=ALU.add),
                     reads=[("mt3", j), xk], writes=[xk])
                out_dmas.append(S.dma("sp", lambda h, xb=xb, tg=tg, ob=ob: h.dma_start(
                    out=x1t[tg][:, ob * 512:(ob + 1) * 512], in_=xb[:]), reads=[xk]))
    return out_dmas


L0_INPUTS = {
    "xall": None, "xown": None, "rope_all": None, "rope_own": None, "xhalo": None, "halo_mask": None,
    "c_col": (128, NCH), "cctx_col": (128, NCH), "ada_w0": (D, 3 * D), "ada_b0": (1, 3 * D),
    "norm_g0_bc": (128, D), "ev_w_in": (D, 5632), "ev_w_out": (D, D), "kn_bc": (128, 128), "qn_bc": (128, 128),
    "dwT": (128, 8, 31), "dwb_col": (128, 8), "elng_col": (128, 8), "elnb_col": (128, 8),
}
L1_INPUTS = {
    "ada_w1": (D, 3 * D), "ada_b1": (1, 3 * D), "norm_g1_bc": (128, D), "od_w_in": (D, 3 * D), "od_w_out": (D, D),
    "lng_bc": (128, D), "lnb_bc": (128, D), "wsT": (128, 8, 128), "bs_col": (128, 8), "fg_bc": (128, D),
}


def build_full(NTK=64, NTO=16, stop=None, upto=None):
    nc = bass.Bass("TRN2", target_bir_lowering=False)
    dr = {}
    shapes = dict(L0_INPUTS)
    shapes.update(L1_INPUTS)
    shapes["xall"] = ((NTK + 2) * 128, D)
    shapes["xown"] = (NTO * 128, D)
    shapes["rope_all"] = (NTK * 128, 128)
    shapes["rope_own"] = (NTO * 128, 128)
    shapes["xhalo"] = ((NTO // 4) * 128, D)
    shapes["halo_mask"] = (NTO // 4, 128, 32)
    for name, shp in shapes.items():
        dr[name] = nc.dram_tensor(name, list(shp), F32, kind="ExternalInput").ap()
    x1s = nc.dram_tensor("x1s", [NTO * 128, D], F32, kind="Internal").ap()
    out = nc.dram_tensor("out", [NTO * 128, D], F32, kind="ExternalOutput").ap()
    with contextlib.ExitStack() as top:
        S = Sched(nc, top)
        C = Ctx()
        C.upto = upto
        C.psF = [top.enter_context(nc.psum_tensor("psF%d" % i, [128, 512], F32)) for i in range(6)]
        C.psT = [top.enter_context(nc.psum_tensor("psT%d" % i, [128, 1024], BF16)) for i in range(2)]
        _common_consts(nc, S, top, C)
        with contextlib.ExitStack() as es0:
            sb = lambda n, s, d: es0.enter_context(nc.sbuf_tensor("t_" + n, s, d))
            QT = sb("QT", [128, 8, NTO * 128], BF16)
            s_bc = sb("s_bc0", [128, D], F32)
            shT0 = sb("shT0", [128, NCH], BF16)
            scbl, scblk = _silu_cols(S, nc, C, es0, dr["c_col"], "lat")
            with contextlib.ExitStack() as esA:
                emit_l0_attn(S, nc, C, esA, dr, QT, s_bc, shT0, scbl, scblk, NTK, NTO)
                if stop == "attn":
                    dbg = nc.dram_tensor("dbg", [128, 8 * NTO * 128], BF16, kind="ExternalOutput").ap()
                    S.dma("sp", lambda h: h.dma_start(out=dbg, in_=QT[:, :, :].rearrange("p a b -> p (a b)")),
                          reads=[("QT", c, q) for c in range(8) for q in range(NTO // 4)])
                S.emit_block()
                if stop == "attn":
                    return nc
            with contextlib.ExitStack() as esM:
                emit_l0_mix(S, nc, C, esM, dr, QT, s_bc, shT0, scbl, scblk, NTO, x1s if stop != "mix" else out)
                S.emit_block()
                if stop == "mix":
                    return nc
        with contextlib.ExitStack() as es1:
            emit_l1(S, nc, C, es1, NTO, 4, x1s, out, dr)
            S.emit_block()
    return nc


def rope_table(n):
    rows = n // 64
    row = np.repeat(np.arange(rows, dtype=np.float32), 64)
    col = np.tile(np.arange(64, dtype=np.float32), rows)
    inv = np.power(np.float32(10000.0), np.arange(32, dtype=np.float32) * np.float32(-2.0 / 64)).astype(np.float32)
    ang = np.concatenate([row[:, None] * inv, col[:, None] * inv], axis=-1).astype(np.float32)
    return np.concatenate([np.cos(ang), np.sin(ang)], axis=-1).astype(np.float32)


def host_inputs(inp, b, T0, NTO, seq):
    f = lambda a: np.ascontiguousarray(np.asarray(a, dtype=np.float32))
    bc = lambda v, n=128: np.ascontiguousarray(np.broadcast_to(f(v)[None, :], (n, np.asarray(v).shape[0])))
    col = lambda v, k: np.ascontiguousarray(f(v).reshape(k, 128).T)
    x = f(inp["x"][b][:seq])
    T = NTO * 128
    rope = rope_table(seq)
    m = {}
    m["xall"] = np.ascontiguousarray(np.concatenate([x, f(inp["ctx"][b])], axis=0))
    m["xown"] = np.ascontiguousarray(x[T0:T0 + T])
    m["rope_all"] = rope
    m["rope_own"] = np.ascontiguousarray(rope[T0:T0 + T])
    NG = NTO // 4
    xh = np.zeros((NG * 128, D), np.float32)
    hm = np.zeros((NG, 128, 32), np.float32)
    for g in range(NG):
        base = T0 + g * 512
        for j in range(15):
            t = base - 15 + j
            if 0 <= t < seq:
                xh[g * 128 + j] = x[t]
                hm[g, :, j] = 1.0
            t = base + 512 + j
            if 0 <= t < seq:
                xh[g * 128 + 15 + j] = x[t]
                hm[g, :, 15 + j] = 1.0
    m["xhalo"] = xh
    m["halo_mask"] = hm
    m["c_col"] = col(inp["c"][b], NCH)
    m["cctx_col"] = col(inp["c_ctx"], NCH)
    m["ada_w0"] = f(inp["ada_w"][0])
    m["ada_b0"] = f(inp["ada_b"][0])[None, :]
    m["norm_g0_bc"] = bc(inp["norm_g"][0])
    m["ev_w_in"] = f(inp["ev_w_in"][0])
    m["ev_w_out"] = f(inp["ev_w_out"][0])
    m["kn_bc"] = bc(inp["ev_k_norm"][0])
    m["qn_bc"] = bc(inp["ev_q_norm"][0])
    m["dwT"] = np.ascontiguousarray(np.transpose(f(inp["ev_dw_w"][0]).reshape(31, 8, 128), (2, 1, 0)))
    m["dwb_col"] = col(inp["ev_dw_b"][0], 8)
    m["elng_col"] = col(inp["ev_ln_g"][0], 8)
    m["elnb_col"] = col(inp["ev_ln_b"][0], 8)
    m.update(l1_host_inputs(inp["c"][b], inp["ada_w"][1], inp["ada_b"][1], inp["norm_g"][1], inp["od_w_in"][0],
                            inp["od_ln_g"][0], inp["od_ln_b"][0], inp["od_ws"][0], inp["od_bs"][0],
                            inp["od_w_out"][0], inp["final_g"]))
    return m


_NC_CACHE = {}


def kernel(**inputs):
    inp = {k: np.asarray(v) for k, v in inputs.items()}
    B, seq, _ = inp["x"].shape
    ncores = 8
    per = ncores // B
    T = seq // per
    NTO = T // 128
    NTK = seq // 128
    key = (NTK, NTO)
    if key not in _NC_CACHE:
        _NC_CACHE[key] = build_full(NTK, NTO)
    nc = _NC_CACHE[key]
    in_maps = []
    for i in range(ncores):
        b, q = divmod(i, per)
        in_maps.append(host_inputs(inp, b, q * T, NTO, seq))
    res = run_bass_kernel_spmd(nc, in_maps, core_ids=list(range(ncores)))
    out = np.zeros((B, seq, D), np.float32)
    for i in range(ncores):
        b, q = divmod(i, per)
        out[b, q * T:(q + 1) * T] = res.results[i]["out"]
    return out
```
